# Optimizing a Trainium2 kernel written in Bass

```python
import math
import jax, jax.numpy as jnp
from jax import lax
import numpy as np

D_MODEL = 1024
BATCH = 8
SEQ = 4096
DEPTH = 1

N_META = 16
D_CONV = D_MODEL
CONV_WIDTH = 31
N_HEADS = 8
HEAD_DIM = 128
N_KV_HEADS = 2
N_IDX_HEADS = 8
IDX_DIM = 64
TOPK_MAX = 256
D_FF = 2816
FFN_CONV_WIDTH = 3
ROPE_THETA = 10000.0
Q_BLOCK = 128
EPS = 1e-6
META_BONUS = 1e30

COLS = (
    2 * D_CONV,
    N_HEADS * HEAD_DIM,
    N_KV_HEADS * HEAD_DIM,
    N_KV_HEADS * HEAD_DIM,
    N_IDX_HEADS * IDX_DIM,
    IDX_DIM,
    N_IDX_HEADS,
    2 * D_MODEL,
)
D_IN_PROJ = sum(COLS)

kernel_name = "hybrid_conformer_dsa_gated_block"


def rms_norm(x, g):
    xf = x.astype(jnp.float32)
    y = xf * lax.rsqrt(jnp.mean(xf * xf, axis=-1, keepdims=True) + EPS)
    return (y * g.astype(jnp.float32)).astype(x.dtype)


def layer_norm(x, g, b):
    xf = x.astype(jnp.float32)
    mu = jnp.mean(xf, axis=-1, keepdims=True)
    var = jnp.mean(jnp.square(xf - mu), axis=-1, keepdims=True)
    y = (xf - mu) * lax.rsqrt(var + EPS)
    return (y * g.astype(jnp.float32) + b.astype(jnp.float32)).astype(x.dtype)


def rope(x, pos):
    half = x.shape[-1] // 2
    inv = ROPE_THETA ** (-jnp.arange(half, dtype=jnp.float32) / half)
    ang = pos.astype(jnp.float32)[:, None] * inv[None, :]
    cos = jnp.cos(ang)[None, :, None, :]
    sin = jnp.sin(ang)[None, :, None, :]
    xf = x.astype(jnp.float32)
    x1, x2 = xf[..., :half], xf[..., half:]
    return jnp.concatenate([x1 * cos - x2 * sin, x2 * cos + x1 * sin], axis=-1).astype(x.dtype)


def causal_depthwise_conv(x, w, b):
    width, c = w.shape
    y = lax.conv_general_dilated(
        x, w[:, None, :].astype(x.dtype), window_strides=(1,),
        padding=[(width - 1, 0)], dimension_numbers=("NWC", "WIO", "NWC"),
        feature_group_count=c)
    return y + b.astype(x.dtype)


def conformer_conv_branch(u, ln_g, ln_b, dw_w, dw_b, pw_out):
    a, gate = jnp.split(u, 2, axis=-1)
    h = a * jax.nn.sigmoid(gate)
    h = causal_depthwise_conv(h, dw_w, dw_b)
    h = layer_norm(h, ln_g, ln_b)
    h = jax.nn.silu(h)
    return h @ pw_out


def dsa_sparse_attention(q, k, v, q_idx, k_idx, w_idx, w_o):
    bsz, seq_len = q.shape[0], q.shape[1]
    top_k = min(TOPK_MAX, seq_len // 4)
    n_blk = -(-seq_len // Q_BLOCK)
    pad = n_blk * Q_BLOCK - seq_len
    rep = N_HEADS // N_KV_HEADS

    def to_blocks(t):
        t = jnp.pad(t, [(0, 0), (0, pad)] + [(0, 0)] * (t.ndim - 2))
        t = t.reshape((bsz, n_blk, Q_BLOCK) + t.shape[2:])
        return jnp.moveaxis(t, 1, 0)

    key_pos = jnp.arange(seq_len, dtype=jnp.int32)
    q_pos_blocks = jnp.arange(n_blk * Q_BLOCK, dtype=jnp.int32).reshape(n_blk, Q_BLOCK)
    k_idx_f = k_idx.astype(jnp.float32)
    gather = jax.vmap(lambda kb, ib: kb[ib])

    def one_block(args):
        qb, qib, wb, qpos = args
        dots = jnp.einsum("bqhd,bsd->bqhs", qib.astype(jnp.float32), k_idx_f) * (IDX_DIM ** -0.5)
        score = jnp.einsum("bqh,bqhs->bqs", wb.astype(jnp.float32) * (N_IDX_HEADS ** -0.5),
                           jax.nn.relu(dots))
        visible = key_pos[None, :] <= qpos[:, None]
        is_meta = visible & (key_pos[None, :] < N_META)
        score = jnp.where(visible[None], score, -jnp.inf)
        score = jnp.where(is_meta[None], jnp.float32(METAB_PLACEHOLDER) if False else jnp.float32(METAB), score) if False else jnp.where(is_meta[None], jnp.float32(META_BONUS), score)
        vals, idx = lax.top_k(score, top_k)
        valid = jnp.isfinite(vals)
        ks = gather(k, idx)
        vs = gather(v, idx)
        qg = qb.reshape(bsz, Q_BLOCK, N_KV_HEADS, rep, HEAD_DIM)
        logits = jnp.einsum("bqgrd,bqkgd->bqgrk", qg, ks).astype(jnp.float32) * (HEAD_DIM ** -0.5)
        logits = jnp.where(valid[:, :, None, None, :], logits, -jnp.inf)
        p = jax.nn.softmax(logits, axis=-1).astype(v.dtype)
        o = jnp.einsum("bqgrk,bqkgd->bqgrd", p, vs)
        return o.reshape(bsz, Q_BLOCK, N_HEADS * HEAD_DIM)

    outs = lax.map(one_block, (to_blocks(q), to_blocks(q_idx), to_blocks(w_idx), q_pos_blocks))
    outs = jnp.moveaxis(outs, 0, 1).reshape(bsz, n_blk * Q_BLOCK, N_HEADS * HEAD_DIM)[:, :seq_len]
    return outs @ w_o


def setup_inputs(seed: int = 0) -> dict:
    key = jax.random.key(seed)
    ks = jax.random.split(key, 20)
    f32 = jnp.float32

    def nrm(k, shape, scale):
        return jax.random.normal(k, shape, f32) * scale

    def gain(k, shape):
        return 1.0 + 0.05 * jax.random.normal(k, shape, f32)

    return {
        "x": jax.random.normal(ks[0], (BATCH, SEQ, D_MODEL), f32),
        "meta_tokens": nrm(ks[1], (N_META, D_MODEL), 1.0),
        "mix_norm_g": gain(ks[2], (DEPTH, D_MODEL)),
        "w_in": nrm(ks[3], (DEPTH, D_MODEL, D_IN_PROJ), D_MODEL ** -0.5),
        "conv_ln_g": gain(ks[4], (DEPTH, D_CONV)),
        "conv_ln_b": nrm(ks[5], (DEPTH, D_CONV), 0.01),
        "conv_dw_w": nrm(ks[6], (DEPTH, CONV_WIDTH, D_CONV), CONV_WIDTH ** -0.5),
        "conv_dw_b": nrm(ks[7], (DEPTH, D_CONV), 0.01),
        "conv_pw_out": nrm(ks[8], (DEPTH, D_CONV, D_MODEL), D_CONV ** -0.5),
        "attn_w_o": nrm(ks[9], (DEPTH, N_HEADS * HEAD_DIM, D_MODEL), (N_HEADS * HEAD_DIM) ** -0.5),
        "w_merge_out": nrm(ks[10], (DEPTH, D_MODEL, D_MODEL), D_MODEL ** -0.5),
        "ffn_norm_g": gain(ks[11], (DEPTH, D_MODEL)),
        "ffn_up": nrm(ks[12], (DEPTH, D_MODEL, 2 * D_FF), D_MODEL ** -0.5),
        "ffn_dw_w": nrm(ks[13], (DEPTH, FFN_CONV_WIDTH, 2 * D_FF), FFN_CONV_WIDTH ** -0.5),
        "ffn_dw_b": nrm(ks[14], (DEPTH, 2 * D_FF), 0.01),
        "ffn_down": nrm(ks[15], (DEPTH, D_FF, D_MODEL), D_FF ** -0.5),
        "final_norm_g": gain(ks[16], (D_MODEL,)),
    }


def reference(x, meta_tokens, mix_norm_g, w_in, conv_ln_g, conv_ln_b, conv_dw_w, conv_dw_b,
              conv_pw_out, attn_w_o, w_merge_out, ffn_norm_g, ffn_up, ffn_dw_w, ffn_dw_b,
              ffn_down, final_norm_g):
    bsz = x.shape[0]
    meta = jnp.broadcast_to(meta_tokens.astype(x.dtype)[None], (bsz, N_META, x.shape[-1]))
    stream = jnp.concatenate([meta, x], axis=1)
    seq_len = stream.shape[1]
    pos = jnp.arange(seq_len, dtype=jnp.int32)
    split_at = np.cumsum(COLS)[:-1].tolist()

    for l in range(DEPTH):
        h = rms_norm(stream, mix_norm_g[l])
        proj = h @ w_in[l]
        u_conv, q, k, v, q_idx, k_idx, w_idx, gates = jnp.split(proj, split_at, axis=-1)
        q = rope(q.reshape(bsz, seq_len, N_HEADS, HEAD_DIM), pos)
        k = rope(k.reshape(bsz, seq_len, N_KV_HEADS, HEAD_DIM), pos)
        v = v.reshape(bsz, seq_len, N_KV_HEADS, HEAD_DIM)
        q_idx = rope(q_idx.reshape(bsz, seq_len, N_IDX_HEADS, IDX_DIM), pos)
        k_idx = rope(k_idx[:, :, None, :], pos)[:, :, 0, :]

        y_a = conformer_conv_branch(u_conv, conv_ln_g[l], conv_ln_b[l], conv_dw_w[l],
                                    conv_dw_b[l], conv_pw_out[l])
        y_b = dsa_sparse_attention(q, k, v, q_idx, k_idx, w_idx, attn_w_o[l])
        g_a, g_b = jnp.split(jax.nn.sigmoid(gates), 2, axis=-1)
        stream = stream + (g_a * y_a + g_b * y_b) @ w_merge_out[l]

        h = rms_norm(stream, ffn_norm_g[l])
        up = causal_depthwise_conv(h @ ffn_up[l], ffn_dw_w[l], ffn_dw_b[l])
        a, b_ = jnp.split(up, 2, axis=-1)
        stream = stream + (jax.nn.silu(a) * b_) @ ffn_down[l]

    out = rms_norm(stream, final_norm_g)
    return out[:, N_META:]
```

```python
import numpy as np
from contextlib import ExitStack
import concourse.bass as bass
import concourse.mybir as mybir
from concourse.bass_utils import run_bass_kernel_spmd

F32 = mybir.dt.float32
BF16 = mybir.dt.bfloat16
AF = mybir.ActivationFunctionType
ALU = mybir.AluOpType
AX = mybir.AxisListType

L = 4112
LP = 4224
NT = 33
D = 1024
KC = 8
NPAIR = 22
EPS = 1e-6
NIT = 17
C_Q, C_K, C_V, C_QI, C_KI, C_WI, C_G = 2048, 3072, 3328, 3584, 4096, 4160, 4168
TGS = [(t0, min(512, LP - t0)) for t0 in range(0, LP, 512)]

ENGS = ['pe', 'act', 'dve', 'pool', 'sp']
NDMASEM = 24


class Op:
    __slots__ = ('eng', 'fn', 'reads', 'writes', 'deps', 'sig', 'tok', 'is_dma', 'idx', 'prevdma')

    def __init__(self, eng, fn, reads, writes, is_dma):
        self.eng = eng
        self.fn = fn
        self.reads = reads
        self.writes = writes
        self.is_dma = is_dma
        self.deps = []
        self.sig = False
        self.tok = None
        self.prevdma = None


class Sched:
    def __init__(self):
        self.ops = {e: [] for e in ENGS}
        self.lastw = {}
        self.readers = {}
        self.n = 0
        self.pending = {e: [] for e in ENGS}
        self.lastc = {e: None for e in ENGS}
        self.dmas_since = []

    def barrier(self):
        for f in ENGS:
            lst = [self.lastc[e] for e in ENGS if e != f and self.lastc[e] is not None]
            self.pending[f] = lst + list(self.dmas_since)
        self.dmas_since = []
        self.lastw = {}
        self.readers = {}

    def add(self, eng, fn, reads=(), writes=(), dma=False):
        if not getattr(self, 'enabled', True):
            return None
        op = Op(eng, fn, tuple(reads), tuple(writes), dma)
        op.idx = self.n
        self.n += 1
        deps = {}
        for r in op.reads:
            w = self.lastw.get(r)
            if w is not None:
                deps[w.idx] = (w, True)
        for wk in op.writes:
            lw = self.lastw.get(wk)
            if lw is not None and lw.idx not in deps:
                deps[lw.idx] = (lw, False)
            for rd in self.readers.get(wk, ()):
                if rd.idx not in deps:
                    deps[rd.idx] = (rd, False)
        for d, raw in deps.values():
            if d is op:
                continue
            if (not d.is_dma) and (not op.is_dma) and d.eng == op.eng:
                if not raw or op.eng == 'pe':
                    continue
            op.deps.append(d)
            d.sig = True
        if self.pending[eng]:
            for d in self.pending[eng]:
                if d is not op and d not in op.deps:
                    op.deps.append(d)
                    d.sig = True
            self.pending[eng] = []
        for r in op.reads:
            self.readers.setdefault(r, []).append(op)
        for wk in op.writes:
            self.lastw[wk] = op
            self.readers[wk] = []
        self.ops[eng].append(op)
        if dma:
            self.dmas_since.append(op)
        else:
            self.lastc[eng] = op
        return op

    def emit(self, nc, es):
        esem = {e: es.enter_context(nc.semaphore('s_' + e)) for e in ENGS}
        dsem = [es.enter_context(nc.semaphore('d_%d' % i)) for i in range(NDMASEM)]
        duse = [0] * NDMASEM
        dlast = [None] * NDMASEM
        allops = sorted([o for e in ENGS for o in self.ops[e]], key=lambda o: o.idx)
        qengs = sorted({o.eng for o in allops if o.is_dma})
        per = NDMASEM // max(1, len(qengs))
        qsems = {e: list(range(i * per, (i + 1) * per)) for i, e in enumerate(qengs)}
        qrr = {e: 0 for e in qengs}
        cnt = {e: 0 for e in ENGS}
        pos = {}
        for e in ENGS:
            for n_, o in enumerate(self.ops[e]):
                pos[id(o)] = n_ + 1
        needed = set()
        for e in ENGS:
            waited0 = {}
            for o in self.ops[e]:
                best0 = {}
                for d in o.deps:
                    if d.is_dma:
                        continue
                    if pos[id(d)] > best0.get(d.eng, (0, None))[0]:
                        best0[d.eng] = (pos[id(d)], d)
                for k_, (v_, d_) in best0.items():
                    if waited0.get(k_, 0) < v_:
                        needed.add(id(d_))
                        waited0[k_] = v_
        for e in ENGS:
            for o in self.ops[e]:
                if not o.is_dma:
                    o.sig = id(o) in needed
        for o in allops:
            if o.is_dma:
                s = qsems[o.eng][qrr[o.eng] % per]
                qrr[o.eng] += 1
                duse[s] += 1
                o.prevdma = dlast[s]
                o.tok = (('d', s), 16 * duse[s])
                dlast[s] = o
            elif o.sig:
                cnt[o.eng] += 1
                o.tok = (('e', o.eng), cnt[o.eng])
        self.stats = dict(cnt)

        def semof(key):
            return esem[key[1]] if key[0] == 'e' else dsem[key[1]]

        for e in ENGS:
            nxt = None
            for o in reversed(self.ops[e]):
                if o.is_dma:
                    continue
                if o.sig:
                    nxt = o.tok
                elif nxt is not None:
                    o.tok = nxt
                else:
                    o.tok = None

        def run(ename, eng):
            waited = {}
            for o in self.ops[ename]:
                deps = list(o.deps)
                if o.is_dma and o.prevdma is not None:
                    deps.append(o.prevdma)
                best = {}
                for d in deps:
                    assert d.tok is not None, "dependency on op with no later signal"
                    k, v = d.tok
                    if best.get(k, 0) < v:
                        best[k] = v
                for k, v in best.items():
                    if waited.get(k, 0) < v:
                        eng.wait_ge(semof(k), v)
                        waited[k] = v
                ins = o.fn(eng)
                if o.is_dma:
                    ins.then_inc(semof(o.tok[0]), 16)
                elif o.sig:
                    ins.then_inc(semof(o.tok[0]), 1)

        with nc.Block() as block:
            @block.tensor
            def _(eng):
                run('pe', eng)

            @block.scalar
            def _(eng):
                run('act', eng)

            @block.vector
            def _(eng):
                run('dve', eng)

            @block.gpsimd
            def _(eng):
                run('pool', eng)

            @block.sync
            def _(eng):
                run('sp', eng)
                for s in range(NDMASEM):
                    if duse[s] > 0:
                        eng.wait_ge(dsem[s], 16 * duse[s])


class Ring:
    def __init__(self, items):
        self.items = items
        self.i = 0

    def next(self):
        it = self.items[self.i % len(self.items)]
        self.i += 1
        return it


def build(debug=False, phases="ACDEF", asub="1234"):
    nc = bass.Bass("TRN2", target_bir_lowering=False)

    def din(name, shape, dt=F32):
        return nc.dram_tensor(name, list(shape), dt, kind="ExternalInput").ap()

    skind = "ExternalOutput" if debug else "Internal"

    def dscr(name, shape, dt):
        return nc.dram_tensor(name, list(shape), dt, kind=skind).ap()

    xs = din("xs", [LP, D])
    w_in = din("w_in", [D, 6216])
    gmix = din("gmix", [128, 8])
    dwT = din("dwT", [128, 8 * 31])
    cvec = din("cvec", [128, 24])
    w_pw = din("w_pw", [D, D])
    w_o = din("w_o", [D, D])
    w_m = din("w_m", [D, D])
    gffn = din("gffn", [128, 8])
    ffn_up = din("ffn_up", [D, 2 * 2816])
    fdw = din("fdw", [128, 44 * 3])
    fdb = din("fdb", [128, 44])
    ffn_down = din("ffn_down", [2816, D])
    fgb = din("fgb", [128, D])
    rope128 = din("rope128", [2, 128, LP])
    rope64 = din("rope64", [2, 128, LP])
    identd = din("identd", [128, 128])
    protd = din("protd", [128, 256])
    cmaskd = din("cmaskd", [128, 256])
    pow2d = din("pow2d", [128, NIT + 1])
    out = nc.dram_tensor("out", [4096, D], F32, kind="ExternalOutput").ap()

    CVd = dscr("CVd", [128, 8, LP], BF16)
    QTd = dscr("QTd", [128, 8, LP], BF16)
    KTd = dscr("KTd", [128, 2, LP], BF16)
    QITd = dscr("QITd", [128, 4, LP], BF16)
    KITd = dscr("KITd", [128, LP], BF16)
    Vd = dscr("Vd", [128, NT, 260], BF16)
    WId = dscr("WId", [128, NT, 16], F32)
    SGd = dscr("SGd", [128, 16, LP], F32)
    OTd = dscr("OTd", [128, 8, LP], BF16)
    S1d = dscr("S1d", [LP, D], F32)
    H2Td = dscr("H2Td", [128, 8, LP], BF16)
    ACTd = dscr("ACTd", [128, NPAIR, LP], BF16)

    S = Sched()
    es = ExitStack()
    import os as _os
    AW = int(_os.environ.get('AWK', '42')) * 1024
    arena = es.enter_context(nc.sbuf_tensor("arena", [128, AW], F32))
    psum = es.enter_context(nc.psum_tensor("psum", [128, 8, 512], F32))
    st = {'off': 0}

    def alloc(shape, dt):
        n = int(np.prod(shape))
        words = n if dt == F32 else (n + 1) // 2
        words = (words + 3) // 4 * 4
        off = st['off']
        if off + words > AW and not getattr(S, 'enabled', True):
            off = 0
        st['off'] = off + words
        assert st['off'] <= AW, ("arena overflow", st['off'])
        ap = arena[:, off:off + words]
        if dt != F32:
            ap = ap.bitcast(dt)
        ap = ap[:, 0:n]
        if len(shape) == 2:
            ap = ap.rearrange("p (a b) -> p a b", a=shape[0])
        elif len(shape) == 3:
            ap = ap.rearrange("p (a b c) -> p a b c", a=shape[0], b=shape[1])
        return ap

    uid = [0]

    def tile(shape, dt, name=None):
        uid[0] += 1
        return (alloc(shape, dt), (name or 't', uid[0]))

    def ring(n, shape, dt, name=None):
        return Ring([tile(shape, dt, name) for _ in range(n)])

    def bank(b):
        return psum[:, b, :]

    def bankbf(b):
        return psum[:, b, :].bitcast(BF16)

    def PB(b):
        return ('ps', b)

    def dma(q, out_, in_, reads, writes):
        S.add(q, lambda e: e.dma_start(out=out_, in_=in_), reads, writes, dma=True)

    def mm(out_, lhsT, rhs, start, stop, reads, writes, sgc=False):
        if sgc:
            S.add('pe', lambda e: e.matmul(out_, lhsT=lhsT, rhs=rhs, start=start, stop=stop, skip_group_check=True), reads, writes)
        else:
            S.add('pe', lambda e: e.matmul(out_, lhsT=lhsT, rhs=rhs, start=start, stop=stop), reads, writes)

    def tr(out_, in_, ident, reads, writes):
        S.add('pe', lambda e: e.transpose(out=out_, in_=in_, identity=ident), reads, writes)

    def act(out_, in_, func, reads, writes, bias=None, scale=None, accum=None):
        kw = {}
        if bias is not None:
            kw['bias'] = bias
        if scale is not None:
            kw['scale'] = scale
        if accum is not None:
            kw['accum_out'] = accum
        S.add('act', lambda e: e.activation(out=out_, in_=in_, func=func, **kw), reads, writes)

    def ts(eng, out_, in0, s1, s2, op0, op1, reads, writes, accum=None):
        if accum is not None:
            S.add(eng, lambda e: e.tensor_scalar(out=out_, in0=in0, scalar1=s1, scalar2=s2, op0=op0, op1=op1, accum_out=accum), reads, writes)
        elif op1 is None:
            S.add(eng, lambda e: e.tensor_scalar(out=out_, in0=in0, scalar1=s1, scalar2=None, op0=op0), reads, writes)
        else:
            S.add(eng, lambda e: e.tensor_scalar(out=out_, in0=in0, scalar1=s1, scalar2=s2, op0=op0, op1=op1), reads, writes)

    def tt(eng, out_, in0, in1, op, reads, writes):
        S.add(eng, lambda e: e.tensor_tensor(out=out_, in0=in0, in1=in1, op=op), reads, writes)

    def stt(out_, in0, scalar, in1, op0, op1, reads, writes):
        S.add('dve', lambda e: e.scalar_tensor_tensor(out=out_, in0=in0, scalar=scalar, in1=in1, op0=op0, op1=op1), reads, writes)

    def cp(eng, out_, in_, reads, writes):
        if eng == 'act':
            act(out_, in_, AF.Copy, reads, writes)
        else:
            S.add(eng, lambda e: e.tensor_copy(out=out_, in_=in_), reads, writes)

    def memset(eng, ap, val, writes):
        S.add(eng, lambda e: e.memset(ap, val), (), writes)

    def recip(out_, in_, reads, writes):
        S.add('dve', lambda e: e.reciprocal(out=out_, in_=in_), reads, writes)

    identf, k_identf = tile([128], F32, 'identf')
    identb, k_identb = tile([128], BF16, 'identb')
    onesb, k_onesb = tile([128], BF16, 'onesb')
    epsT, k_eps = tile([1], F32, 'eps')
    gmixT, k_gmix = tile([8], F32, 'gmix')
    ngmixT, k_ngmix = tile([8], F32, 'ngmix')
    gffnT, k_gffn = tile([8], F32, 'gffn')
    cvecT, k_cvec = tile([24], F32, 'cvec')
    dma('sp', identf, identd[:, :], [], [k_identf])
    dma('sp', gmixT, gmix[:, :], [], [k_gmix])
    dma('sp', gffnT, gffn[:, :], [], [k_gffn])
    dma('sp', cvecT, cvec[:, :], [], [k_cvec])
    cp('dve', identb, identf, [k_identf], [k_identb])
    memset('dve', onesb, 1.0, [k_onesb])
    memset('dve', epsT, EPS, [k_eps])
    ts('dve', ngmixT, gmixT, -1.0, None, ALU.mult, None, [k_gmix], [k_ngmix])
    persist_mark = st['off']

    def rms_to_T(xt, k_xt, i, ssT, k_ss, junk, k_junk, xn, k_xn, trb, dstT, k_dst, evac_eng):
        kx = list(k_xt) if isinstance(k_xt, list) else [k_xt]
        act(junk, xt, AF.Square, kx, [k_junk, (k_ss, i, 0)], accum=ssT[:, 3 * i:3 * i + 1])
        act(ssT[:, 3 * i + 1:3 * i + 2], ssT[:, 3 * i:3 * i + 1], AF.Sqrt, [(k_ss, i, 0), k_eps], [(k_ss, i, 1)],
            bias=epsT[:, 0:1], scale=1.0 / D)
        recip(ssT[:, 3 * i + 2:3 * i + 3], ssT[:, 3 * i + 1:3 * i + 2], [(k_ss, i, 1)], [(k_ss, i, 2)])
        ts('dve', xn, xt, ssT[:, 3 * i + 2:3 * i + 3], None, ALU.mult, None, kx + [(k_ss, i, 2)], [k_xn])
        pT = bankbf(trb).rearrange("p (a b) -> p a b", a=8)
        for kc in range(8):
            tr(pT[:, kc, :], xn[:, kc * 128:(kc + 1) * 128], identb, [k_xn, k_identb], [PB(trb)])
        cp(evac_eng, dstT, pT, [PB(trb)], [k_dst])

    def load_w(src3, ncols, stage, k_stage, dst, k_dst, gT, k_g, ceng, rot=None):
        if isinstance(src3, list):
            for (s_ap, st_ap) in src3:
                dma('sp', st_ap, s_ap, [], [k_stage])
        else:
            dma('sp', stage, src3, [], [k_stage])
        if gT is None:
            cp(ceng, dst, stage, [k_stage], [k_dst])
        else:
            for kc in range(dst.shape[1]):
                ts(ceng, dst[:, kc], stage[:, kc], gT[:, kc:kc + 1], 0.0, ALU.mult, ALU.add,
                   [k_stage, k_g], [k_dst])
        if rot is not None:
            dstr, k_dstr, half, ngT, k_ng = rot
            for kc in range(8):
                sv = stage[:, kc].rearrange("p (b two h) -> p b two h", two=2, h=half)
                dv = dstr[:, kc].rearrange("p (b two h) -> p b two h", two=2, h=half)
                ts(ceng, dv[:, :, 0, :], sv[:, :, 1, :], ngT[:, kc:kc + 1], 0.0, ALU.mult, ALU.add,
                   [k_stage, k_ng], [k_dstr])
                ts(ceng, dv[:, :, 1, :], sv[:, :, 0, :], gT[:, kc:kc + 1], 0.0, ALU.mult, ALU.add,
                   [k_stage, k_g], [k_dstr])

    def wsrc(w2d, col0, ncols):
        return w2d[:, col0:col0 + ncols].rearrange("(kc p) c -> p kc c", p=128)

    if 'A' in phases:
        hT, k_hT = tile([8, LP], BF16, 'hT')

        def hk(t0, n):
            return [(k_hT, i) for i in range(t0 // 128, (t0 + n) // 128)]

        m0 = st['off']
        xt_r = ring(3, [D], F32, 'xt')
        xn_r = ring(2, [D], BF16, 'xn')
        junk, k_junk = tile([D], F32, 'junk')
        ssT, k_ss = tile([3 * NT], F32, 'ss')
        for i in range(NT):
            xt, k_xt = xt_r.next()
            xn, k_xn = xn_r.next()
            dma('sp', xt, xs[i * 128:(i + 1) * 128, :], [], [k_xt])
            rms_to_T(xt, k_xt, i, ssT, k_ss, junk, k_junk, xn, k_xn, 6 + (i % 2),
                     hT[:, :, i * 128:(i + 1) * 128], (k_hT, i), 'act' if i % 2 else 'dve')

        mA = st['off']
        stage_r = ring(2, [8, 256], F32, 'stage')

        S.enabled = '1' in asub
        wA_r = ring(2, [8, 256], BF16, 'wA')
        dg_r = ring(2, [31, 128], BF16, 'dg')
        glu_r = ring(2, [30 + LP], BF16, 'glu')
        sg_r = ring(2, [512], F32, 'sg')
        cv_r = ring(3, [512], BF16, 'cv')
        dwTt, k_dwT = tile([8 * 31], F32, 'dwT')
        dma('sp', dwTt, dwT[:, :], [], [k_dwT])
        for g_, kg in glu_r.items:
            memset('pool', g_[:, 0:30], 0.0, [(kg, -1)])
        pa_r = Ring([0, 1])
        pg_r = Ring([2, 3])
        pc_r = Ring([4, 5])
        for cc in range(8):
            stg, k_stg = stage_r.next()
            wA, k_wA = wA_r.next()
            dg, k_dg = dg_r.next()
            glu, k_glu = glu_r.next()
            src = [(wsrc(w_in, cc * 128, 128), stg[:, :, 0:128]), (wsrc(w_in, 1024 + cc * 128, 128), stg[:, :, 128:256])]
            load_w(src, 256, stg, k_stg, wA, k_wA, gmixT, k_gmix, 'pool')
            for k in range(31):
                ts('pool', dg[:, k, :], identb, dwTt[:, cc * 31 + k:cc * 31 + k + 1], 0.0, ALU.mult, ALU.add,
                   [k_identb, k_dwT], [k_dg])

            def proj(tg):
                t0, n = TGS[tg]
                pa = pa_r.next()
                pg = pg_r.next()
                for kc in range(8):
                    mm(bank(pa)[:, 0:n], wA[:, kc, 0:128], hT[:, kc, t0:t0 + n], kc == 0, kc == 7,
                       [k_wA] + hk(t0, n), [PB(pa)])
                for kc in range(8):
                    mm(bank(pg)[:, 0:n], wA[:, kc, 128:256], hT[:, kc, t0:t0 + n], kc == 0, kc == 7,
                       [k_wA] + hk(t0, n), [PB(pg)])
                sg, k_sg = sg_r.next()
                act(sg[:, 0:n], bank(pg)[:, 0:n], AF.Sigmoid, [PB(pg)], [k_sg])
                tt('dve', glu[:, 30 + t0:30 + t0 + n], bank(pa)[:, 0:n], sg[:, 0:n], ALU.mult,
                   [PB(pa), k_sg], [(k_glu, tg)])

            def conv(tg):
                t0, n = TGS[tg]
                pc = pc_r.next()
                for k in range(31):
                    mm(bank(pc)[:, 0:n], dg[:, k, :], glu[:, t0 + k:t0 + k + n], k == 0, k == 30,
                       [k_dg, (k_glu, tg), (k_glu, tg - 1)], [PB(pc)])
                cv, k_cv = cv_r.next()
                act(cv[:, 0:n], bank(pc)[:, 0:n], AF.Identity, [PB(pc), k_cvec], [k_cv], bias=cvecT[:, cc:cc + 1])
                dma('pool', CVd[:, cc, t0:t0 + n], cv[:, 0:n], [k_cv], [('CVd', cc, tg)])

            for tg in range(len(TGS) + 1):
                if tg < len(TGS):
                    proj(tg)
                if tg >= 1:
                    conv(tg - 1)

        S.barrier()
        mA = m0
        st['off'] = mA
        stage_r = ring(1, [8, 256], F32, 'stage')
        S.enabled = '2' in asub
        WR, k_WR = tile([8, 15 * 128], BF16, 'WR')
        protf, k_protf = tile([256], F32, 'protf')
        protb, k_protb = tile([256], BF16, 'protb')
        dma('sp', protf, protd[:, :], [], [k_protf])
        cp('dve', protb, protf, [k_protf], [k_protb])
        blocks = [(C_Q, 0), (C_Q + 256, 2), (C_Q + 512, 4), (C_Q + 768, 6), (C_K, 8), (C_QI, 10), (C_QI + 256, 12)]
        stage2_r = ring(2, [8, 256], F32, 'stage2')
        for bi, (col0, ch0) in enumerate(blocks):
            stg, k_stg = stage2_r.next()
            load_w(wsrc(w_in, col0, 256), 256, stg, k_stg, WR[:, :, ch0 * 128:(ch0 + 2) * 128], (k_WR, bi),
                   gmixT, k_gmix, 'pool' if bi % 2 else 'dve')
        stg, k_stg = stage2_r.next()
        for hh in range(2):
            load_w(wsrc(w_in, C_KI, 64), 64, stg[:, :, hh * 64:(hh + 1) * 64], k_stg,
                   WR[:, :, 14 * 128 + hh * 64:14 * 128 + (hh + 1) * 64], (k_WR, 7 + hh), gmixT, k_gmix, 'dve')
        kWRall = [(k_WR, b) for b in range(9)]
        rp_r = ring(2, [4, 512], F32, 'rp')
        t1_r = ring(2, [512], F32, 't1')
        t2_r = ring(2, [512], F32, 't2')
        ob_r = ring(3, [512], BF16, 'ob')
        qb_r = ring(3, [512], BF16, 'qb')
        qf_r = ring(3, [512], F32, 'qf')
        pA_r = Ring([0, 1, 2])
        pB_r = Ring([3, 4, 5])
        work = [(tg, c) for tg in range(len(TGS)) for c in range(15)]
        rpof = {}
        pAof = {}

        def a2_mm(w):
            tg, c = work[w]
            t0, n = TGS[tg]
            if c == 0:
                rp, k_rp = rp_r.next()
                dma('sp', rp[:, 0:2, 0:n], rope128[:, :, t0:t0 + n].rearrange("a p t -> p a t"), [], [(k_rp, 0)])
                dma('sp', rp[:, 2:4, 0:n], rope64[:, :, t0:t0 + n].rearrange("a p t -> p a t"), [], [(k_rp, 1)])
                rpof[tg] = (rp, k_rp)
            pA = pA_r.next()
            pAof[w] = pA
            for kc in range(8):
                mm(bank(pA)[:, 0:n], WR[:, kc, c * 128:(c + 1) * 128], hT[:, kc, t0:t0 + n], kc == 0, kc == 7,
                   kWRall + hk(t0, n), [PB(pA)])

        a2_mm(0)
        for w, (tg, c) in enumerate(work):
            t0, n = TGS[tg]
            if w + 1 < len(work):
                a2_mm(w + 1)
            rp, k_rp = rpof[tg]
            pA = pAof[w]
            pB = pB_r.next()
            qb, k_qb = qb_r.next()
            cp('act', qb[:, 0:n], bank(pA)[:, 0:n], [PB(pA)], [k_qb])
            qf, k_qf = qf_r.next()
            cp('act', qf[:, 0:n], bank(pA)[:, 0:n], [PB(pA)], [k_qf])
            pcol = 0 if c < 10 else 128
            mm(bank(pB)[:, 0:n], protb[:, pcol:pcol + 128], qb[:, 0:n], True, True, [k_protb, k_qb], [PB(pB)])
            ti = 0 if c < 10 else 2
            t1, k_t1 = t1_r.next()
            t2, k_t2 = t2_r.next()
            ob, k_ob = ob_r.next()
            tt('dve', t1[:, 0:n], qf[:, 0:n], rp[:, ti, 0:n], ALU.mult, [k_qf, (k_rp, ti // 2)], [k_t1])
            tt('dve', t2[:, 0:n], bank(pB)[:, 0:n], rp[:, ti + 1, 0:n], ALU.mult, [PB(pB), (k_rp, ti // 2)], [k_t2])
            tt('pool' if w % 2 else 'dve', ob[:, 0:n], t1[:, 0:n], t2[:, 0:n], ALU.add, [k_t1, k_t2], [k_ob])
            if c < 8:
                dst = QTd[:, c, t0:t0 + n]
            elif c < 10:
                dst = KTd[:, c - 8, t0:t0 + n]
            elif c < 14:
                dst = QITd[:, c - 10, t0:t0 + n]
            else:
                dst = KITd[:, t0:t0 + n]
            dma('pool', dst, ob[:, 0:n], [k_ob], [('A2o', c, tg)])

        S.barrier()
        st['off'] = mA
        stage_r = ring(2, [8, 256], F32, 'stage')
        S.enabled = '3' in asub
        wv, k_wv = tile([8, 264], BF16, 'wv')
        stg, k_stg = stage_r.next()
        load_w(wsrc(w_in, C_V, 256), 256, stg, k_stg, wv[:, :, 0:256], (k_wv, 0), gmixT, k_gmix, 'dve')
        stg, k_stg = stage_r.next()
        dma('sp', stg[:, :, 0:64], wsrc(w_in, C_WI, 64), [], [k_stg])
        for kc in range(8):
            ts('dve', wv[:, kc, 256:264], stg[:, kc, 0:8], gmixT[:, kc:kc + 1], 0.0, ALU.mult, ALU.add,
               [k_stg, k_gmix], [(k_wv, 1)])
        vt_r = ring(3, [2, 130], BF16, 'vt')
        wi_r = ring(3, [16], F32, 'wi')
        wr_r = ring(3, [8], F32, 'wr')
        for v_, kv in vt_r.items:
            memset('dve', v_, 1.0, [(kv, 'one'), (kv, 'v')])
        pv_r = Ring([6, 7])
        CIDX = (8.0 ** -0.5) * (64.0 ** -0.5)
        for i in range(NT):
            pv = pv_r.next()
            for kc in range(8):
                mm(bank(pv)[:, 0:264], hT[:, kc, i * 128:(i + 1) * 128], wv[:, kc, :], kc == 0, kc == 7,
                   [(k_wv, 0), (k_wv, 1), (k_hT, i)], [PB(pv)])
            vt, k_vt = vt_r.next()
            wi, k_wi = wi_r.next()
            CUT = int(_os.environ.get('A3CUT', '9'))
            if CUT < 3:
                continue
            cp('act', vt[:, :, 0:128], bank(pv)[:, 0:256].rearrange("p (g d) -> p g d", g=2), [PB(pv)], [(k_vt, 'v')])
            if CUT < 4:
                continue
            wr, k_wr = wr_r.next()
            cp('act', wr, bank(pv)[:, 256:264], [PB(pv)], [k_wr])
            ts('dve', wi[:, 8:16], wr, 0.0, 0.5, ALU.is_ge, ALU.subtract, [k_wr], [(k_wi, 1)])
            stt(wi[:, 0:8], wr, 4.0 * CIDX, wi[:, 8:16], ALU.mult, ALU.mult, [k_wr, (k_wi, 1)], [(k_wi, 0)])
            if CUT < 5:
                continue
            dma('sp', Vd[:, i, :], vt.rearrange("p g d -> p (g d)"), [(k_vt, 'v'), (k_vt, 'one')], [('Vd', i)])
            dma('sp', WId[:, i, :], wi, [(k_wi, 0), (k_wi, 1)], [('WId', i)])

        S.enabled = '4' in asub
        wg_r = ring(2, [8, 256], BF16, 'wg')
        sgo_r = ring(3, [512], F32, 'sgo')
        pq_r = Ring([0, 1, 2, 3])
        for blk in range(8):
            stg, k_stg = stage_r.next()
            wg, k_wg = wg_r.next()
            load_w(wsrc(w_in, C_G + blk * 256, 256), 256, stg, k_stg, wg, k_wg, gmixT, k_gmix, 'pool' if blk % 2 else 'dve')
            for tg, (t0, n) in enumerate(TGS):
                for cl in range(2):
                    c = blk * 2 + cl
                    pq = pq_r.next()
                    for kc in range(8):
                        mm(bank(pq)[:, 0:n], wg[:, kc, cl * 128:(cl + 1) * 128], hT[:, kc, t0:t0 + n], kc == 0, kc == 7,
                           [k_wg] + hk(t0, n), [PB(pq)])
                    sgo, k_sgo = sgo_r.next()
                    act(sgo[:, 0:n], bank(pq)[:, 0:n], AF.Sigmoid, [PB(pq)], [k_sgo])
                    dma('pool', SGd[:, c, t0:t0 + n], sgo[:, 0:n], [k_sgo], [('SGd', c, tg)])
        S.enabled = True
        S.barrier()
        st['off'] = persist_mark

    if 'C' in phases:
        KT, k_KT = tile([2, LP], BF16, 'KT')
        KIT, k_KIT = tile([LP], BF16, 'KIT')
        Vt, k_V = tile([NT, 260], BF16, 'V')
        WIt, k_WI = tile([NT, 16], F32, 'WI')
        cmt, k_cm = tile([256], F32, 'cm')
        cnegb, k_cnegb = tile([128], BF16, 'cnegb')
        negI4, k_negI4 = tile([512], BF16, 'negI4')
        pow2, k_pow2 = tile([NIT + 1], F32, 'pow2')
        dma('sp', KT, KTd[:, :, :], [], [k_KT])
        dma('sp', KIT, KITd[:, :], [], [k_KIT])
        dma('sp', Vt, Vd[:, :, :], [], [k_V])
        dma('sp', WIt, WId[:, :, :], [], [k_WI])
        dma('sp', cmt, cmaskd[:, :], [], [k_cm])
        dma('sp', pow2, pow2d[:, :], [], [k_pow2])
        cp('dve', cnegb, cmt[:, 128:256], [k_cm], [k_cnegb])
        for r4 in range(4):
            ts('dve', negI4[:, r4 * 128:(r4 + 1) * 128], identf, -32768.0, None, ALU.mult, None, [k_identf], [k_negI4])
        score_r = ring(2, [LP], F32, 'score')
        mneg_r = ring(2, [LP], BF16, 'mneg')
        junkb, k_junkb = tile([LP], BF16, 'junkb')
        junka, k_junka = tile([LP], BF16, 'junka')
        R_r = ring(4, [512], BF16, 'R')
        dg_r = ring(2, [8, 128], BF16, 'dgs')
        SCB = 7
        qt_r = ring(2, [8, 128], BF16, 'qt')
        qp_r = ring(2, [4, 2, 128], BF16, 'qp')
        pt_r = ring(3, [512], BF16, 'pt')
        obuf_r = ring(2, [D], BF16, 'obuf')
        oT_r = ring(2, [8, 128], BF16, 'oTt')
        sm_r = ring(2, [8 + 2 * NIT + 16], F32, 'sm')
        rden_r = ring(2, [8], F32, 'rden')
        for qp_, kq in qp_r.items:
            memset('pool', qp_, 0.0, [(kq, 0), (kq, 1)])
        pi_r = Ring([0, 1])
        pl_r = Ring([2, 3])
        ACCB = [4, 5, 6]
        TRB = 2
        idx_state = {}
        ACT_COUNT = set()

        sc_state = {}

        def gen_scores(i):
            nk = 128 * (i + 1)
            qp, k_qp = qp_r.next()
            dma('sp', qp[0:64, :, 0, :], QITd[0:64, :, i * 128:(i + 1) * 128], [], [(k_qp, 0)])
            dma('sp', qp[64:128, :, 1, :], QITd[64:128, :, i * 128:(i + 1) * 128], [], [(k_qp, 1)])
            score, k_sc = score_r.next()
            sm, k_sm = sm_r.next()
            dg, k_dg = dg_r.next()
            for h in range(8):
                ts('dve', dg[:, h, :], identb, WIt[:, i, 8 + h:9 + h], None, ALU.mult, None, [k_identb, k_WI], [(k_dg, h)])
            rounds = [(k0, h) for k0 in range(0, nk, 512) for h in range(8)]
            piof = {}
            rof = {}

            def emit_mm(r):
                k0, h = rounds[r]
                n = min(512, nk - k0)
                c, par = h // 2, h % 2
                pi = pi_r.next()
                piof[r] = pi
                mm(bank(pi)[:, 0:n], qp[:, c, par, :], KIT[:, k0:k0 + n],
                   True, True, [(k_qp, 0), (k_qp, 1), k_KIT], [PB(pi)])

            def emit_sum(r):
                k0, h = rounds[r]
                n = min(512, nk - k0)
                R, k_R = rof[r]
                mm(bank(SCB)[:, 0:n], dg[:, h, :], R[:, 0:n], h == 0, h == 7, [k_R, (k_dg, h)], [PB(SCB)])
                if h == 7:
                    cp('dve', score[:, k0:k0 + n], bank(SCB)[:, 0:n], [PB(SCB)], [(k_sc, k0)])

            emit_mm(0)
            for r, (k0, h) in enumerate(rounds):
                n = min(512, nk - k0)
                if r + 1 < len(rounds):
                    emit_mm(r + 1)
                pi = piof[r]
                R, k_R = R_r.next()
                rof[r] = (R, k_R)
                act(R[:, 0:n], bank(pi)[:, 0:n], AF.Relu, [PB(pi), k_WI], [k_R], scale=WIt[:, i, h:h + 1])
                if r >= 1:
                    emit_sum(r - 1)
                yield
            emit_sum(len(rounds) - 1)
            allsc = [(k_sc, k0) for k0 in range(0, nk, 512)]
            memset('dve', score[:, 0:16], 1e30, [(k_sc, 0)])
            d0 = 128 * i
            tt('dve', score[:, d0:d0 + 128], score[:, d0:d0 + 128], cmt[:, 0:128], ALU.min,
               [k_cm, (k_sc, d0 // 512 * 512)], [(k_sc, d0 // 512 * 512)])
            S.add('dve', lambda e: e.tensor_reduce(out=sm[:, 0:1], in_=score[:, 16:nk], op=ALU.max, axis=AX.X),
                  allsc, [(k_sm, 'hi')])
            yield
            S.add('dve', lambda e: e.tensor_reduce(out=sm[:, 1:2], in_=score[:, 0:d0], op=ALU.min, axis=AX.X),
                  allsc, [(k_sm, 'lo')])
            tt('dve', sm[:, 2:3], sm[:, 0:1], sm[:, 1:2], ALU.subtract, [(k_sm, 'hi'), (k_sm, 'lo')], [(k_sm, 'r')])
            stt(sm[:, 8:9], sm[:, 2:3], 0.5, sm[:, 1:2], ALU.mult, ALU.add, [(k_sm, 'r'), (k_sm, 'lo')], [(k_sm, 'mid', 0)])
            ts('dve', sm[:, 8 + NIT + 1:8 + 2 * NIT + 2], pow2, sm[:, 2:3], None, ALU.mult, None, [k_pow2, (k_sm, 'r')],
               [(k_sm, 'rk')])
            sc_state[i] = (score, k_sc, sm, k_sm, nk, allsc)
            yield

        def gen_bisect(i):
            score, k_sc, sm, k_sm, nk, allsc = sc_state[i]
            mneg, k_mn = mneg_r.next()
            for k in range(1, NIT + 1):
                midp = sm[:, 8 + k - 1:8 + k]
                if k in ACT_COUNT:
                    act(junka[:, 0:nk], score[:, 0:nk], AF.Sign, allsc + [(k_sm, 'mid', k - 1)], [k_junka, (k_sm, 'cnt')],
                        bias=midp, scale=-1.0, accum=sm[:, 3:4])
                    ts('dve', sm[:, 4:5], sm[:, 3:4], float(nk - 511), 0.5, ALU.is_le, ALU.subtract, [(k_sm, 'cnt')], [(k_sm, 'sg')])
                else:
                    ts('dve', junkb[:, 0:nk], score[:, 0:nk], midp, None, ALU.is_ge, ALU.add,
                       allsc + [(k_sm, 'mid', k - 1)], [k_junkb, (k_sm, 'cnt')], accum=sm[:, 3:4])
                    ts('dve', sm[:, 4:5], sm[:, 3:4], 255.5, 0.5, ALU.is_ge, ALU.subtract, [(k_sm, 'cnt')], [(k_sm, 'sg')])
                stt(sm[:, 8 + k:8 + k + 1], sm[:, 4:5], sm[:, 8 + NIT + 1 + k:8 + NIT + 2 + k], midp, ALU.mult, ALU.add,
                    [(k_sm, 'sg'), (k_sm, 'rk'), (k_sm, 'mid', k - 1)], [(k_sm, 'mid', k)])
                yield
            stt(sm[:, 5:6], sm[:, 8 + 2 * NIT + 1:8 + 2 * NIT + 2], -0.5, sm[:, 8 + NIT:8 + NIT + 1], ALU.mult, ALU.add,
                [(k_sm, 'rk'), (k_sm, 'mid', NIT)], [(k_sm, 'thr')])
            ts('dve', mneg[:, 0:nk], score[:, 0:nk], sm[:, 5:6], None, ALU.is_lt, None, allsc + [(k_sm, 'thr')], [k_mn])
            idx_state[i] = (mneg, k_mn)
            yield

        def attention(i):
            qt, k_qt = qt_r.next()
            dma('sp', qt, QTd[:, :, i * 128:(i + 1) * 128], [], [k_qt])
            first_in_bank = {4: True, 5: True, 6: True}
            steps = [(j, g) for j in range(i + 1) for g in range(2)]
            plof = {}

            def emit_qk(sidx):
                j, g = steps[sidx]
                pl = pl_r.next()
                plof[sidx] = pl
                need_mask = (i >= 2) or (j == i)
                mm(bank(pl), KT[:, g, j * 128:(j + 1) * 128], qt[:, 4 * g:4 * g + 4, :], True, not need_mask,
                   [k_KT, k_qt], [PB(pl)])
                if need_mask:
                    if i >= 2:
                        mneg, k_mn = idx_state[i]
                        mm(bank(pl), mneg[:, j * 128:(j + 1) * 128], negI4, False, True, [k_mn, k_negI4], [PB(pl)])
                    else:
                        mm(bank(pl), cnegb, negI4, False, True, [k_cnegb, k_negI4], [PB(pl)])

            emit_qk(0)
            for sidx, (j, g) in enumerate(steps):
                if sidx + 1 < len(steps):
                    emit_qk(sidx + 1)
                pl = plof[sidx]
                pt, k_pt = pt_r.next()
                act(pt, bank(pl), AF.Exp, [PB(pl)], [k_pt], scale=128.0 ** -0.5)
                for hh in range(4):
                    h = 4 * g + hh
                    b = ACCB[h // 3]
                    o0 = (h % 3) * 129
                    stf = (j == 0) and first_in_bank[b]
                    if j == 0:
                        first_in_bank[b] = False
                    mm(bank(b)[:, o0:o0 + 129], pt[:, hh * 128:(hh + 1) * 128], Vt[:, j, g * 130:g * 130 + 129],
                       stf, j == i, [k_pt, k_V], [PB(b)], sgc=True)
                yield
            rden, k_rden = rden_r.next()
            ob, k_ob = obuf_r.next()
            for h in range(8):
                b = ACCB[h // 3]
                o0 = (h % 3) * 129
                recip(rden[:, h:h + 1], bank(b)[:, o0 + 128:o0 + 129], [PB(b)], [(k_rden, h)])
                ts('dve', ob[:, h * 128:(h + 1) * 128], bank(b)[:, o0:o0 + 128], rden[:, h:h + 1], None, ALU.mult, None,
                   [PB(b), (k_rden, h)], [(k_ob, h)])
            pT = bankbf(TRB).rearrange("p (a b) -> p a b", a=8)
            for h in range(8):
                tr(pT[:, h, :], ob[:, h * 128:(h + 1) * 128], identb, [(k_ob, h), k_identb], [PB(TRB)])
            oTt, k_oT = oT_r.next()
            cp('act', oTt, pT, [PB(TRB)], [k_oT])
            dma('pool', OTd[:, :, i * 128:(i + 1) * 128], oTt, [k_oT], [('OTd', i)])
            yield

        def run_interleaved(gens):
            state = [[g, max(n, 1), 0, True] for g, n in gens if g is not None]
            while any(a[3] for a in state):
                best = None
                for a in state:
                    if a[3] and (best is None or a[2] / a[1] < best[2] / best[1]):
                        best = a
                try:
                    next(best[0])
                    best[2] += 1
                except StopIteration:
                    best[3] = False

        for T in range(NT):
            gens = [(attention(T), 2 * (T + 1) + 1)]
            if 2 <= T + 1 < NT:
                gens.append((gen_bisect(T + 1), NIT + 1))
            if 2 <= T + 2 < NT:
                gens.append((gen_scores(T + 2), 8 * ((128 * (T + 3) + 511) // 512) + 2))
            run_interleaved(gens)
        S.barrier()
        st['off'] = persist_mark

    if 'D' in phases:
        wpw, k_wpw = tile([8, D], BF16, 'wpw')
        wo, k_wo = tile([8, D], BF16, 'wo')
        wm, k_wm = tile([8, D], BF16, 'wm')
        mD = st['off']
        stage_r = ring(2, [8, 256], F32, 'stage')
        bi = 0
        for (wsrc2, wdst, kd) in ((w_pw, wpw, k_wpw), (w_o, wo, k_wo), (w_m, wm, k_wm)):
            for b4 in range(4):
                stg, k_stg = stage_r.next()
                load_w(wsrc(wsrc2, b4 * 256, 256), 256, stg, k_stg, wdst[:, :, b4 * 256:(b4 + 1) * 256], (kd, b4),
                       None, None, 'pool' if bi % 2 else 'dve')
                bi += 1
        S.barrier()
        st['off'] = mD
        kwpw, kwo, kwm = [], [], []
        cvl, k_cvl = tile([8, 512], BF16, 'cvl')
        ot_r = ring(2, [8, 512], BF16, 'otl')
        zt_r = ring(2, [8, 512], BF16, 'zt')
        mt_r = ring(2, [8, 512], BF16, 'mt')
        lnt, k_lnt = tile([4, 512], F32, 'lnt')
        rn_r = ring(2, [2, 512], F32, 'rn')
        tA_r = ring(2, [512], F32, 'tA')
        tB_r = ring(2, [512], F32, 'tB')
        tA2_r = ring(1, [512], F32, 'tA2')
        tB2_r = ring(1, [512], F32, 'tB2')
        sga_r = ring(2, [512], F32, 'sga')
        sgb_r = ring(2, [512], F32, 'sgb')
        xr_r = ring(2, [D], F32, 'xr')
        s1_r = ring(2, [D], F32, 's1')
        xn_r = ring(2, [D], BF16, 'xn2')
        h2_r = ring(2, [8, 128], BF16, 'h2t')
        junk, k_junk = tile([D], BF16, 'junkD')
        ssT, k_ss = tile([3 * NT], F32, 'ssD')
        py_r = Ring([2, 3, 4, 5])
        dstate = {}

        def d_ln(tg):
            t0, n = TGS[tg]
            zt, k_zt = zt_r.next()
            otl, k_otl = ot_r.next()
            rn, k_rn = rn_r.next()
            dma('sp', cvl[:, :, 0:n], CVd[:, :, t0:t0 + n], [], [k_cvl])
            dma('sp', otl[:, :, 0:n], OTd[:, :, t0:t0 + n], [], [k_otl])
            kzt = [(k_zt, cc) for cc in range(8)]
            act(zt[:, :, 0:n], cvl[:, :, 0:n], AF.Square, [k_cvl], kzt)
            for kc in range(8):
                mm(bank(0)[:, 0:n], onesb, cvl[:, kc, 0:n], kc == 0, kc == 7, [k_onesb, k_cvl], [PB(0)])
            for kc in range(8):
                mm(bank(1)[:, 0:n], onesb, zt[:, kc, 0:n], kc == 0, kc == 7, [k_onesb] + kzt, [PB(1)])
            dstate[tg] = dict(zt=zt, kzt=kzt, otl=otl, k_otl=k_otl)
            yield
            mu, musq, var, sdv = [lnt[:, q, 0:n] for q in range(4)]
            rstd, nmr = rn[:, 0, 0:n], rn[:, 1, 0:n]
            ts('dve', mu, bank(0)[:, 0:n], 1.0 / D, None, ALU.mult, None, [PB(0)], [(k_lnt, 0)])
            tt('dve', musq, mu, mu, ALU.mult, [(k_lnt, 0)], [(k_lnt, 1)])
            stt(var, bank(1)[:, 0:n], 1.0 / D, musq, ALU.mult, ALU.subtract, [PB(1), (k_lnt, 1)], [(k_lnt, 2)])
            ts('dve', var, var, 0.0, None, ALU.max, None, [(k_lnt, 2)], [(k_lnt, 2)])
            act(sdv, var, AF.Sqrt, [(k_lnt, 2), k_eps], [(k_lnt, 3)], bias=epsT[:, 0:1], scale=1.0)
            recip(rstd, sdv, [(k_lnt, 3)], [(k_rn, 0)])
            stt(nmr, mu, -1.0, rstd, ALU.mult, ALU.mult, [(k_lnt, 0), (k_rn, 0)], [(k_rn, 1)])
            yield
            for cc in range(8):
                tA, k_tA = tA2_r.next()
                tB, k_tB = tB2_r.next()
                tt('dve', tA[:, 0:n], cvl[:, cc, 0:n], rstd, ALU.mult, [k_cvl, (k_rn, 0)], [k_tA])
                tt('dve', tB[:, 0:n], tA[:, 0:n], nmr, ALU.add, [k_tA, (k_rn, 1)], [k_tB])
                act(zt[:, cc, 0:n], tB[:, 0:n], AF.Silu, [k_tB, k_cvec], [(k_zt, cc)],
                    bias=cvecT[:, 16 + cc:17 + cc], scale=cvecT[:, 8 + cc:9 + cc])
                yield

        def d_proj(tg):
            t0, n = TGS[tg]
            d = dstate[tg]
            zt, kzt, otl, k_otl = d['zt'], d['kzt'], d['otl'], d['k_otl']
            mt, k_mt = mt_r.next()
            for c in range(8):
                sga, k_sga = sga_r.next()
                sgb, k_sgb = sgb_r.next()
                dma('sp', sga[:, 0:n], SGd[:, c, t0:t0 + n], [], [k_sga])
                dma('sp', sgb[:, 0:n], SGd[:, 8 + c, t0:t0 + n], [], [k_sgb])
                pya = py_r.next()
                pyb = py_r.next()
                for kc in range(8):
                    mm(bank(pya)[:, 0:n], wpw[:, kc, c * 128:(c + 1) * 128], zt[:, kc, 0:n], kc == 0, kc == 7,
                       kwpw + kzt, [PB(pya)])
                for kc in range(8):
                    mm(bank(pyb)[:, 0:n], wo[:, kc, c * 128:(c + 1) * 128], otl[:, kc, 0:n], kc == 0, kc == 7,
                       kwo + [k_otl], [PB(pyb)])
                tA, k_tA = tA_r.next()
                tB, k_tB = tB_r.next()
                tt('dve', tA[:, 0:n], bank(pya)[:, 0:n], sga[:, 0:n], ALU.mult, [PB(pya), k_sga], [k_tA])
                tt('dve', tB[:, 0:n], bank(pyb)[:, 0:n], sgb[:, 0:n], ALU.mult, [PB(pyb), k_sgb], [k_tB])
                tt('pool', mt[:, c, 0:n], tA[:, 0:n], tB[:, 0:n], ALU.add, [k_tA, k_tB], [(k_mt, c)])
                d['mt'] = mt
                d['kmt'] = [(k_mt, cq) for cq in range(8)]
                yield

        po_r = Ring([6, 7])

        def d_merge(tg):
            t0, n = TGS[tg]
            d = dstate[tg]
            mt, kmt = d['mt'], d['kmt']
            ntl = n // 128
            m1 = {}

            def M1(tl):
                i = t0 // 128 + tl
                xr, k_xr = xr_r.next()
                s1, k_s1 = s1_r.next()
                dma('sp', xr, xs[i * 128:(i + 1) * 128, :], [], [k_xr])
                for half in range(2):
                    po = po_r.next()
                    for c in range(8):
                        mm(bank(po), mt[:, c, tl * 128:(tl + 1) * 128], wm[:, c, half * 512:(half + 1) * 512],
                           c == 0, c == 7, kmt + kwm, [PB(po)])
                    tt('dve', s1[:, half * 512:(half + 1) * 512], bank(po), xr[:, half * 512:(half + 1) * 512], ALU.add,
                       [PB(po), k_xr], [(k_s1, half)])
                dma('pool', S1d[i * 128:(i + 1) * 128, :], s1, [(k_s1, 0), (k_s1, 1)], [('S1d', i)])
                m1[tl] = (s1, k_s1)

            def M2(tl):
                i = t0 // 128 + tl
                s1, k_s1 = m1[tl]
                xn, k_xn = xn_r.next()
                h2t, k_h2 = h2_r.next()
                rms_to_T(s1, [(k_s1, 0), (k_s1, 1)], i, ssT, k_ss, junk, k_junk, xn, k_xn, 1, h2t, k_h2, 'act')
                dma('pool', H2Td[:, :, i * 128:(i + 1) * 128], h2t, [k_h2], [('H2Td', i)])

            M1(0)
            yield
            for tl in range(ntl):
                if tl + 1 < ntl:
                    M1(tl + 1)
                M2(tl)
                yield

        NG = len(TGS)

        def alternate(*gens):
            alive = [g for g in gens if g is not None]
            while alive:
                for g in list(alive):
                    try:
                        next(g)
                    except StopIteration:
                        alive.remove(g)

        for _ in d_ln(0):
            pass
        for tg in range(NG + 1):
            ga = d_ln(tg + 1) if tg + 1 < NG else None
            gb = d_proj(tg) if tg < NG else None
            gc = d_merge(tg - 1) if tg >= 1 else None
            alternate(gb, ga, gc)
        S.barrier()
        st['off'] = persist_mark

    if 'E' in phases:
        h2T, k_h2T = tile([8, LP], BF16, 'h2T')
        for q4 in range(4):
            c0 = q4 * 1056
            dma('sp', h2T[:, :, c0:c0 + 1056], H2Td[:, :, c0:c0 + 1056], [], [(k_h2T, q4)])
        kh2 = [(k_h2T, q4) for q4 in range(4)]
        fdwT, k_fdw = tile([44 * 3], F32, 'fdw')
        fdbT, k_fdb = tile([44], F32, 'fdb')
        dma('sp', fdwT, fdw[:, :], [], [k_fdw])
        dma('sp', fdbT, fdb[:, :], [], [k_fdb])
        stage_r = ring(2, [8, 256], F32, 'stage')
        wu_r = ring(2, [8, 256], BF16, 'wu')
        ua_r = ring(4, [514], F32, 'ua')
        ub_r = ring(4, [514], F32, 'ub')
        ca_r = ring(4, [512], F32, 'ca')
        cb_r = ring(4, [512], F32, 'cb')
        sa_r = ring(3, [512], F32, 'sa')
        at_r = ring(3, [512], BF16, 'at')
        pa_r = Ring([0, 1, 2])
        pb_r = Ring([3, 4, 5])
        est = {}
        wts = {}
        iters = [(p, tg) for p in range(NPAIR) for tg in range(len(TGS))]

        def e_stage1(it):
            p, tg = iters[it]
            t0, n = TGS[tg]
            if tg == 0:
                stg, k_stg = stage_r.next()
                wu, k_wu = wu_r.next()
                src = [(wsrc(ffn_up, p * 128, 128), stg[:, :, 0:128]), (wsrc(ffn_up, 2816 + p * 128, 128), stg[:, :, 128:256])]
                load_w(src, 256, stg, k_stg, wu, k_wu, gffnT, k_gffn, 'pool')
                wts[p] = (wu, k_wu)
            wu, k_wu = wts[p]
            pa = pa_r.next()
            pb = pb_r.next()
            for kc in range(8):
                mm(bank(pa)[:, 0:n], wu[:, kc, 0:128], h2T[:, kc, t0:t0 + n], kc == 0, kc == 7, [k_wu] + kh2, [PB(pa)])
            for kc in range(8):
                mm(bank(pb)[:, 0:n], wu[:, kc, 128:256], h2T[:, kc, t0:t0 + n], kc == 0, kc == 7, [k_wu] + kh2, [PB(pb)])
            ua, k_ua = ua_r.next()
            ub, k_ub = ub_r.next()
            if tg == 0:
                memset('dve', ua[:, 0:2], 0.0, [(k_ua, 'h')])
                memset('dve', ub[:, 0:2], 0.0, [(k_ub, 'h')])
            else:
                pv_ = est[it - 1]
                pn = TGS[tg - 1][1]
                cp('act', ua[:, 0:2], pv_['ua'][:, pn:pn + 2], [(pv_['k_ua'], 'b')], [(k_ua, 'h')])
                cp('act', ub[:, 0:2], pv_['ub'][:, pn:pn + 2], [(pv_['k_ub'], 'b')], [(k_ub, 'h')])
            cp('act', ua[:, 2:2 + n], bank(pa)[:, 0:n], [PB(pa)], [(k_ua, 'b')])
            cp('act', ub[:, 2:2 + n], bank(pb)[:, 0:n], [PB(pb)], [(k_ub, 'b')])
            ca, k_ca = ca_r.next()
            cb, k_cb = cb_r.next()
            for (cx, k_cx, ci, pbk) in ((ca, k_ca, p, pa), (cb, k_cb, NPAIR + p, pb)):
                act(cx[:, 0:n], bank(pbk)[:, 0:n], AF.Identity, [PB(pbk), k_fdw, k_fdb], [k_cx],
                    bias=fdbT[:, ci:ci + 1], scale=fdwT[:, ci * 3 + 2:ci * 3 + 3])
            est[it] = dict(ua=ua, k_ua=k_ua, ub=ub, k_ub=k_ub, ca=ca, k_ca=k_ca, cb=cb, k_cb=k_cb)

        def e_stage2(it):
            p, tg = iters[it]
            t0, n = TGS[tg]
            d = est[it]
            for (u, k_u, cx, k_cx, ci) in ((d['ua'], d['k_ua'], d['ca'], d['k_ca'], p),
                                           (d['ub'], d['k_ub'], d['cb'], d['k_cb'], NPAIR + p)):
                rk = [(k_u, 'h'), (k_u, 'b'), k_fdw]
                stt(cx[:, 0:n], u[:, 1:1 + n], fdwT[:, ci * 3 + 1:ci * 3 + 2], cx[:, 0:n], ALU.mult, ALU.add,
                    rk + [k_cx], [k_cx])
                stt(cx[:, 0:n], u[:, 0:n], fdwT[:, ci * 3:ci * 3 + 1], cx[:, 0:n], ALU.mult, ALU.add,
                    rk + [k_cx], [k_cx])

        def e_stage3(it):
            p, tg = iters[it]
            t0, n = TGS[tg]
            d = est[it]
            sa, k_sa = sa_r.next()
            at, k_at = at_r.next()
            act(sa[:, 0:n], d['ca'][:, 0:n], AF.Silu, [d['k_ca']], [k_sa])
            tt('dve', at[:, 0:n], sa[:, 0:n], d['cb'][:, 0:n], ALU.mult, [k_sa, d['k_cb']], [k_at])
            dma('pool', ACTd[:, p, t0:t0 + n], at[:, 0:n], [k_at], [('ACTd', p, tg)])

        NI = len(iters)
        for step in range(NI + 2):
            if step < NI:
                e_stage1(step)
            if 0 <= step - 1 < NI:
                e_stage2(step - 1)
            if 0 <= step - 2 < NI:
                e_stage3(step - 2)
        S.barrier()
        st['off'] = persist_mark

    if 'F' in phases:
        wd, k_wd = tile([NPAIR, D], BF16, 'wd')
        fgT, k_fg = tile([D], F32, 'fg')
        dma('sp', fgT, fgb[:, :], [], [k_fg])
        stage_r = ring(2, [2, D], F32, 'stageF')
        for b in range(NPAIR // 2):
            stg, k_stg = stage_r.next()
            src = ffn_down[b * 256:(b + 1) * 256, :].rearrange("(kc p) c -> p kc c", p=128)
            dma('sp', stg, src, [], [k_stg])
            cp('pool' if b % 2 else 'dve', wd[:, 2 * b:2 * b + 2, :], stg, [k_stg], [(k_wd, b)])
        kwd = [(k_wd, b) for b in range(NPAIR // 2)]
        al_r = ring(2, [NPAIR, 128], BF16, 'al')
        s1_r = ring(2, [D], F32, 's1l')
        s2_r = ring(2, [D], F32, 's2')
        y_r = ring(2, [D], F32, 'y')
        junk, k_junk = tile([D], F32, 'junkF')
        ssT, k_ss = tile([3 * NT], F32, 'ssF')
        po_r = Ring([0, 1, 2, 3])
        for i in range(NT):
            al, k_al = al_r.next()
            s1l, k_s1l = s1_r.next()
            dma('sp', al, ACTd[:, :, i * 128:(i + 1) * 128], [], [k_al])
            dma('sp', s1l, S1d[i * 128:(i + 1) * 128, :], [], [k_s1l])
            s2, k_s2 = s2_r.next()
            for half in range(2):
                po = po_r.next()
                for kc in range(NPAIR):
                    mm(bank(po), al[:, kc, :], wd[:, kc, half * 512:(half + 1) * 512], kc == 0, kc == NPAIR - 1,
                       [k_al] + kwd, [PB(po)])
                tt('dve', s2[:, half * 512:(half + 1) * 512], bank(po), s1l[:, half * 512:(half + 1) * 512], ALU.add,
                   [PB(po), k_s1l], [(k_s2, half)])
            ks2 = [(k_s2, 0), (k_s2, 1)]
            act(junk, s2, AF.Square, ks2, [k_junk, (k_ss, i, 0)], accum=ssT[:, 3 * i:3 * i + 1])
            act(ssT[:, 3 * i + 1:3 * i + 2], ssT[:, 3 * i:3 * i + 1], AF.Sqrt, [(k_ss, i, 0), k_eps], [(k_ss, i, 1)],
                bias=epsT[:, 0:1], scale=1.0 / D)
            recip(ssT[:, 3 * i + 2:3 * i + 3], ssT[:, 3 * i + 1:3 * i + 2], [(k_ss, i, 1)], [(k_ss, i, 2)])
            y, k_y = y_r.next()
            stt(y, s2, ssT[:, 3 * i + 2:3 * i + 3], fgT, ALU.mult, ALU.mult, ks2 + [(k_ss, i, 2), k_fg], [k_y])
            if i == 0:
                dma('pool', out[0:112, :], y[16:128, :], [k_y], [('out', i)])
            elif i < NT - 1:
                dma('pool', out[i * 128 - 16:i * 128 + 112, :], y, [k_y], [('out', i)])
            else:
                dma('pool', out[4080:4096, :], y[0:16, :], [k_y], [('out', i)])

    S.emit(nc, es)
    es.close()
    build.stats = S.stats
    return nc


def _fm(v, nchunk):
    return np.ascontiguousarray(np.asarray(v, np.float32).reshape(nchunk, 128).T)


def _rope_tab(hd):
    half = hd // 2
    inv = (np.float32(10000.0) ** (-np.arange(half, dtype=np.float32) / np.float32(half))).astype(np.float32)
    pos = np.arange(LP, dtype=np.float32)
    ang = (pos[:, None] * inv[None, :]).astype(np.float32)
    cos = np.cos(ang).astype(np.float32).T
    sin = np.sin(ang).astype(np.float32).T
    reps = 128 // half
    return np.ascontiguousarray(np.stack([np.tile(cos, (reps, 1)), np.tile(sin, (reps, 1))], 0))


def _prot():
    out = np.zeros((128, 256), np.float32)
    for col0, hd in ((0, 128), (128, 64)):
        half = hd // 2
        for dp in range(128):
            b, j = dp // hd, dp % hd
            if j < half:
                out[b * hd + j + half, col0 + dp] = -1.0
            else:
                out[b * hd + j - half, col0 + dp] = 1.0
    return out


def make_in_maps(x, meta_tokens, mix_norm_g, w_in, conv_ln_g, conv_ln_b, conv_dw_w, conv_dw_b,
                 conv_pw_out, attn_w_o, w_merge_out, ffn_norm_g, ffn_up, ffn_dw_w, ffn_dw_b,
                 ffn_down, final_norm_g):
    f = lambda a: np.ascontiguousarray(np.asarray(a, np.float32))
    x = f(x)
    B = x.shape[0]
    common = {
        "w_in": f(w_in[0]),
        "gmix": _fm(mix_norm_g[0], 8),
        "dwT": np.ascontiguousarray(np.transpose(f(conv_dw_w[0]).reshape(31, 8, 128), (2, 1, 0)).reshape(128, 8 * 31)),
        "cvec": np.ascontiguousarray(np.concatenate([_fm(conv_dw_b[0], 8), _fm(conv_ln_g[0], 8), _fm(conv_ln_b[0], 8)], 1)),
        "w_pw": f(conv_pw_out[0]),
        "w_o": f(attn_w_o[0]),
        "w_m": f(w_merge_out[0]),
        "gffn": _fm(ffn_norm_g[0], 8),
        "ffn_up": f(ffn_up[0]),
        "fdw": np.ascontiguousarray(np.transpose(f(ffn_dw_w[0]).reshape(3, 44, 128), (2, 1, 0)).reshape(128, 44 * 3)),
        "fdb": _fm(ffn_dw_b[0], 44),
        "ffn_down": f(ffn_down[0]),
        "fgb": np.ascontiguousarray(np.broadcast_to(f(final_norm_g)[None, :], (128, D))),
        "rope128": _rope_tab(128),
        "rope64": _rope_tab(64),
        "identd": np.eye(128, dtype=np.float32),
        "protd": _prot(),
        "pow2d": np.ascontiguousarray(np.broadcast_to((2.0 ** -np.arange(NIT + 1, dtype=np.float32))[None, :], (128, NIT + 1))),
    }
    tq = np.arange(128)[:, None]
    sk = np.arange(128)[None, :]
    vis = sk <= tq
    cm = np.concatenate([np.where(vis, np.float32(3e38), np.float32(-1e30)),
                         np.where(vis, np.float32(0.0), np.float32(1.0))], 1).astype(np.float32)
    common["cmaskd"] = np.ascontiguousarray(cm)
    meta = f(meta_tokens)
    pad = np.zeros((LP - L, D), np.float32)
    maps = []
    for b in range(B):
        d = dict(common)
        d["xs"] = np.ascontiguousarray(np.concatenate([meta, x[b], pad], 0))
        maps.append(d)
    return maps


_NC_CACHE = {}


def kernel(**inputs):
    maps = make_in_maps(**inputs)
    if 'nc' not in _NC_CACHE:
        _NC_CACHE['nc'] = build()
    nc = _NC_CACHE['nc']
    res = run_bass_kernel_spmd(nc, maps, core_ids=list(range(len(maps))))
    outs = [np.asarray(r["out"], np.float32) for r in res.results]
    return np.stack(outs, 0)
```

```python
import numpy as np
from contextlib import ExitStack
import concourse.bass as bass
import concourse.mybir as mybir
from concourse.bass_utils import run_bass_kernel_spmd

F32 = mybir.dt.float32
BF16 = mybir.dt.bfloat16
AF = mybir.ActivationFunctionType
ALU = mybir.AluOpType
AX = mybir.AxisListType

L = 4112
LP = 4224
NT = 33
D = 1024
KC = 8
NPAIR = 22
EPS = 1e-6
NIT = 17
C_Q, C_K, C_V, C_QI, C_KI, C_WI, C_G = 2048, 3072, 3328, 3584, 4096, 4160, 4168
TGS = [(t0, min(512, LP - t0)) for t0 in range(0, LP, 512)]

ENGS = ['pe', 'act', 'dve', 'pool', 'sp']
NDMASEM = 24


class Op:
    __slots__ = ('eng', 'fn', 'reads', 'writes', 'deps', 'sig', 'tok', 'is_dma', 'idx', 'prevdma')

    def __init__(self, eng, fn, reads, writes, is_dma):
        self.eng = eng
        self.fn = fn
        self.reads = reads
        self.writes = writes
        self.is_dma = is_dma
        self.deps = []
        self.sig = False
        self.tok = None
        self.prevdma = None


class Sched:
    def __init__(self):
        self.ops = {e: [] for e in ENGS}
        self.lastw = {}
        self.readers = {}
        self.n = 0
        self.pending = {e: [] for e in ENGS}
        self.lastc = {e: None for e in ENGS}
        self.dmas_since = []

    def barrier(self):
        for f in ENGS:
            lst = [self.lastc[e] for e in ENGS if e != f and self.lastc[e] is not None]
            self.pending[f] = lst + list(self.dmas_since)
        self.dmas_since = []
        self.lastw = {}
        self.readers = {}

    def add(self, eng, fn, reads=(), writes=(), dma=False):
        if not getattr(self, 'enabled', True):
            return None
        op = Op(eng, fn, tuple(reads), tuple(writes), dma)
        op.idx = self.n
        self.n += 1
        deps = {}
        for r in op.reads:
            w = self.lastw.get(r)
            if w is not None:
                deps[w.idx] = (w, True)
        for wk in op.writes:
            lw = self.lastw.get(wk)
            if lw is not None and lw.idx not in deps:
                deps[lw.idx] = (lw, False)
            for rd in self.readers.get(wk, ()):
                if rd.idx not in deps:
                    deps[rd.idx] = (rd, False)
        for d, raw in deps.values():
            if d is op:
                continue
            if (not d.is_dma) and (not op.is_dma) and d.eng == op.eng:
                if not raw or op.eng == 'pe':
                    continue
            op.deps.append(d)
            d.sig = True
        if self.pending[eng]:
            for d in self.pending[eng]:
                if d is not op and d not in op.deps:
                    op.deps.append(d)
                    d.sig = True
            self.pending[eng] = []
        for r in op.reads:
            self.readers.setdefault(r, []).append(op)
        for wk in op.writes:
            self.lastw[wk] = op
            self.readers[wk] = []
        self.ops[eng].append(op)
        if dma:
            self.dmas_since.append(op)
        else:
            self.lastc[eng] = op
        return op

    def emit(self, nc, es):
        esem = {e: es.enter_context(nc.semaphore('s_' + e)) for e in ENGS}
        dsem = [es.enter_context(nc.semaphore('d_%d' % i)) for i in range(NDMASEM)]
        duse = [0] * NDMASEM
        dlast = [None] * NDMASEM
        allops = sorted([o for e in ENGS for o in self.ops[e]], key=lambda o: o.idx)
        qengs = sorted({o.eng for o in allops if o.is_dma})
        per = NDMASEM // max(1, len(qengs))
        qsems = {e: list(range(i * per, (i + 1) * per)) for i, e in enumerate(qengs)}
        qrr = {e: 0 for e in qengs}
        cnt = {e: 0 for e in ENGS}
        pos = {}
        for e in ENGS:
            for n_, o in enumerate(self.ops[e]):
                pos[id(o)] = n_ + 1
        needed = set()
        for e in ENGS:
            waited0 = {}
            for o in self.ops[e]:
                best0 = {}
                for d in o.deps:
                    if d.is_dma:
                        continue
                    if pos[id(d)] > best0.get(d.eng, (0, None))[0]:
                        best0[d.eng] = (pos[id(d)], d)
                for k_, (v_, d_) in best0.items():
                    if waited0.get(k_, 0) < v_:
                        needed.add(id(d_))
                        waited0[k_] = v_
        for e in ENGS:
            for o in self.ops[e]:
                if not o.is_dma:
                    o.sig = id(o) in needed
        for o in allops:
            if o.is_dma:
                s = qsems[o.eng][qrr[o.eng] % per]
                qrr[o.eng] += 1
                duse[s] += 1
                o.prevdma = dlast[s]
                o.tok = (('d', s), 16 * duse[s])
                dlast[s] = o
            elif o.sig:
                cnt[o.eng] += 1
                o.tok = (('e', o.eng), cnt[o.eng])
        self.stats = dict(cnt)

        def semof(key):
            return esem[key[1]] if key[0] == 'e' else dsem[key[1]]

        for e in ENGS:
            nxt = None
            for o in reversed(self.ops[e]):
                if o.is_dma:
                    continue
                if o.sig:
                    nxt = o.tok
                elif nxt is not None:
                    o.tok = nxt
                else:
                    o.tok = None

        def run(ename, eng):
            waited = {}
            for o in self.ops[ename]:
                deps = list(o.deps)
                if o.is_dma and o.prevdma is not None:
                    deps.append(o.prevdma)
                best = {}
                for d in deps:
                    assert d.tok is not None, "dependency on op with no later signal"
                    k, v = d.tok
                    if best.get(k, 0) < v:
                        best[k] = v
                for k, v in best.items():
                    if waited.get(k, 0) < v:
                        eng.wait_ge(semof(k), v)
                        waited[k] = v
                ins = o.fn(eng)
                if o.is_dma:
                    ins.then_inc(semof(o.tok[0]), 16)
                elif o.sig:
                    ins.then_inc(semof(o.tok[0]), 1)

        with nc.Block() as block:
            @block.tensor
            def _(eng):
                run('pe', eng)

            @block.scalar
            def _(eng):
                run('act', eng)

            @block.vector
            def _(eng):
                run('dve', eng)

            @block.gpsimd
            def _(eng):
                run('pool', eng)

            @block.sync
            def _(eng):
                run('sp', eng)
                for s in range(NDMASEM):
                    if duse[s] > 0:
                        eng.wait_ge(dsem[s], 16 * duse[s])


class Ring:
    def __init__(self, items):
        self.items = items
        self.i = 0

    def next(self):
        it = self.items[self.i % len(self.items)]
        self.i += 1
        return it


def build(debug=False, phases="ACDEF", asub="1234"):
    nc = bass.Bass("TRN2", target_bir_lowering=False)

    def din(name, shape, dt=F32):
        return nc.dram_tensor(name, list(shape), dt, kind="ExternalInput").ap()

    skind = "ExternalOutput" if debug else "Internal"

    def dscr(name, shape, dt):
        return nc.dram_tensor(name, list(shape), dt, kind=skind).ap()

    xs = din("xs", [LP, D])
    w_in = din("w_in", [D, 6216])
    gmix = din("gmix", [128, 8])
    dwT = din("dwT", [128, 8 * 31])
    cvec = din("cvec", [128, 24])
    w_pw = din("w_pw", [D, D])
    w_o = din("w_o", [D, D])
    w_m = din("w_m", [D, D])
    gffn = din("gffn", [128, 8])
    ffn_up = din("ffn_up", [D, 2 * 2816])
    fdw = din("fdw", [128, 44 * 3])
    fdb = din("fdb", [128, 44])
    ffn_down = din("ffn_down", [2816, D])
    fgb = din("fgb", [128, D])
    rope128 = din("rope128", [2, 128, LP])
    rope64 = din("rope64", [2, 128, LP])
    identd = din("identd", [128, 128])
    protd = din("protd", [128, 256])
    cmaskd = din("cmaskd", [128, 256])
    pow2d = din("pow2d", [128, NIT + 1])
    out = nc.dram_tensor("out", [4096, D], F32, kind="ExternalOutput").ap()

    CVd = dscr("CVd", [128, 8, LP], BF16)
    QTd = dscr("QTd", [128, 8, LP], BF16)
    KTd = dscr("KTd", [128, 2, LP], BF16)
    QITd = dscr("QITd", [128, 4, LP], BF16)
    KITd = dscr("KITd", [128, LP], BF16)
    Vd = dscr("Vd", [128, NT, 260], BF16)
    WId = dscr("WId", [128, NT, 16], F32)
    SGd = dscr("SGd", [128, 16, LP], F32)
    OTd = dscr("OTd", [128, 8, LP], BF16)
    S1d = dscr("S1d", [LP, D], F32)
    H2Td = dscr("H2Td", [128, 8, LP], BF16)
    ACTd = dscr("ACTd", [128, NPAIR, LP], BF16)

    S = Sched()
    es = ExitStack()
    import os as _os
    AW = int(_os.environ.get('AWK', '42')) * 1024
    arena = es.enter_context(nc.sbuf_tensor("arena", [128, AW], F32))
    psum = es.enter_context(nc.psum_tensor("psum", [128, 8, 512], F32))
    st = {'off': 0}

    def alloc(shape, dt):
        n = int(np.prod(shape))
        words = n if dt == F32 else (n + 1) // 2
        words = (words + 3) // 4 * 4
        off = st['off']
        if off + words > AW and not getattr(S, 'enabled', True):
            off = 0
        st['off'] = off + words
        assert st['off'] <= AW, ("arena overflow", st['off'])
        ap = arena[:, off:off + words]
        if dt != F32:
            ap = ap.bitcast(dt)
        ap = ap[:, 0:n]
        if len(shape) == 2:
            ap = ap.rearrange("p (a b) -> p a b", a=shape[0])
        elif len(shape) == 3:
            ap = ap.rearrange("p (a b c) -> p a b c", a=shape[0], b=shape[1])
        return ap

    uid = [0]

    def tile(shape, dt, name=None):
        uid[0] += 1
        return (alloc(shape, dt), (name or 't', uid[0]))

    def ring(n, shape, dt, name=None):
        return Ring([tile(shape, dt, name) for _ in range(n)])

    def bank(b):
        return psum[:, b, :]

    def bankbf(b):
        return psum[:, b, :].bitcast(BF16)

    def PB(b):
        return ('ps', b)

    def dma(q, out_, in_, reads, writes):
        S.add(q, lambda e: e.dma_start(out=out_, in_=in_), reads, writes, dma=True)

    def mm(out_, lhsT, rhs, start, stop, reads, writes, sgc=False):
        if sgc:
            S.add('pe', lambda e: e.matmul(out_, lhsT=lhsT, rhs=rhs, start=start, stop=stop, skip_group_check=True), reads, writes)
        else:
            S.add('pe', lambda e: e.matmul(out_, lhsT=lhsT, rhs=rhs, start=start, stop=stop), reads, writes)

    def tr(out_, in_, ident, reads, writes):
        S.add('pe', lambda e: e.transpose(out=out_, in_=in_, identity=ident), reads, writes)

    def act(out_, in_, func, reads, writes, bias=None, scale=None, accum=None):
        kw = {}
        if bias is not None:
            kw['bias'] = bias
        if scale is not None:
            kw['scale'] = scale
        if accum is not None:
            kw['accum_out'] = accum
        S.add('act', lambda e: e.activation(out=out_, in_=in_, func=func, **kw), reads, writes)

    def ts(eng, out_, in0, s1, s2, op0, op1, reads, writes, accum=None):
        if accum is not None:
            S.add(eng, lambda e: e.tensor_scalar(out=out_, in0=in0, scalar1=s1, scalar2=s2, op0=op0, op1=op1, accum_out=accum), reads, writes)
        elif op1 is None:
            S.add(eng, lambda e: e.tensor_scalar(out=out_, in0=in0, scalar1=s1, scalar2=None, op0=op0), reads, writes)
        else:
            S.add(eng, lambda e: e.tensor_scalar(out=out_, in0=in0, scalar1=s1, scalar2=s2, op0=op0, op1=op1), reads, writes)

    def tt(eng, out_, in0, in1, op, reads, writes):
        S.add(eng, lambda e: e.tensor_tensor(out=out_, in0=in0, in1=in1, op=op), reads, writes)

    def stt(out_, in0, scalar, in1, op0, op1, reads, writes):
        S.add('dve', lambda e: e.scalar_tensor_tensor(out=out_, in0=in0, scalar=scalar, in1=in1, op0=op0, op1=op1), reads, writes)

    def cp(eng, out_, in_, reads, writes):
        if eng == 'act':
            act(out_, in_, AF.Copy, reads, writes)
        else:
            S.add(eng, lambda e: e.tensor_copy(out=out_, in_=in_), reads, writes)

    def memset(eng, ap, val, writes):
        S.add(eng, lambda e: e.memset(ap, val), (), writes)

    def recip(out_, in_, reads, writes):
        S.add('dve', lambda e: e.reciprocal(out=out_, in_=in_), reads, writes)

    identf, k_identf = tile([128], F32, 'identf')
    identb, k_identb = tile([128], BF16, 'identb')
    onesb, k_onesb = tile([128], BF16, 'onesb')
    epsT, k_eps = tile([1], F32, 'eps')
    gmixT, k_gmix = tile([8], F32, 'gmix')
    ngmixT, k_ngmix = tile([8], F32, 'ngmix')
    gffnT, k_gffn = tile([8], F32, 'gffn')
    cvecT, k_cvec = tile([24], F32, 'cvec')
    dma('sp', identf, identd[:, :], [], [k_identf])
    dma('sp', gmixT, gmix[:, :], [], [k_gmix])
    dma('sp', gffnT, gffn[:, :], [], [k_gffn])
    dma('sp', cvecT, cvec[:, :], [], [k_cvec])
    cp('dve', identb, identf, [k_identf], [k_identb])
    memset('dve', onesb, 1.0, [k_onesb])
    memset('dve', epsT, EPS, [k_eps])
    ts('dve', ngmixT, gmixT, -1.0, None, ALU.mult, None, [k_gmix], [k_ngmix])
    persist_mark = st['off']

    def rms_to_T(xt, k_xt, i, ssT, k_ss, junk, k_junk, xn, k_xn, trb, dstT, k_dst, evac_eng):
        kx = list(k_xt) if isinstance(k_xt, list) else [k_xt]
        act(junk, xt, AF.Square, kx, [k_junk, (k_ss, i, 0)], accum=ssT[:, 3 * i:3 * i + 1])
        act(ssT[:, 3 * i + 1:3 * i + 2], ssT[:, 3 * i:3 * i + 1], AF.Sqrt, [(k_ss, i, 0), k_eps], [(k_ss, i, 1)],
            bias=epsT[:, 0:1], scale=1.0 / D)
        recip(ssT[:, 3 * i + 2:3 * i + 3], ssT[:, 3 * i + 1:3 * i + 2], [(k_ss, i, 1)], [(k_ss, i, 2)])
        ts('dve', xn, xt, ssT[:, 3 * i + 2:3 * i + 3], None, ALU.mult, None, kx + [(k_ss, i, 2)], [k_xn])
        pT = bankbf(trb).rearrange("p (a b) -> p a b", a=8)
        for kc in range(8):
            tr(pT[:, kc, :], xn[:, kc * 128:(kc + 1) * 128], identb, [k_xn, k_identb], [PB(trb)])
        cp(evac_eng, dstT, pT, [PB(trb)], [k_dst])

    def load_w(src3, ncols, stage, k_stage, dst, k_dst, gT, k_g, ceng, rot=None):
        if isinstance(src3, list):
            for (s_ap, st_ap) in src3:
                dma('sp', st_ap, s_ap, [], [k_stage])
        else:
            dma('sp', stage, src3, [], [k_stage])
        if gT is None:
            cp(ceng, dst, stage, [k_stage], [k_dst])
        else:
            for kc in range(dst.shape[1]):
                ts(ceng, dst[:, kc], stage[:, kc], gT[:, kc:kc + 1], 0.0, ALU.mult, ALU.add,
                   [k_stage, k_g], [k_dst])
        if rot is not None:
            dstr, k_dstr, half, ngT, k_ng = rot
            for kc in range(8):
                sv = stage[:, kc].rearrange("p (b two h) -> p b two h", two=2, h=half)
                dv = dstr[:, kc].rearrange("p (b two h) -> p b two h", two=2, h=half)
                ts(ceng, dv[:, :, 0, :], sv[:, :, 1, :], ngT[:, kc:kc + 1], 0.0, ALU.mult, ALU.add,
                   [k_stage, k_ng], [k_dstr])
                ts(ceng, dv[:, :, 1, :], sv[:, :, 0, :], gT[:, kc:kc + 1], 0.0, ALU.mult, ALU.add,
                   [k_stage, k_g], [k_dstr])

    def wsrc(w2d, col0, ncols):
        return w2d[:, col0:col0 + ncols].rearrange("(kc p) c -> p kc c", p=128)

    if 'A' in phases:
        hT, k_hT = tile([8, LP], BF16, 'hT')

        def hk(t0, n):
            return [(k_hT, i) for i in range(t0 // 128, (t0 + n) // 128)]

        m0 = st['off']
        xt_r = ring(3, [D], F32, 'xt')
        xn_r = ring(2, [D], BF16, 'xn')
        junk, k_junk = tile([D], F32, 'junk')
        ssT, k_ss = tile([3 * NT], F32, 'ss')
        for i in range(NT):
            xt, k_xt = xt_r.next()
            xn, k_xn = xn_r.next()
            dma('sp', xt, xs[i * 128:(i + 1) * 128, :], [], [k_xt])
            rms_to_T(xt, k_xt, i, ssT, k_ss, junk, k_junk, xn, k_xn, 6 + (i % 2),
                     hT[:, :, i * 128:(i + 1) * 128], (k_hT, i), 'act' if i % 2 else 'dve')

        mA = st['off']
        stage_r = ring(2, [8, 256], F32, 'stage')

        S.enabled = '1' in asub
        wA_r = ring(2, [8, 256], BF16, 'wA')
        dg_r = ring(2, [31, 128], BF16, 'dg')
        glu_r = ring(2, [30 + LP], BF16, 'glu')
        sg_r = ring(2, [512], F32, 'sg')
        cv_r = ring(3, [512], BF16, 'cv')
        dwTt, k_dwT = tile([8 * 31], F32, 'dwT')
        dma('sp', dwTt, dwT[:, :], [], [k_dwT])
        for g_, kg in glu_r.items:
            memset('pool', g_[:, 0:30], 0.0, [(kg, -1)])
        pa_r = Ring([0, 1])
        pg_r = Ring([2, 3])
        pc_r = Ring([4, 5])
        for cc in range(8):
            stg, k_stg = stage_r.next()
            wA, k_wA = wA_r.next()
            dg, k_dg = dg_r.next()
            glu, k_glu = glu_r.next()
            src = [(wsrc(w_in, cc * 128, 128), stg[:, :, 0:128]), (wsrc(w_in, 1024 + cc * 128, 128), stg[:, :, 128:256])]
            load_w(src, 256, stg, k_stg, wA, k_wA, gmixT, k_gmix, 'pool')
            for k in range(31):
                ts('pool', dg[:, k, :], identb, dwTt[:, cc * 31 + k:cc * 31 + k + 1], 0.0, ALU.mult, ALU.add,
                   [k_identb, k_dwT], [k_dg])

            def proj(tg):
                t0, n = TGS[tg]
                pa = pa_r.next()
                pg = pg_r.next()
                for kc in range(8):
                    mm(bank(pa)[:, 0:n], wA[:, kc, 0:128], hT[:, kc, t0:t0 + n], kc == 0, kc == 7,
                       [k_wA] + hk(t0, n), [PB(pa)])
                for kc in range(8):
                    mm(bank(pg)[:, 0:n], wA[:, kc, 128:256], hT[:, kc, t0:t0 + n], kc == 0, kc == 7,
                       [k_wA] + hk(t0, n), [PB(pg)])
                sg, k_sg = sg_r.next()
                act(sg[:, 0:n], bank(pg)[:, 0:n], AF.Sigmoid, [PB(pg)], [k_sg])
                tt('dve', glu[:, 30 + t0:30 + t0 + n], bank(pa)[:, 0:n], sg[:, 0:n], ALU.mult,
                   [PB(pa), k_sg], [(k_glu, tg)])

            def conv(tg):
                t0, n = TGS[tg]
                pc = pc_r.next()
                for k in range(31):
                    mm(bank(pc)[:, 0:n], dg[:, k, :], glu[:, t0 + k:t0 + k + n], k == 0, k == 30,
                       [k_dg, (k_glu, tg), (k_glu, tg - 1)], [PB(pc)])
                cv, k_cv = cv_r.next()
                act(cv[:, 0:n], bank(pc)[:, 0:n], AF.Identity, [PB(pc), k_cvec], [k_cv], bias=cvecT[:, cc:cc + 1])
                dma('pool', CVd[:, cc, t0:t0 + n], cv[:, 0:n], [k_cv], [('CVd', cc, tg)])

            for tg in range(len(TGS) + 1):
                if tg < len(TGS):
                    proj(tg)
                if tg >= 1:
                    conv(tg - 1)

        S.barrier()
        mA = m0
        st['off'] = mA
        stage_r = ring(1, [8, 256], F32, 'stage')
        S.enabled = '2' in asub
        WR, k_WR = tile([8, 15 * 128], BF16, 'WR')
        protf, k_protf = tile([256], F32, 'protf')
        protb, k_protb = tile([256], BF16, 'protb')
        dma('sp', protf, protd[:, :], [], [k_protf])
        cp('dve', protb, protf, [k_protf], [k_protb])
        blocks = [(C_Q, 0), (C_Q + 256, 2), (C_Q + 512, 4), (C_Q + 768, 6), (C_K, 8), (C_QI, 10), (C_QI + 256, 12)]
        stage2_r = ring(2, [8, 256], F32, 'stage2')
        for bi, (col0, ch0) in enumerate(blocks):
            stg, k_stg = stage2_r.next()
            load_w(wsrc(w_in, col0, 256), 256, stg, k_stg, WR[:, :, ch0 * 128:(ch0 + 2) * 128], (k_WR, bi),
                   gmixT, k_gmix, 'pool' if bi % 2 else 'dve')
        stg, k_stg = stage2_r.next()
        for hh in range(2):
            load_w(wsrc(w_in, C_KI, 64), 64, stg[:, :, hh * 64:(hh + 1) * 64], k_stg,
                   WR[:, :, 14 * 128 + hh * 64:14 * 128 + (hh + 1) * 64], (k_WR, 7 + hh), gmixT, k_gmix, 'dve')
        kWRall = [(k_WR, b) for b in range(9)]
        rp_r = ring(2, [4, 512], F32, 'rp')
        t1_r = ring(2, [512], F32, 't1')
        t2_r = ring(2, [512], F32, 't2')
        ob_r = ring(3, [512], BF16, 'ob')
        qb_r = ring(3, [512], BF16, 'qb')
        qf_r = ring(3, [512], F32, 'qf')
        pA_r = Ring([0, 1, 2])
        pB_r = Ring([3, 4, 5])
        work = [(tg, c) for tg in range(len(TGS)) for c in range(15)]
        rpof = {}
        pAof = {}

        def a2_mm(w):
            tg, c = work[w]
            t0, n = TGS[tg]
            if c == 0:
                rp, k_rp = rp_r.next()
                dma('sp', rp[:, 0:2, 0:n], rope128[:, :, t0:t0 + n].rearrange("a p t -> p a t"), [], [(k_rp, 0)])
                dma('sp', rp[:, 2:4, 0:n], rope64[:, :, t0:t0 + n].rearrange("a p t -> p a t"), [], [(k_rp, 1)])
                rpof[tg] = (rp, k_rp)
            pA = pA_r.next()
            pAof[w] = pA
            for kc in range(8):
                mm(bank(pA)[:, 0:n], WR[:, kc, c * 128:(c + 1) * 128], hT[:, kc, t0:t0 + n], kc == 0, kc == 7,
                   kWRall + hk(t0, n), [PB(pA)])

        a2_mm(0)
        for w, (tg, c) in enumerate(work):
            t0, n = TGS[tg]
            if w + 1 < len(work):
                a2_mm(w + 1)
            rp, k_rp = rpof[tg]
            pA = pAof[w]
            pB = pB_r.next()
            qb, k_qb = qb_r.next()
            cp('act', qb[:, 0:n], bank(pA)[:, 0:n], [PB(pA)], [k_qb])
            qf, k_qf = qf_r.next()
            cp('act', qf[:, 0:n], bank(pA)[:, 0:n], [PB(pA)], [k_qf])
            pcol = 0 if c < 10 else 128
            mm(bank(pB)[:, 0:n], protb[:, pcol:pcol + 128], qb[:, 0:n], True, True, [k_protb, k_qb], [PB(pB)])
            ti = 0 if c < 10 else 2
            t1, k_t1 = t1_r.next()
            t2, k_t2 = t2_r.next()
            ob, k_ob = ob_r.next()
            tt('dve', t1[:, 0:n], qf[:, 0:n], rp[:, ti, 0:n], ALU.mult, [k_qf, (k_rp, ti // 2)], [k_t1])
            tt('dve', t2[:, 0:n], bank(pB)[:, 0:n], rp[:, ti + 1, 0:n], ALU.mult, [PB(pB), (k_rp, ti // 2)], [k_t2])
            tt('pool' if w % 2 else 'dve', ob[:, 0:n], t1[:, 0:n], t2[:, 0:n], ALU.add, [k_t1, k_t2], [k_ob])
            if c < 8:
                dst = QTd[:, c, t0:t0 + n]
            elif c < 10:
                dst = KTd[:, c - 8, t0:t0 + n]
            elif c < 14:
                dst = QITd[:, c - 10, t0:t0 + n]
            else:
                dst = KITd[:, t0:t0 + n]
            dma('pool', dst, ob[:, 0:n], [k_ob], [('A2o', c, tg)])

        S.barrier()
        st['off'] = mA
        stage_r = ring(2, [8, 256], F32, 'stage')
        S.enabled = '3' in asub
        wv, k_wv = tile([8, 264], BF16, 'wv')
        stg, k_stg = stage_r.next()
        load_w(wsrc(w_in, C_V, 256), 256, stg, k_stg, wv[:, :, 0:256], (k_wv, 0), gmixT, k_gmix, 'dve')
        stg, k_stg = stage_r.next()
        dma('sp', stg[:, :, 0:64], wsrc(w_in, C_WI, 64), [], [k_stg])
        for kc in range(8):
            ts('dve', wv[:, kc, 256:264], stg[:, kc, 0:8], gmixT[:, kc:kc + 1], 0.0, ALU.mult, ALU.add,
               [k_stg, k_gmix], [(k_wv, 1)])
        vt_r = ring(3, [2, 130], BF16, 'vt')
        wi_r = ring(3, [16], F32, 'wi')
        wr_r = ring(3, [8], F32, 'wr')
        for v_, kv in vt_r.items:
            memset('dve', v_, 1.0, [(kv, 'one'), (kv, 'v')])
        pv_r = Ring([6, 7])
        CIDX = (8.0 ** -0.5) * (64.0 ** -0.5)
        for i in range(NT):
            pv = pv_r.next()
            for kc in range(8):
                mm(bank(pv)[:, 0:264], hT[:, kc, i * 128:(i + 1) * 128], wv[:, kc, :], kc == 0, kc == 7,
                   [(k_wv, 0), (k_wv, 1), (k_hT, i)], [PB(pv)])
            vt, k_vt = vt_r.next()
            wi, k_wi = wi_r.next()
            CUT = int(_os.environ.get('A3CUT', '9'))
            if CUT < 3:
                continue
            cp('act', vt[:, :, 0:128], bank(pv)[:, 0:256].rearrange("p (g d) -> p g d", g=2), [PB(pv)], [(k_vt, 'v')])
            if CUT < 4:
                continue
            wr, k_wr = wr_r.next()
            cp('act', wr, bank(pv)[:, 256:264], [PB(pv)], [k_wr])
            ts('dve', wi[:, 8:16], wr, 0.0, 0.5, ALU.is_ge, ALU.subtract, [k_wr], [(k_wi, 1)])
            stt(wi[:, 0:8], wr, 4.0 * CIDX, wi[:, 8:16], ALU.mult, ALU.mult, [k_wr, (k_wi, 1)], [(k_wi, 0)])
            if CUT < 5:
                continue
            dma('sp', Vd[:, i, :], vt.rearrange("p g d -> p (g d)"), [(k_vt, 'v'), (k_vt, 'one')], [('Vd', i)])
            dma('sp', WId[:, i, :], wi, [(k_wi, 0), (k_wi, 1)], [('WId', i)])

        S.enabled = '4' in asub
        wg_r = ring(2, [8, 256], BF16, 'wg')
        sgo_r = ring(3, [512], F32, 'sgo')
        pq_r = Ring([0, 1, 2, 3])
        for blk in range(8):
            stg, k_stg = stage_r.next()
            wg, k_wg = wg_r.next()
            load_w(wsrc(w_in, C_G + blk * 256, 256), 256, stg, k_stg, wg, k_wg, gmixT, k_gmix, 'pool' if blk % 2 else 'dve')
            for tg, (t0, n) in enumerate(TGS):
                for cl in range(2):
                    c = blk * 2 + cl
                    pq = pq_r.next()
                    for kc in range(8):
                        mm(bank(pq)[:, 0:n], wg[:, kc, cl * 128:(cl + 1) * 128], hT[:, kc, t0:t0 + n], kc == 0, kc == 7,
                           [k_wg] + hk(t0, n), [PB(pq)])
                    sgo, k_sgo = sgo_r.next()
                    act(sgo[:, 0:n], bank(pq)[:, 0:n], AF.Sigmoid, [PB(pq)], [k_sgo])
                    dma('pool', SGd[:, c, t0:t0 + n], sgo[:, 0:n], [k_sgo], [('SGd', c, tg)])
        S.enabled = True
        S.barrier()
        st['off'] = persist_mark

    if 'C' in phases:
        KT, k_KT = tile([2, LP], BF16, 'KT')
        KIT, k_KIT = tile([LP], BF16, 'KIT')
        Vt, k_V = tile([NT, 260], BF16, 'V')
        WIt, k_WI = tile([NT, 16], F32, 'WI')
        cmt, k_cm = tile([256], F32, 'cm')
        cnegb, k_cnegb = tile([128], BF16, 'cnegb')
        negI4, k_negI4 = tile([512], BF16, 'negI4')
        pow2, k_pow2 = tile([NIT + 1], F32, 'pow2')
        dma('sp', KT, KTd[:, :, :], [], [k_KT])
        dma('sp', KIT, KITd[:, :], [], [k_KIT])
        dma('sp', Vt, Vd[:, :, :], [], [k_V])
        dma('sp', WIt, WId[:, :, :], [], [k_WI])
        dma('sp', cmt, cmaskd[:, :], [], [k_cm])
        dma('sp', pow2, pow2d[:, :], [], [k_pow2])
        cp('dve', cnegb, cmt[:, 128:256], [k_cm], [k_cnegb])
        for r4 in range(4):
            ts('dve', negI4[:, r4 * 128:(r4 + 1) * 128], identf, -32768.0, None, ALU.mult, None, [k_identf], [k_negI4])
        score_r = ring(2, [LP], F32, 'score')
        mneg_r = ring(2, [LP], BF16, 'mneg')
        junkb, k_junkb = tile([LP], BF16, 'junkb')
        junka, k_junka = tile([LP], BF16, 'junka')
        R_r = ring(4, [512], BF16, 'R')
        dg_r = ring(2, [8, 128], BF16, 'dgs')
        SCB = 7
        qt_r = ring(2, [8, 128], BF16, 'qt')
        qp_r = ring(2, [4, 2, 128], BF16, 'qp')
        pt_r = ring(3, [512], BF16, 'pt')
        obuf_r = ring(2, [D], BF16, 'obuf')
        oT_r = ring(2, [8, 128], BF16, 'oTt')
        sm_r = ring(2, [8 + 2 * NIT + 16], F32, 'sm')
        rden_r = ring(2, [8], F32, 'rden')
        for qp_, kq in qp_r.items:
            memset('pool', qp_, 0.0, [(kq, 0), (kq, 1)])
        pi_r = Ring([0, 1])
        pl_r = Ring([2, 3])
        ACCB = [4, 5, 6]
        TRB = 2
        idx_state = {}
        ACT_COUNT = set()

        sc_state = {}
        CMX = 8 + 2 * NIT + 4

        def gen_scores(i):
            nk = 128 * (i + 1)
            qp, k_qp = qp_r.next()
            dma('sp', qp[0:64, :, 0, :], QITd[0:64, :, i * 128:(i + 1) * 128], [], [(k_qp, 0)])
            dma('sp', qp[64:128, :, 1, :], QITd[64:128, :, i * 128:(i + 1) * 128], [], [(k_qp, 1)])
            score, k_sc = score_r.next()
            sm, k_sm = sm_r.next()
            dg, k_dg = dg_r.next()
            for h in range(8):
                ts('dve', dg[:, h, :], identb, WIt[:, i, 8 + h:9 + h], None, ALU.mult, None, [k_identb, k_WI], [(k_dg, h)])
            rounds = [(k0, h) for k0 in range(0, nk, 512) for h in range(8)]
            piof = {}
            rof = {}

            def emit_mm(r):
                k0, h = rounds[r]
                n = min(512, nk - k0)
                c, par = h // 2, h % 2
                pi = pi_r.next()
                piof[r] = pi
                mm(bank(pi)[:, 0:n], qp[:, c, par, :], KIT[:, k0:k0 + n],
                   True, True, [(k_qp, 0), (k_qp, 1), k_KIT], [PB(pi)])

            def emit_sum(r):
                k0, h = rounds[r]
                n = min(512, nk - k0)
                R, k_R = rof[r]
                mm(bank(SCB)[:, 0:n], dg[:, h, :], R[:, 0:n], h == 0, h == 7, [k_R, (k_dg, h)], [PB(SCB)])
                if h == 7:
                    ci_ = k0 // 512
                    ts('dve', score[:, k0:k0 + n], bank(SCB)[:, 0:n], 1.0, None, ALU.mult, ALU.max,
                       [PB(SCB)], [(k_sc, k0), (k_sm, 'cm', ci_)], accum=sm[:, CMX + ci_:CMX + ci_ + 1])

            emit_mm(0)
            for r, (k0, h) in enumerate(rounds):
                n = min(512, nk - k0)
                if r + 1 < len(rounds):
                    emit_mm(r + 1)
                pi = piof[r]
                R, k_R = R_r.next()
                rof[r] = (R, k_R)
                act(R[:, 0:n], bank(pi)[:, 0:n], AF.Relu, [PB(pi), k_WI], [k_R], scale=WIt[:, i, h:h + 1])
                if r >= 1:
                    emit_sum(r - 1)
                yield
            emit_sum(len(rounds) - 1)
            allsc = [(k_sc, k0) for k0 in range(0, nk, 512)]
            memset('dve', score[:, 0:16], 1e30, [(k_sc, 0)])
            d0 = 128 * i
            tt('dve', score[:, d0:d0 + 128], score[:, d0:d0 + 128], cmt[:, 0:128], ALU.min,
               [k_cm, (k_sc, d0 // 512 * 512)], [(k_sc, d0 // 512 * 512)])
            nch_ = (nk + 511) // 512
            S.add('dve', lambda e: e.tensor_reduce(out=sm[:, 0:1], in_=sm[:, CMX:CMX + nch_], op=ALU.max, axis=AX.X),
                  [(k_sm, 'cm', c_) for c_ in range(nch_)], [(k_sm, 'hi')])
            yield
            S.add('dve', lambda e: e.tensor_reduce(out=sm[:, 1:2], in_=score[:, 0:d0], op=ALU.min, axis=AX.X),
                  allsc, [(k_sm, 'lo')])
            tt('dve', sm[:, 2:3], sm[:, 0:1], sm[:, 1:2], ALU.subtract, [(k_sm, 'hi'), (k_sm, 'lo')], [(k_sm, 'r')])
            stt(sm[:, 8:9], sm[:, 2:3], 0.5, sm[:, 1:2], ALU.mult, ALU.add, [(k_sm, 'r'), (k_sm, 'lo')], [(k_sm, 'mid', 0)])
            ts('dve', sm[:, 8 + NIT + 1:8 + 2 * NIT + 2], pow2, sm[:, 2:3], None, ALU.mult, None, [k_pow2, (k_sm, 'r')],
               [(k_sm, 'rk')])
            sc_state[i] = (score, k_sc, sm, k_sm, nk, allsc)
            yield

        def gen_bisect(i):
            score, k_sc, sm, k_sm, nk, allsc = sc_state[i]
            mneg, k_mn = mneg_r.next()
            for k in range(1, NIT + 1):
                midp = sm[:, 8 + k - 1:8 + k]
                if k in ACT_COUNT:
                    act(junka[:, 0:nk], score[:, 0:nk], AF.Sign, allsc + [(k_sm, 'mid', k - 1)], [k_junka, (k_sm, 'cnt')],
                        bias=midp, scale=-1.0, accum=sm[:, 3:4])
                    ts('dve', sm[:, 4:5], sm[:, 3:4], float(nk - 511), 0.5, ALU.is_le, ALU.subtract, [(k_sm, 'cnt')], [(k_sm, 'sg')])
                else:
                    ts('dve', junkb[:, 0:nk], score[:, 0:nk], midp, None, ALU.is_ge, ALU.add,
                       allsc + [(k_sm, 'mid', k - 1)], [k_junkb, (k_sm, 'cnt')], accum=sm[:, 3:4])
                    ts('dve', sm[:, 4:5], sm[:, 3:4], 255.5, 0.5, ALU.is_ge, ALU.subtract, [(k_sm, 'cnt')], [(k_sm, 'sg')])
                stt(sm[:, 8 + k:8 + k + 1], sm[:, 4:5], sm[:, 8 + NIT + 1 + k:8 + NIT + 2 + k], midp, ALU.mult, ALU.add,
                    [(k_sm, 'sg'), (k_sm, 'rk'), (k_sm, 'mid', k - 1)], [(k_sm, 'mid', k)])
                yield
            stt(sm[:, 5:6], sm[:, 8 + 2 * NIT + 1:8 + 2 * NIT + 2], -0.5, sm[:, 8 + NIT:8 + NIT + 1], ALU.mult, ALU.add,
                [(k_sm, 'rk'), (k_sm, 'mid', NIT)], [(k_sm, 'thr')])
            ts('dve', mneg[:, 0:nk], score[:, 0:nk], sm[:, 5:6], None, ALU.is_lt, None, allsc + [(k_sm, 'thr')], [k_mn])
            idx_state[i] = (mneg, k_mn)
            yield

        def attention(i):
            qt, k_qt = qt_r.next()
            dma('sp', qt, QTd[:, :, i * 128:(i + 1) * 128], [], [k_qt])
            first_in_bank = {4: True, 5: True, 6: True}
            steps = [(j, g) for j in range(i + 1) for g in range(2)]
            plof = {}

            def emit_qk(sidx):
                j, g = steps[sidx]
                pl = pl_r.next()
                plof[sidx] = pl
                need_mask = (i >= 2) or (j == i)
                mm(bank(pl), KT[:, g, j * 128:(j + 1) * 128], qt[:, 4 * g:4 * g + 4, :], True, not need_mask,
                   [k_KT, k_qt], [PB(pl)])
                if need_mask:
                    if i >= 2:
                        mneg, k_mn = idx_state[i]
                        mm(bank(pl), mneg[:, j * 128:(j + 1) * 128], negI4, False, True, [k_mn, k_negI4], [PB(pl)])
                    else:
                        mm(bank(pl), cnegb, negI4, False, True, [k_cnegb, k_negI4], [PB(pl)])

            emit_qk(0)
            for sidx, (j, g) in enumerate(steps):
                if sidx + 1 < len(steps):
                    emit_qk(sidx + 1)
                pl = plof[sidx]
                pt, k_pt = pt_r.next()
                act(pt, bank(pl), AF.Exp, [PB(pl)], [k_pt], scale=128.0 ** -0.5)
                for hh in range(4):
                    h = 4 * g + hh
                    b = ACCB[h // 3]
                    o0 = (h % 3) * 129
                    stf = (j == 0) and first_in_bank[b]
                    if j == 0:
                        first_in_bank[b] = False
                    mm(bank(b)[:, o0:o0 + 129], pt[:, hh * 128:(hh + 1) * 128], Vt[:, j, g * 130:g * 130 + 129],
                       stf, j == i, [k_pt, k_V], [PB(b)], sgc=True)
                yield
            rden, k_rden = rden_r.next()
            ob, k_ob = obuf_r.next()
            for h in range(8):
                b = ACCB[h // 3]
                o0 = (h % 3) * 129
                recip(rden[:, h:h + 1], bank(b)[:, o0 + 128:o0 + 129], [PB(b)], [(k_rden, h)])
                ts('dve', ob[:, h * 128:(h + 1) * 128], bank(b)[:, o0:o0 + 128], rden[:, h:h + 1], None, ALU.mult, None,
                   [PB(b), (k_rden, h)], [(k_ob, h)])
            pT = bankbf(TRB).rearrange("p (a b) -> p a b", a=8)
            for h in range(8):
                tr(pT[:, h, :], ob[:, h * 128:(h + 1) * 128], identb, [(k_ob, h), k_identb], [PB(TRB)])
            oTt, k_oT = oT_r.next()
            cp('act', oTt, pT, [PB(TRB)], [k_oT])
            dma('pool', OTd[:, :, i * 128:(i + 1) * 128], oTt, [k_oT], [('OTd', i)])
            yield

        def run_interleaved(gens):
            state = [[g, max(n, 1), 0, True] for g, n in gens if g is not None]
            while any(a[3] for a in state):
                best = None
                for a in state:
                    if a[3] and (best is None or a[2] / a[1] < best[2] / best[1]):
                        best = a
                try:
                    next(best[0])
                    best[2] += 1
                except StopIteration:
                    best[3] = False

        for T in range(NT):
            gens = [(attention(T), 2 * (T + 1) + 1)]
            if 2 <= T + 1 < NT:
                gens.append((gen_bisect(T + 1), NIT + 1))
            if 2 <= T + 2 < NT:
                gens.append((gen_scores(T + 2), 8 * ((128 * (T + 3) + 511) // 512) + 2))
            run_interleaved(gens)
        S.barrier()
        st['off'] = persist_mark

    if 'D' in phases:
        wpw, k_wpw = tile([8, D], BF16, 'wpw')
        wo, k_wo = tile([8, D], BF16, 'wo')
        wm, k_wm = tile([8, D], BF16, 'wm')
        mD = st['off']
        stage_r = ring(2, [8, 256], F32, 'stage')
        bi = 0
        for (wsrc2, wdst, kd) in ((w_pw, wpw, k_wpw), (w_o, wo, k_wo), (w_m, wm, k_wm)):
            for b4 in range(4):
                stg, k_stg = stage_r.next()
                load_w(wsrc(wsrc2, b4 * 256, 256), 256, stg, k_stg, wdst[:, :, b4 * 256:(b4 + 1) * 256], (kd, b4),
                       None, None, 'pool' if bi % 2 else 'dve')
                bi += 1
        S.barrier()
        st['off'] = mD
        kwpw, kwo, kwm = [], [], []
        cvl, k_cvl = tile([8, 512], BF16, 'cvl')
        ot_r = ring(2, [8, 512], BF16, 'otl')
        zt_r = ring(2, [8, 512], BF16, 'zt')
        mt_r = ring(2, [8, 512], BF16, 'mt')
        lnt, k_lnt = tile([4, 512], F32, 'lnt')
        rn_r = ring(2, [2, 512], F32, 'rn')
        tA_r = ring(2, [512], F32, 'tA')
        tB_r = ring(2, [512], F32, 'tB')
        tA2_r = ring(1, [512], F32, 'tA2')
        tB2_r = ring(1, [512], F32, 'tB2')
        sga_r = ring(2, [512], F32, 'sga')
        sgb_r = ring(2, [512], F32, 'sgb')
        xr_r = ring(2, [D], F32, 'xr')
        s1_r = ring(2, [D], F32, 's1')
        xn_r = ring(2, [D], BF16, 'xn2')
        h2_r = ring(2, [8, 128], BF16, 'h2t')
        junk, k_junk = tile([D], BF16, 'junkD')
        ssT, k_ss = tile([3 * NT], F32, 'ssD')
        py_r = Ring([2, 3, 4, 5])
        dstate = {}

        def d_ln(tg):
            t0, n = TGS[tg]
            zt, k_zt = zt_r.next()
            otl, k_otl = ot_r.next()
            rn, k_rn = rn_r.next()
            dma('sp', cvl[:, :, 0:n], CVd[:, :, t0:t0 + n], [], [k_cvl])
            dma('sp', otl[:, :, 0:n], OTd[:, :, t0:t0 + n], [], [k_otl])
            kzt = [(k_zt, cc) for cc in range(8)]
            act(zt[:, :, 0:n], cvl[:, :, 0:n], AF.Square, [k_cvl], kzt)
            for kc in range(8):
                mm(bank(0)[:, 0:n], onesb, cvl[:, kc, 0:n], kc == 0, kc == 7, [k_onesb, k_cvl], [PB(0)])
            for kc in range(8):
                mm(bank(1)[:, 0:n], onesb, zt[:, kc, 0:n], kc == 0, kc == 7, [k_onesb] + kzt, [PB(1)])
            dstate[tg] = dict(zt=zt, kzt=kzt, otl=otl, k_otl=k_otl)
            yield
            mu, musq, var, sdv = [lnt[:, q, 0:n] for q in range(4)]
            rstd, nmr = rn[:, 0, 0:n], rn[:, 1, 0:n]
            ts('dve', mu, bank(0)[:, 0:n], 1.0 / D, None, ALU.mult, None, [PB(0)], [(k_lnt, 0)])
            tt('dve', musq, mu, mu, ALU.mult, [(k_lnt, 0)], [(k_lnt, 1)])
            stt(var, bank(1)[:, 0:n], 1.0 / D, musq, ALU.mult, ALU.subtract, [PB(1), (k_lnt, 1)], [(k_lnt, 2)])
            ts('dve', var, var, 0.0, None, ALU.max, None, [(k_lnt, 2)], [(k_lnt, 2)])
            act(sdv, var, AF.Sqrt, [(k_lnt, 2), k_eps], [(k_lnt, 3)], bias=epsT[:, 0:1], scale=1.0)
            recip(rstd, sdv, [(k_lnt, 3)], [(k_rn, 0)])
            stt(nmr, mu, -1.0, rstd, ALU.mult, ALU.mult, [(k_lnt, 0), (k_rn, 0)], [(k_rn, 1)])
            yield
            for cc in range(8):
                tA, k_tA = tA2_r.next()
                tB, k_tB = tB2_r.next()
                tt('dve', tA[:, 0:n], cvl[:, cc, 0:n], rstd, ALU.mult, [k_cvl, (k_rn, 0)], [k_tA])
                tt('dve', tB[:, 0:n], tA[:, 0:n], nmr, ALU.add, [k_tA, (k_rn, 1)], [k_tB])
                act(zt[:, cc, 0:n], tB[:, 0:n], AF.Silu, [k_tB, k_cvec], [(k_zt, cc)],
                    bias=cvecT[:, 16 + cc:17 + cc], scale=cvecT[:, 8 + cc:9 + cc])
                yield

        def d_proj(tg):
            t0, n = TGS[tg]
            d = dstate[tg]
            zt, kzt, otl, k_otl = d['zt'], d['kzt'], d['otl'], d['k_otl']
            mt, k_mt = mt_r.next()
            for c in range(8):
                sga, k_sga = sga_r.next()
                sgb, k_sgb = sgb_r.next()
                dma('sp', sga[:, 0:n], SGd[:, c, t0:t0 + n], [], [k_sga])
                dma('sp', sgb[:, 0:n], SGd[:, 8 + c, t0:t0 + n], [], [k_sgb])
                pya = py_r.next()
                pyb = py_r.next()
                for kc in range(8):
                    mm(bank(pya)[:, 0:n], wpw[:, kc, c * 128:(c + 1) * 128], zt[:, kc, 0:n], kc == 0, kc == 7,
                       kwpw + kzt, [PB(pya)])
                for kc in range(8):
                    mm(bank(pyb)[:, 0:n], wo[:, kc, c * 128:(c + 1) * 128], otl[:, kc, 0:n], kc == 0, kc == 7,
                       kwo + [k_otl], [PB(pyb)])
                tA, k_tA = tA_r.next()
                tB, k_tB = tB_r.next()
                tt('dve', tA[:, 0:n], bank(pya)[:, 0:n], sga[:, 0:n], ALU.mult, [PB(pya), k_sga], [k_tA])
                tt('dve', tB[:, 0:n], bank(pyb)[:, 0:n], sgb[:, 0:n], ALU.mult, [PB(pyb), k_sgb], [k_tB])
                tt('pool', mt[:, c, 0:n], tA[:, 0:n], tB[:, 0:n], ALU.add, [k_tA, k_tB], [(k_mt, c)])
                d['mt'] = mt
                d['kmt'] = [(k_mt, cq) for cq in range(8)]
                yield

        po_r = Ring([6, 7])

        def d_merge(tg):
            t0, n = TGS[tg]
            d = dstate[tg]
            mt, kmt = d['mt'], d['kmt']
            ntl = n // 128
            m1 = {}

            def M1(tl):
                i = t0 // 128 + tl
                xr, k_xr = xr_r.next()
                s1, k_s1 = s1_r.next()
                dma('sp', xr, xs[i * 128:(i + 1) * 128, :], [], [k_xr])
                for half in range(2):
                    po = po_r.next()
                    for c in range(8):
                        mm(bank(po), mt[:, c, tl * 128:(tl + 1) * 128], wm[:, c, half * 512:(half + 1) * 512],
                           c == 0, c == 7, kmt + kwm, [PB(po)])
                    tt('dve', s1[:, half * 512:(half + 1) * 512], bank(po), xr[:, half * 512:(half + 1) * 512], ALU.add,
                       [PB(po), k_xr], [(k_s1, half)])
                dma('pool', S1d[i * 128:(i + 1) * 128, :], s1, [(k_s1, 0), (k_s1, 1)], [('S1d', i)])
                m1[tl] = (s1, k_s1)

            def M2(tl):
                i = t0 // 128 + tl
                s1, k_s1 = m1[tl]
                xn, k_xn = xn_r.next()
                h2t, k_h2 = h2_r.next()
                rms_to_T(s1, [(k_s1, 0), (k_s1, 1)], i, ssT, k_ss, junk, k_junk, xn, k_xn, 1, h2t, k_h2, 'act')
                dma('pool', H2Td[:, :, i * 128:(i + 1) * 128], h2t, [k_h2], [('H2Td', i)])

            M1(0)
            for tl in range(ntl):
                if tl + 1 < ntl:
                    M1(tl + 1)
                M2(tl)

        NG = len(TGS)

        def alternate(ga, gb):
            a_alive, b_alive = ga is not None, gb is not None
            while a_alive or b_alive:
                if b_alive:
                    try:
                        next(gb)
                    except StopIteration:
                        b_alive = False
                if a_alive:
                    try:
                        next(ga)
                    except StopIteration:
                        a_alive = False

        for _ in d_ln(0):
            pass
        for tg in range(NG + 1):
            ga = d_ln(tg + 1) if tg + 1 < NG else None
            gb = d_proj(tg) if tg < NG else None
            alternate(ga, gb)
            if tg >= 1:
                d_merge(tg - 1)
        S.barrier()
        st['off'] = persist_mark

    if 'E' in phases:
        h2T, k_h2T = tile([8, LP], BF16, 'h2T')
        for q4 in range(4):
            c0 = q4 * 1056
            dma('sp', h2T[:, :, c0:c0 + 1056], H2Td[:, :, c0:c0 + 1056], [], [(k_h2T, q4)])
        kh2 = [(k_h2T, q4) for q4 in range(4)]
        fdwT, k_fdw = tile([44 * 3], F32, 'fdw')
        fdbT, k_fdb = tile([44], F32, 'fdb')
        dma('sp', fdwT, fdw[:, :], [], [k_fdw])
        dma('sp', fdbT, fdb[:, :], [], [k_fdb])
        stage_r = ring(2, [8, 256], F32, 'stage')
        wu_r = ring(2, [8, 256], BF16, 'wu')
        ua_r = ring(4, [514], F32, 'ua')
        ub_r = ring(4, [514], F32, 'ub')
        ca_r = ring(4, [512], F32, 'ca')
        cb_r = ring(4, [512], F32, 'cb')
        sa_r = ring(3, [512], F32, 'sa')
        at_r = ring(3, [512], BF16, 'at')
        pa_r = Ring([0, 1, 2])
        pb_r = Ring([3, 4, 5])
        est = {}
        wts = {}
        iters = [(p, tg) for p in range(NPAIR) for tg in range(len(TGS))]

        def e_stage1(it):
            p, tg = iters[it]
            t0, n = TGS[tg]
            if tg == 0:
                stg, k_stg = stage_r.next()
                wu, k_wu = wu_r.next()
                src = [(wsrc(ffn_up, p * 128, 128), stg[:, :, 0:128]), (wsrc(ffn_up, 2816 + p * 128, 128), stg[:, :, 128:256])]
                load_w(src, 256, stg, k_stg, wu, k_wu, gffnT, k_gffn, 'pool')
                wts[p] = (wu, k_wu)
            wu, k_wu = wts[p]
            pa = pa_r.next()
            pb = pb_r.next()
            for kc in range(8):
                mm(bank(pa)[:, 0:n], wu[:, kc, 0:128], h2T[:, kc, t0:t0 + n], kc == 0, kc == 7, [k_wu] + kh2, [PB(pa)])
            for kc in range(8):
                mm(bank(pb)[:, 0:n], wu[:, kc, 128:256], h2T[:, kc, t0:t0 + n], kc == 0, kc == 7, [k_wu] + kh2, [PB(pb)])
            ua, k_ua = ua_r.next()
            ub, k_ub = ub_r.next()
            if tg == 0:
                memset('dve', ua[:, 0:2], 0.0, [(k_ua, 'h')])
                memset('dve', ub[:, 0:2], 0.0, [(k_ub, 'h')])
            else:
                pv_ = est[it - 1]
                pn = TGS[tg - 1][1]
                cp('act', ua[:, 0:2], pv_['ua'][:, pn:pn + 2], [(pv_['k_ua'], 'b')], [(k_ua, 'h')])
                cp('act', ub[:, 0:2], pv_['ub'][:, pn:pn + 2], [(pv_['k_ub'], 'b')], [(k_ub, 'h')])
            cp('act', ua[:, 2:2 + n], bank(pa)[:, 0:n], [PB(pa)], [(k_ua, 'b')])
            cp('act', ub[:, 2:2 + n], bank(pb)[:, 0:n], [PB(pb)], [(k_ub, 'b')])
            ca, k_ca = ca_r.next()
            cb, k_cb = cb_r.next()
            for (cx, k_cx, ci, pbk) in ((ca, k_ca, p, pa), (cb, k_cb, NPAIR + p, pb)):
                act(cx[:, 0:n], bank(pbk)[:, 0:n], AF.Identity, [PB(pbk), k_fdw, k_fdb], [k_cx],
                    bias=fdbT[:, ci:ci + 1], scale=fdwT[:, ci * 3 + 2:ci * 3 + 3])
            est[it] = dict(ua=ua, k_ua=k_ua, ub=ub, k_ub=k_ub, ca=ca, k_ca=k_ca, cb=cb, k_cb=k_cb)

        def e_stage2(it):
            p, tg = iters[it]
            t0, n = TGS[tg]
            d = est[it]
            for (u, k_u, cx, k_cx, ci) in ((d['ua'], d['k_ua'], d['ca'], d['k_ca'], p),
                                           (d['ub'], d['k_ub'], d['cb'], d['k_cb'], NPAIR + p)):
                rk = [(k_u, 'h'), (k_u, 'b'), k_fdw]
                stt(cx[:, 0:n], u[:, 1:1 + n], fdwT[:, ci * 3 + 1:ci * 3 + 2], cx[:, 0:n], ALU.mult, ALU.add,
                    rk + [k_cx], [k_cx])
                stt(cx[:, 0:n], u[:, 0:n], fdwT[:, ci * 3:ci * 3 + 1], cx[:, 0:n], ALU.mult, ALU.add,
                    rk + [k_cx], [k_cx])

        def e_stage3(it):
            p, tg = iters[it]
            t0, n = TGS[tg]
            d = est[it]
            sa, k_sa = sa_r.next()
            at, k_at = at_r.next()
            act(sa[:, 0:n], d['ca'][:, 0:n], AF.Silu, [d['k_ca']], [k_sa])
            tt('dve', at[:, 0:n], sa[:, 0:n], d['cb'][:, 0:n], ALU.mult, [k_sa, d['k_cb']], [k_at])
            dma('pool', ACTd[:, p, t0:t0 + n], at[:, 0:n], [k_at], [('ACTd', p, tg)])

        NI = len(iters)
        for step in range(NI + 2):
            if step < NI:
                e_stage1(step)
            if 0 <= step - 1 < NI:
                e_stage2(step - 1)
            if 0 <= step - 2 < NI:
                e_stage3(step - 2)
        S.barrier()
        st['off'] = persist_mark

    if 'F' in phases:
        wd, k_wd = tile([NPAIR, D], BF16, 'wd')
        fgT, k_fg = tile([D], F32, 'fg')
        dma('sp', fgT, fgb[:, :], [], [k_fg])
        stage_r = ring(2, [2, D], F32, 'stageF')
        for b in range(NPAIR // 2):
            stg, k_stg = stage_r.next()
            src = ffn_down[b * 256:(b + 1) * 256, :].rearrange("(kc p) c -> p kc c", p=128)
            dma('sp', stg, src, [], [k_stg])
            cp('pool' if b % 2 else 'dve', wd[:, 2 * b:2 * b + 2, :], stg, [k_stg], [(k_wd, b)])
        kwd = [(k_wd, b) for b in range(NPAIR // 2)]
        al_r = ring(2, [NPAIR, 128], BF16, 'al')
        s1_r = ring(2, [D], F32, 's1l')
        s2_r = ring(2, [D], F32, 's2')
        y_r = ring(2, [D], F32, 'y')
        junk, k_junk = tile([D], F32, 'junkF')
        ssT, k_ss = tile([3 * NT], F32, 'ssF')
        po_r = Ring([0, 1, 2, 3])
        for i in range(NT):
            al, k_al = al_r.next()
            s1l, k_s1l = s1_r.next()
            dma('sp', al, ACTd[:, :, i * 128:(i + 1) * 128], [], [k_al])
            dma('sp', s1l, S1d[i * 128:(i + 1) * 128, :], [], [k_s1l])
            s2, k_s2 = s2_r.next()
            for half in range(2):
                po = po_r.next()
                for kc in range(NPAIR):
                    mm(bank(po), al[:, kc, :], wd[:, kc, half * 512:(half + 1) * 512], kc == 0, kc == NPAIR - 1,
                       [k_al] + kwd, [PB(po)])
                tt('dve', s2[:, half * 512:(half + 1) * 512], bank(po), s1l[:, half * 512:(half + 1) * 512], ALU.add,
                   [PB(po), k_s1l], [(k_s2, half)])
            ks2 = [(k_s2, 0), (k_s2, 1)]
            act(junk, s2, AF.Square, ks2, [k_junk, (k_ss, i, 0)], accum=ssT[:, 3 * i:3 * i + 1])
            act(ssT[:, 3 * i + 1:3 * i + 2], ssT[:, 3 * i:3 * i + 1], AF.Sqrt, [(k_ss, i, 0), k_eps], [(k_ss, i, 1)],
                bias=epsT[:, 0:1], scale=1.0 / D)
            recip(ssT[:, 3 * i + 2:3 * i + 3], ssT[:, 3 * i + 1:3 * i + 2], [(k_ss, i, 1)], [(k_ss, i, 2)])
            y, k_y = y_r.next()
            stt(y, s2, ssT[:, 3 * i + 2:3 * i + 3], fgT, ALU.mult, ALU.mult, ks2 + [(k_ss, i, 2), k_fg], [k_y])
            if i == 0:
                dma('pool', out[0:112, :], y[16:128, :], [k_y], [('out', i)])
            elif i < NT - 1:
                dma('pool', out[i * 128 - 16:i * 128 + 112, :], y, [k_y], [('out', i)])
            else:
                dma('pool', out[4080:4096, :], y[0:16, :], [k_y], [('out', i)])

    S.emit(nc, es)
    es.close()
    build.stats = S.stats
    return nc


def _fm(v, nchunk):
    return np.ascontiguousarray(np.asarray(v, np.float32).reshape(nchunk, 128).T)


def _rope_tab(hd):
    half = hd // 2
    inv = (np.float32(10000.0) ** (-np.arange(half, dtype=np.float32) / np.float32(half))).astype(np.float32)
    pos = np.arange(LP, dtype=np.float32)
    ang = (pos[:, None] * inv[None, :]).astype(np.float32)
    cos = np.cos(ang).astype(np.float32).T
    sin = np.sin(ang).astype(np.float32).T
    reps = 128 // half
    return np.ascontiguousarray(np.stack([np.tile(cos, (reps, 1)), np.tile(sin, (reps, 1))], 0))


def _prot():
    out = np.zeros((128, 256), np.float32)
    for col0, hd in ((0, 128), (128, 64)):
        half = hd // 2
        for dp in range(128):
            b, j = dp // hd, dp % hd
            if j < half:
                out[b * hd + j + half, col0 + dp] = -1.0
            else:
                out[b * hd + j - half, col0 + dp] = 1.0
    return out


def make_in_maps(x, meta_tokens, mix_norm_g, w_in, conv_ln_g, conv_ln_b, conv_dw_w, conv_dw_b,
                 conv_pw_out, attn_w_o, w_merge_out, ffn_norm_g, ffn_up, ffn_dw_w, ffn_dw_b,
                 ffn_down, final_norm_g):
    f = lambda a: np.ascontiguousarray(np.asarray(a, np.float32))
    x = f(x)
    B = x.shape[0]
    common = {
        "w_in": f(w_in[0]),
        "gmix": _fm(mix_norm_g[0], 8),
        "dwT": np.ascontiguousarray(np.transpose(f(conv_dw_w[0]).reshape(31, 8, 128), (2, 1, 0)).reshape(128, 8 * 31)),
        "cvec": np.ascontiguousarray(np.concatenate([_fm(conv_dw_b[0], 8), _fm(conv_ln_g[0], 8), _fm(conv_ln_b[0], 8)], 1)),
        "w_pw": f(conv_pw_out[0]),
        "w_o": f(attn_w_o[0]),
        "w_m": f(w_merge_out[0]),
        "gffn": _fm(ffn_norm_g[0], 8),
        "ffn_up": f(ffn_up[0]),
        "fdw": np.ascontiguousarray(np.transpose(f(ffn_dw_w[0]).reshape(3, 44, 128), (2, 1, 0)).reshape(128, 44 * 3)),
        "fdb": _fm(ffn_dw_b[0], 44),
        "ffn_down": f(ffn_down[0]),
        "fgb": np.ascontiguousarray(np.broadcast_to(f(final_norm_g)[None, :], (128, D))),
        "rope128": _rope_tab(128),
        "rope64": _rope_tab(64),
        "identd": np.eye(128, dtype=np.float32),
        "protd": _prot(),
        "pow2d": np.ascontiguousarray(np.broadcast_to((2.0 ** -np.arange(NIT + 1, dtype=np.float32))[None, :], (128, NIT + 1))),
    }
    tq = np.arange(128)[:, None]
    sk = np.arange(128)[None, :]
    vis = sk <= tq
    cm = np.concatenate([np.where(vis, np.float32(3e38), np.float32(-1e30)),
                         np.where(vis, np.float32(0.0), np.float32(1.0))], 1).astype(np.float32)
    common["cmaskd"] = np.ascontiguousarray(cm)
    meta = f(meta_tokens)
    pad = np.zeros((LP - L, D), np.float32)
    maps = []
    for b in range(B):
        d = dict(common)
        d["xs"] = np.ascontiguousarray(np.concatenate([meta, x[b], pad], 0))
        maps.append(d)
    return maps


_NC_CACHE = {}


def kernel(**inputs):
    maps = make_in_maps(**inputs)
    if 'nc' not in _NC_CACHE:
        _NC_CACHE['nc'] = build()
    nc = _NC_CACHE['nc']
    res = run_bass_kernel_spmd(nc, maps, core_ids=list(range(len(maps))))
    outs = [np.asarray(r["out"], np.float32) for r in res.results]
    return np.stack(outs, 0)
```

```python
import numpy as np
from contextlib import ExitStack
import concourse.bass as bass
import concourse.mybir as mybir
from concourse.bass_utils import run_bass_kernel_spmd

F32 = mybir.dt.float32
BF16 = mybir.dt.bfloat16
AF = mybir.ActivationFunctionType
ALU = mybir.AluOpType
AX = mybir.AxisListType

L = 4112
LP = 4224
NT = 33
D = 1024
KC = 8
NPAIR = 22
EPS = 1e-6
NIT = 17
C_Q, C_K, C_V, C_QI, C_KI, C_WI, C_G = 2048, 3072, 3328, 3584, 4096, 4160, 4168
TGS = [(t0, min(512, LP - t0)) for t0 in range(0, LP, 512)]

ENGS = ['pe', 'act', 'dve', 'pool', 'sp']
NDMASEM = 24


class Op:
    __slots__ = ('eng', 'fn', 'reads', 'writes', 'deps', 'sig', 'tok', 'is_dma', 'idx', 'prevdma')

    def __init__(self, eng, fn, reads, writes, is_dma):
        self.eng = eng
        self.fn = fn
        self.reads = reads
        self.writes = writes
        self.is_dma = is_dma
        self.deps = []
        self.sig = False
        self.tok = None
        self.prevdma = None


class Sched:
    def __init__(self):
        self.ops = {e: [] for e in ENGS}
        self.lastw = {}
        self.readers = {}
        self.n = 0
        self.pending = {e: [] for e in ENGS}
        self.lastc = {e: None for e in ENGS}
        self.dmas_since = []

    def barrier(self):
        for f in ENGS:
            lst = [self.lastc[e] for e in ENGS if e != f and self.lastc[e] is not None]
            self.pending[f] = lst + list(self.dmas_since)
        self.dmas_since = []
        self.lastw = {}
        self.readers = {}

    def add(self, eng, fn, reads=(), writes=(), dma=False):
        if not getattr(self, 'enabled', True):
            return None
        op = Op(eng, fn, tuple(reads), tuple(writes), dma)
        op.idx = self.n
        self.n += 1
        deps = {}
        for r in op.reads:
            w = self.lastw.get(r)
            if w is not None:
                deps[w.idx] = (w, True)
        for wk in op.writes:
            lw = self.lastw.get(wk)
            if lw is not None and lw.idx not in deps:
                deps[lw.idx] = (lw, False)
            for rd in self.readers.get(wk, ()):
                if rd.idx not in deps:
                    deps[rd.idx] = (rd, False)
        for d, raw in deps.values():
            if d is op:
                continue
            if (not d.is_dma) and (not op.is_dma) and d.eng == op.eng:
                if not raw or op.eng == 'pe':
                    continue
            op.deps.append(d)
            d.sig = True
        if self.pending[eng]:
            for d in self.pending[eng]:
                if d is not op and d not in op.deps:
                    op.deps.append(d)
                    d.sig = True
            self.pending[eng] = []
        for r in op.reads:
            self.readers.setdefault(r, []).append(op)
        for wk in op.writes:
            self.lastw[wk] = op
            self.readers[wk] = []
        self.ops[eng].append(op)
        if dma:
            self.dmas_since.append(op)
        else:
            self.lastc[eng] = op
        return op

    def emit(self, nc, es):
        esem = {e: es.enter_context(nc.semaphore('s_' + e)) for e in ENGS}
        dsem = [es.enter_context(nc.semaphore('d_%d' % i)) for i in range(NDMASEM)]
        duse = [0] * NDMASEM
        dlast = [None] * NDMASEM
        allops = sorted([o for e in ENGS for o in self.ops[e]], key=lambda o: o.idx)
        qengs = sorted({o.eng for o in allops if o.is_dma})
        per = NDMASEM // max(1, len(qengs))
        qsems = {e: list(range(i * per, (i + 1) * per)) for i, e in enumerate(qengs)}
        qrr = {e: 0 for e in qengs}
        cnt = {e: 0 for e in ENGS}
        pos = {}
        for e in ENGS:
            for n_, o in enumerate(self.ops[e]):
                pos[id(o)] = n_ + 1
        needed = set()
        for e in ENGS:
            waited0 = {}
            for o in self.ops[e]:
                best0 = {}
                for d in o.deps:
                    if d.is_dma:
                        continue
                    if pos[id(d)] > best0.get(d.eng, (0, None))[0]:
                        best0[d.eng] = (pos[id(d)], d)
                for k_, (v_, d_) in best0.items():
                    if waited0.get(k_, 0) < v_:
                        needed.add(id(d_))
                        waited0[k_] = v_
        for e in ENGS:
            for o in self.ops[e]:
                if not o.is_dma:
                    o.sig = id(o) in needed
        for o in allops:
            if o.is_dma:
                s = qsems[o.eng][qrr[o.eng] % per]
                qrr[o.eng] += 1
                duse[s] += 1
                o.prevdma = dlast[s]
                o.tok = (('d', s), 16 * duse[s])
                dlast[s] = o
            elif o.sig:
                cnt[o.eng] += 1
                o.tok = (('e', o.eng), cnt[o.eng])
        self.stats = dict(cnt)

        def semof(key):
            return esem[key[1]] if key[0] == 'e' else dsem[key[1]]

        for e in ENGS:
            nxt = None
            for o in reversed(self.ops[e]):
                if o.is_dma:
                    continue
                if o.sig:
                    nxt = o.tok
                elif nxt is not None:
                    o.tok = nxt
                else:
                    o.tok = None

        def run(ename, eng):
            waited = {}
            for o in self.ops[ename]:
                deps = list(o.deps)
                if o.is_dma and o.prevdma is not None:
                    deps.append(o.prevdma)
                best = {}
                for d in deps:
                    assert d.tok is not None, "dependency on op with no later signal"
                    k, v = d.tok
                    if best.get(k, 0) < v:
                        best[k] = v
                for k, v in best.items():
                    if waited.get(k, 0) < v:
                        eng.wait_ge(semof(k), v)
                        waited[k] = v
                ins = o.fn(eng)
                if o.is_dma:
                    ins.then_inc(semof(o.tok[0]), 16)
                elif o.sig:
                    ins.then_inc(semof(o.tok[0]), 1)

        with nc.Block() as block:
            @block.tensor
            def _(eng):
                run('pe', eng)

            @block.scalar
            def _(eng):
                run('act', eng)

            @block.vector
            def _(eng):
                run('dve', eng)

            @block.gpsimd
            def _(eng):
                run('pool', eng)

            @block.sync
            def _(eng):
                run('sp', eng)
                for s in range(NDMASEM):
                    if duse[s] > 0:
                        eng.wait_ge(dsem[s], 16 * duse[s])


class Ring:
    def __init__(self, items):
        self.items = items
        self.i = 0

    def next(self):
        it = self.items[self.i % len(self.items)]
        self.i += 1
        return it


def build(debug=False, phases="ACDEF", asub="1234"):
    nc = bass.Bass("TRN2", target_bir_lowering=False)

    def din(name, shape, dt=F32):
        return nc.dram_tensor(name, list(shape), dt, kind="ExternalInput").ap()

    skind = "ExternalOutput" if debug else "Internal"

    def dscr(name, shape, dt):
        return nc.dram_tensor(name, list(shape), dt, kind=skind).ap()

    xs = din("xs", [LP, D])
    w_in = din("w_in", [D, 6216])
    gmix = din("gmix", [128, 8])
    dwT = din("dwT", [128, 8 * 31])
    cvec = din("cvec", [128, 24])
    w_pw = din("w_pw", [D, D])
    w_o = din("w_o", [D, D])
    w_m = din("w_m", [D, D])
    gffn = din("gffn", [128, 8])
    ffn_up = din("ffn_up", [D, 2 * 2816])
    fdw = din("fdw", [128, 44 * 3])
    fdb = din("fdb", [128, 44])
    ffn_down = din("ffn_down", [2816, D])
    fgb = din("fgb", [128, D])
    rope128 = din("rope128", [2, 128, LP])
    rope64 = din("rope64", [2, 128, LP])
    identd = din("identd", [128, 128])
    protd = din("protd", [128, 256])
    cmaskd = din("cmaskd", [128, 256])
    pow2d = din("pow2d", [128, NIT + 1])
    out = nc.dram_tensor("out", [4096, D], F32, kind="ExternalOutput").ap()

    CVd = dscr("CVd", [128, 8, LP], BF16)
    QTd = dscr("QTd", [128, 8, LP], BF16)
    KTd = dscr("KTd", [128, 2, LP], BF16)
    QITd = dscr("QITd", [128, 4, LP], BF16)
    KITd = dscr("KITd", [128, LP], BF16)
    Vd = dscr("Vd", [128, NT, 260], BF16)
    WId = dscr("WId", [128, NT, 16], F32)
    SGd = dscr("SGd", [128, 16, LP], F32)
    OTd = dscr("OTd", [128, 8, LP], BF16)
    S1d = dscr("S1d", [LP, D], F32)
    H2Td = dscr("H2Td", [128, 8, LP], BF16)
    ACTd = dscr("ACTd", [128, NPAIR, LP], BF16)

    S = Sched()
    es = ExitStack()
    import os as _os
    AW = int(_os.environ.get('AWK', '42')) * 1024
    arena = es.enter_context(nc.sbuf_tensor("arena", [128, AW], F32))
    psum = es.enter_context(nc.psum_tensor("psum", [128, 8, 512], F32))
    st = {'off': 0}

    def alloc(shape, dt):
        n = int(np.prod(shape))
        words = n if dt == F32 else (n + 1) // 2
        words = (words + 3) // 4 * 4
        off = st['off']
        if off + words > AW and not getattr(S, 'enabled', True):
            off = 0
        st['off'] = off + words
        assert st['off'] <= AW, ("arena overflow", st['off'])
        ap = arena[:, off:off + words]
        if dt != F32:
            ap = ap.bitcast(dt)
        ap = ap[:, 0:n]
        if len(shape) == 2:
            ap = ap.rearrange("p (a b) -> p a b", a=shape[0])
        elif len(shape) == 3:
            ap = ap.rearrange("p (a b c) -> p a b c", a=shape[0], b=shape[1])
        return ap

    uid = [0]

    def tile(shape, dt, name=None):
        uid[0] += 1
        return (alloc(shape, dt), (name or 't', uid[0]))

    def ring(n, shape, dt, name=None):
        return Ring([tile(shape, dt, name) for _ in range(n)])

    def bank(b):
        return psum[:, b, :]

    def bankbf(b):
        return psum[:, b, :].bitcast(BF16)

    def PB(b):
        return ('ps', b)

    def dma(q, out_, in_, reads, writes):
        S.add(q, lambda e: e.dma_start(out=out_, in_=in_), reads, writes, dma=True)

    def mm(out_, lhsT, rhs, start, stop, reads, writes, sgc=False):
        if sgc:
            S.add('pe', lambda e: e.matmul(out_, lhsT=lhsT, rhs=rhs, start=start, stop=stop, skip_group_check=True), reads, writes)
        else:
            S.add('pe', lambda e: e.matmul(out_, lhsT=lhsT, rhs=rhs, start=start, stop=stop), reads, writes)

    def tr(out_, in_, ident, reads, writes):
        S.add('pe', lambda e: e.transpose(out=out_, in_=in_, identity=ident), reads, writes)

    def act(out_, in_, func, reads, writes, bias=None, scale=None, accum=None):
        kw = {}
        if bias is not None:
            kw['bias'] = bias
        if scale is not None:
            kw['scale'] = scale
        if accum is not None:
            kw['accum_out'] = accum
        S.add('act', lambda e: e.activation(out=out_, in_=in_, func=func, **kw), reads, writes)

    def ts(eng, out_, in0, s1, s2, op0, op1, reads, writes, accum=None):
        if accum is not None:
            S.add(eng, lambda e: e.tensor_scalar(out=out_, in0=in0, scalar1=s1, scalar2=s2, op0=op0, op1=op1, accum_out=accum), reads, writes)
        elif op1 is None:
            S.add(eng, lambda e: e.tensor_scalar(out=out_, in0=in0, scalar1=s1, scalar2=None, op0=op0), reads, writes)
        else:
            S.add(eng, lambda e: e.tensor_scalar(out=out_, in0=in0, scalar1=s1, scalar2=s2, op0=op0, op1=op1), reads, writes)

    def tt(eng, out_, in0, in1, op, reads, writes):
        S.add(eng, lambda e: e.tensor_tensor(out=out_, in0=in0, in1=in1, op=op), reads, writes)

    def stt(out_, in0, scalar, in1, op0, op1, reads, writes):
        S.add('dve', lambda e: e.scalar_tensor_tensor(out=out_, in0=in0, scalar=scalar, in1=in1, op0=op0, op1=op1), reads, writes)

    def cp(eng, out_, in_, reads, writes):
        if eng == 'act':
            act(out_, in_, AF.Copy, reads, writes)
        else:
            S.add(eng, lambda e: e.tensor_copy(out=out_, in_=in_), reads, writes)

    def memset(eng, ap, val, writes):
        S.add(eng, lambda e: e.memset(ap, val), (), writes)

    def recip(out_, in_, reads, writes):
        S.add('dve', lambda e: e.reciprocal(out=out_, in_=in_), reads, writes)

    identf, k_identf = tile([128], F32, 'identf')
    identb, k_identb = tile([128], BF16, 'identb')
    onesb, k_onesb = tile([128], BF16, 'onesb')
    epsT, k_eps = tile([1], F32, 'eps')
    gmixT, k_gmix = tile([8], F32, 'gmix')
    ngmixT, k_ngmix = tile([8], F32, 'ngmix')
    gffnT, k_gffn = tile([8], F32, 'gffn')
    cvecT, k_cvec = tile([24], F32, 'cvec')
    dma('sp', identf, identd[:, :], [], [k_identf])
    dma('sp', gmixT, gmix[:, :], [], [k_gmix])
    dma('sp', gffnT, gffn[:, :], [], [k_gffn])
    dma('sp', cvecT, cvec[:, :], [], [k_cvec])
    cp('dve', identb, identf, [k_identf], [k_identb])
    memset('dve', onesb, 1.0, [k_onesb])
    memset('dve', epsT, EPS, [k_eps])
    ts('dve', ngmixT, gmixT, -1.0, None, ALU.mult, None, [k_gmix], [k_ngmix])
    persist_mark = st['off']

    def rms_to_T(xt, k_xt, i, ssT, k_ss, junk, k_junk, xn, k_xn, trb, dstT, k_dst, evac_eng):
        kx = list(k_xt) if isinstance(k_xt, list) else [k_xt]
        act(junk, xt, AF.Square, kx, [k_junk, (k_ss, i, 0)], accum=ssT[:, 3 * i:3 * i + 1])
        act(ssT[:, 3 * i + 1:3 * i + 2], ssT[:, 3 * i:3 * i + 1], AF.Sqrt, [(k_ss, i, 0), k_eps], [(k_ss, i, 1)],
            bias=epsT[:, 0:1], scale=1.0 / D)
        recip(ssT[:, 3 * i + 2:3 * i + 3], ssT[:, 3 * i + 1:3 * i + 2], [(k_ss, i, 1)], [(k_ss, i, 2)])
        ts('dve', xn, xt, ssT[:, 3 * i + 2:3 * i + 3], None, ALU.mult, None, kx + [(k_ss, i, 2)], [k_xn])
        pT = bankbf(trb).rearrange("p (a b) -> p a b", a=8)
        for kc in range(8):
            tr(pT[:, kc, :], xn[:, kc * 128:(kc + 1) * 128], identb, [k_xn, k_identb], [PB(trb)])
        cp(evac_eng, dstT, pT, [PB(trb)], [k_dst])

    def load_w(src3, ncols, stage, k_stage, dst, k_dst, gT, k_g, ceng, rot=None):
        if isinstance(src3, list):
            for (s_ap, st_ap) in src3:
                dma('sp', st_ap, s_ap, [], [k_stage])
        else:
            dma('sp', stage, src3, [], [k_stage])
        if gT is None:
            cp(ceng, dst, stage, [k_stage], [k_dst])
        else:
            for kc in range(dst.shape[1]):
                ts(ceng, dst[:, kc], stage[:, kc], gT[:, kc:kc + 1], 0.0, ALU.mult, ALU.add,
                   [k_stage, k_g], [k_dst])
        if rot is not None:
            dstr, k_dstr, half, ngT, k_ng = rot
            for kc in range(8):
                sv = stage[:, kc].rearrange("p (b two h) -> p b two h", two=2, h=half)
                dv = dstr[:, kc].rearrange("p (b two h) -> p b two h", two=2, h=half)
                ts(ceng, dv[:, :, 0, :], sv[:, :, 1, :], ngT[:, kc:kc + 1], 0.0, ALU.mult, ALU.add,
                   [k_stage, k_ng], [k_dstr])
                ts(ceng, dv[:, :, 1, :], sv[:, :, 0, :], gT[:, kc:kc + 1], 0.0, ALU.mult, ALU.add,
                   [k_stage, k_g], [k_dstr])

    def wsrc(w2d, col0, ncols):
        return w2d[:, col0:col0 + ncols].rearrange("(kc p) c -> p kc c", p=128)

    if 'A' in phases:
        hT, k_hT = tile([8, LP], BF16, 'hT')

        def hk(t0, n):
            return [(k_hT, i) for i in range(t0 // 128, (t0 + n) // 128)]

        m0 = st['off']
        xt_r = ring(3, [D], F32, 'xt')
        xn_r = ring(2, [D], BF16, 'xn')
        junk, k_junk = tile([D], F32, 'junk')
        ssT, k_ss = tile([3 * NT], F32, 'ss')
        for i in range(NT):
            xt, k_xt = xt_r.next()
            xn, k_xn = xn_r.next()
            dma('sp', xt, xs[i * 128:(i + 1) * 128, :], [], [k_xt])
            rms_to_T(xt, k_xt, i, ssT, k_ss, junk, k_junk, xn, k_xn, 6 + (i % 2),
                     hT[:, :, i * 128:(i + 1) * 128], (k_hT, i), 'act' if i % 2 else 'dve')

        mA = st['off']
        stage_r = ring(2, [8, 256], F32, 'stage')

        S.enabled = '1' in asub
        wA_r = ring(2, [8, 256], BF16, 'wA')
        dg_r = ring(2, [31, 128], BF16, 'dg')
        glu_r = ring(2, [30 + LP], BF16, 'glu')
        sg_r = ring(2, [512], F32, 'sg')
        cv_r = ring(3, [512], BF16, 'cv')
        dwTt, k_dwT = tile([8 * 31], F32, 'dwT')
        dma('sp', dwTt, dwT[:, :], [], [k_dwT])
        for g_, kg in glu_r.items:
            memset('pool', g_[:, 0:30], 0.0, [(kg, -1)])
        pa_r = Ring([0, 1])
        pg_r = Ring([2, 3])
        pc_r = Ring([4, 5])
        for cc in range(8):
            stg, k_stg = stage_r.next()
            wA, k_wA = wA_r.next()
            dg, k_dg = dg_r.next()
            glu, k_glu = glu_r.next()
            src = [(wsrc(w_in, cc * 128, 128), stg[:, :, 0:128]), (wsrc(w_in, 1024 + cc * 128, 128), stg[:, :, 128:256])]
            load_w(src, 256, stg, k_stg, wA, k_wA, gmixT, k_gmix, 'pool')
            for k in range(31):
                ts('pool', dg[:, k, :], identb, dwTt[:, cc * 31 + k:cc * 31 + k + 1], 0.0, ALU.mult, ALU.add,
                   [k_identb, k_dwT], [k_dg])

            def proj(tg):
                t0, n = TGS[tg]
                pa = pa_r.next()
                pg = pg_r.next()
                for kc in range(8):
                    mm(bank(pa)[:, 0:n], wA[:, kc, 0:128], hT[:, kc, t0:t0 + n], kc == 0, kc == 7,
                       [k_wA] + hk(t0, n), [PB(pa)])
                for kc in range(8):
                    mm(bank(pg)[:, 0:n], wA[:, kc, 128:256], hT[:, kc, t0:t0 + n], kc == 0, kc == 7,
                       [k_wA] + hk(t0, n), [PB(pg)])
                sg, k_sg = sg_r.next()
                act(sg[:, 0:n], bank(pg)[:, 0:n], AF.Sigmoid, [PB(pg)], [k_sg])
                tt('dve', glu[:, 30 + t0:30 + t0 + n], bank(pa)[:, 0:n], sg[:, 0:n], ALU.mult,
                   [PB(pa), k_sg], [(k_glu, tg)])

            def conv(tg):
                t0, n = TGS[tg]
                pc = pc_r.next()
                for k in range(31):
                    mm(bank(pc)[:, 0:n], dg[:, k, :], glu[:, t0 + k:t0 + k + n], k == 0, k == 30,
                       [k_dg, (k_glu, tg), (k_glu, tg - 1)], [PB(pc)])
                cv, k_cv = cv_r.next()
                act(cv[:, 0:n], bank(pc)[:, 0:n], AF.Identity, [PB(pc), k_cvec], [k_cv], bias=cvecT[:, cc:cc + 1])
                dma('pool', CVd[:, cc, t0:t0 + n], cv[:, 0:n], [k_cv], [('CVd', cc, tg)])

            for tg in range(len(TGS) + 1):
                if tg < len(TGS):
                    proj(tg)
                if tg >= 1:
                    conv(tg - 1)

        S.barrier()
        mA = m0
        st['off'] = mA
        stage_r = ring(1, [8, 256], F32, 'stage')
        S.enabled = '2' in asub
        WR, k_WR = tile([8, 15 * 128], BF16, 'WR')
        protf, k_protf = tile([256], F32, 'protf')
        protb, k_protb = tile([256], BF16, 'protb')
        dma('sp', protf, protd[:, :], [], [k_protf])
        cp('dve', protb, protf, [k_protf], [k_protb])
        blocks = [(C_Q, 0), (C_Q + 256, 2), (C_Q + 512, 4), (C_Q + 768, 6), (C_K, 8), (C_QI, 10), (C_QI + 256, 12)]
        stage2_r = ring(2, [8, 256], F32, 'stage2')
        for bi, (col0, ch0) in enumerate(blocks):
            stg, k_stg = stage2_r.next()
            load_w(wsrc(w_in, col0, 256), 256, stg, k_stg, WR[:, :, ch0 * 128:(ch0 + 2) * 128], (k_WR, bi),
                   gmixT, k_gmix, 'pool' if bi % 2 else 'dve')
        stg, k_stg = stage2_r.next()
        for hh in range(2):
            load_w(wsrc(w_in, C_KI, 64), 64, stg[:, :, hh * 64:(hh + 1) * 64], k_stg,
                   WR[:, :, 14 * 128 + hh * 64:14 * 128 + (hh + 1) * 64], (k_WR, 7 + hh), gmixT, k_gmix, 'dve')
        kWRall = [(k_WR, b) for b in range(9)]
        rp_r = ring(2, [4, 512], F32, 'rp')
        t1_r = ring(2, [512], F32, 't1')
        t2_r = ring(2, [512], F32, 't2')
        ob_r = ring(3, [512], BF16, 'ob')
        qb_r = ring(3, [512], BF16, 'qb')
        qf_r = ring(3, [512], F32, 'qf')
        pA_r = Ring([0, 1, 2])
        pB_r = Ring([3, 4, 5])
        work = [(tg, c) for tg in range(len(TGS)) for c in range(15)]
        rpof = {}
        pAof = {}

        def a2_mm(w):
            tg, c = work[w]
            t0, n = TGS[tg]
            if c == 0:
                rp, k_rp = rp_r.next()
                dma('sp', rp[:, 0:2, 0:n], rope128[:, :, t0:t0 + n].rearrange("a p t -> p a t"), [], [(k_rp, 0)])
                dma('sp', rp[:, 2:4, 0:n], rope64[:, :, t0:t0 + n].rearrange("a p t -> p a t"), [], [(k_rp, 1)])
                rpof[tg] = (rp, k_rp)
            pA = pA_r.next()
            pAof[w] = pA
            for kc in range(8):
                mm(bank(pA)[:, 0:n], WR[:, kc, c * 128:(c + 1) * 128], hT[:, kc, t0:t0 + n], kc == 0, kc == 7,
                   kWRall + hk(t0, n), [PB(pA)])

        a2_mm(0)
        for w, (tg, c) in enumerate(work):
            t0, n = TGS[tg]
            if w + 1 < len(work):
                a2_mm(w + 1)
            rp, k_rp = rpof[tg]
            pA = pAof[w]
            pB = pB_r.next()
            qb, k_qb = qb_r.next()
            cp('act', qb[:, 0:n], bank(pA)[:, 0:n], [PB(pA)], [k_qb])
            qf, k_qf = qf_r.next()
            cp('act', qf[:, 0:n], bank(pA)[:, 0:n], [PB(pA)], [k_qf])
            pcol = 0 if c < 10 else 128
            mm(bank(pB)[:, 0:n], protb[:, pcol:pcol + 128], qb[:, 0:n], True, True, [k_protb, k_qb], [PB(pB)])
            ti = 0 if c < 10 else 2
            t1, k_t1 = t1_r.next()
            t2, k_t2 = t2_r.next()
            ob, k_ob = ob_r.next()
            tt('dve', t1[:, 0:n], qf[:, 0:n], rp[:, ti, 0:n], ALU.mult, [k_qf, (k_rp, ti // 2)], [k_t1])
            tt('dve', t2[:, 0:n], bank(pB)[:, 0:n], rp[:, ti + 1, 0:n], ALU.mult, [PB(pB), (k_rp, ti // 2)], [k_t2])
            tt('pool' if w % 2 else 'dve', ob[:, 0:n], t1[:, 0:n], t2[:, 0:n], ALU.add, [k_t1, k_t2], [k_ob])
            if c < 8:
                dst = QTd[:, c, t0:t0 + n]
            elif c < 10:
                dst = KTd[:, c - 8, t0:t0 + n]
            elif c < 14:
                dst = QITd[:, c - 10, t0:t0 + n]
            else:
                dst = KITd[:, t0:t0 + n]
            dma('pool', dst, ob[:, 0:n], [k_ob], [('A2o', c, tg)])

        S.barrier()
        st['off'] = mA
        stage_r = ring(2, [8, 256], F32, 'stage')
        S.enabled = '3' in asub
        wv, k_wv = tile([8, 264], BF16, 'wv')
        stg, k_stg = stage_r.next()
        load_w(wsrc(w_in, C_V, 256), 256, stg, k_stg, wv[:, :, 0:256], (k_wv, 0), gmixT, k_gmix, 'dve')
        stg, k_stg = stage_r.next()
        dma('sp', stg[:, :, 0:64], wsrc(w_in, C_WI, 64), [], [k_stg])
        for kc in range(8):
            ts('dve', wv[:, kc, 256:264], stg[:, kc, 0:8], gmixT[:, kc:kc + 1], 0.0, ALU.mult, ALU.add,
               [k_stg, k_gmix], [(k_wv, 1)])
        vt_r = ring(3, [2, 130], BF16, 'vt')
        wi_r = ring(3, [16], F32, 'wi')
        wr_r = ring(3, [8], F32, 'wr')
        for v_, kv in vt_r.items:
            memset('dve', v_, 1.0, [(kv, 'one'), (kv, 'v')])
        pv_r = Ring([6, 7])
        CIDX = (8.0 ** -0.5) * (64.0 ** -0.5)
        for i in range(NT):
            pv = pv_r.next()
            for kc in range(8):
                mm(bank(pv)[:, 0:264], hT[:, kc, i * 128:(i + 1) * 128], wv[:, kc, :], kc == 0, kc == 7,
                   [(k_wv, 0), (k_wv, 1), (k_hT, i)], [PB(pv)])
            vt, k_vt = vt_r.next()
            wi, k_wi = wi_r.next()
            CUT = int(_os.environ.get('A3CUT', '9'))
            if CUT < 3:
                continue
            cp('act', vt[:, :, 0:128], bank(pv)[:, 0:256].rearrange("p (g d) -> p g d", g=2), [PB(pv)], [(k_vt, 'v')])
            if CUT < 4:
                continue
            wr, k_wr = wr_r.next()
            cp('act', wr, bank(pv)[:, 256:264], [PB(pv)], [k_wr])
            ts('dve', wi[:, 8:16], wr, 0.0, 0.5, ALU.is_ge, ALU.subtract, [k_wr], [(k_wi, 1)])
            stt(wi[:, 0:8], wr, 4.0 * CIDX, wi[:, 8:16], ALU.mult, ALU.mult, [k_wr, (k_wi, 1)], [(k_wi, 0)])
            if CUT < 5:
                continue
            dma('sp', Vd[:, i, :], vt.rearrange("p g d -> p (g d)"), [(k_vt, 'v'), (k_vt, 'one')], [('Vd', i)])
            dma('sp', WId[:, i, :], wi, [(k_wi, 0), (k_wi, 1)], [('WId', i)])

        S.enabled = '4' in asub
        wg_r = ring(2, [8, 256], BF16, 'wg')
        sgo_r = ring(3, [512], F32, 'sgo')
        pq_r = Ring([0, 1, 2, 3])
        for blk in range(8):
            stg, k_stg = stage_r.next()
            wg, k_wg = wg_r.next()
            load_w(wsrc(w_in, C_G + blk * 256, 256), 256, stg, k_stg, wg, k_wg, gmixT, k_gmix, 'pool' if blk % 2 else 'dve')
            for tg, (t0, n) in enumerate(TGS):
                for cl in range(2):
                    c = blk * 2 + cl
                    pq = pq_r.next()
                    for kc in range(8):
                        mm(bank(pq)[:, 0:n], wg[:, kc, cl * 128:(cl + 1) * 128], hT[:, kc, t0:t0 + n], kc == 0, kc == 7,
                           [k_wg] + hk(t0, n), [PB(pq)])
                    sgo, k_sgo = sgo_r.next()
                    act(sgo[:, 0:n], bank(pq)[:, 0:n], AF.Sigmoid, [PB(pq)], [k_sgo])
                    dma('pool', SGd[:, c, t0:t0 + n], sgo[:, 0:n], [k_sgo], [('SGd', c, tg)])
        S.enabled = True
        S.barrier()
        st['off'] = persist_mark

    if 'C' in phases:
        KT, k_KT = tile([2, LP], BF16, 'KT')
        KIT, k_KIT = tile([LP], BF16, 'KIT')
        Vt, k_V = tile([NT, 260], BF16, 'V')
        WIt, k_WI = tile([NT, 16], F32, 'WI')
        cmt, k_cm = tile([256], F32, 'cm')
        cnegb, k_cnegb = tile([128], BF16, 'cnegb')
        negI4, k_negI4 = tile([512], BF16, 'negI4')
        pow2, k_pow2 = tile([NIT + 1], F32, 'pow2')
        dma('sp', KT, KTd[:, :, :], [], [k_KT])
        dma('sp', KIT, KITd[:, :], [], [k_KIT])
        dma('sp', Vt, Vd[:, :, :], [], [k_V])
        dma('sp', WIt, WId[:, :, :], [], [k_WI])
        dma('sp', cmt, cmaskd[:, :], [], [k_cm])
        dma('sp', pow2, pow2d[:, :], [], [k_pow2])
        cp('dve', cnegb, cmt[:, 128:256], [k_cm], [k_cnegb])
        for r4 in range(4):
            ts('dve', negI4[:, r4 * 128:(r4 + 1) * 128], identf, -32768.0, None, ALU.mult, None, [k_identf], [k_negI4])
        score_r = ring(2, [LP], F32, 'score')
        mneg_r = ring(2, [LP], BF16, 'mneg')
        junkb, k_junkb = tile([LP], BF16, 'junkb')
        junka, k_junka = tile([LP], BF16, 'junka')
        R_r = ring(4, [512], BF16, 'R')
        dg_r = ring(2, [8, 128], BF16, 'dgs')
        SCB = 7
        qt_r = ring(2, [8, 128], BF16, 'qt')
        qp_r = ring(2, [4, 2, 128], BF16, 'qp')
        pt_r = ring(3, [512], BF16, 'pt')
        obuf_r = ring(2, [D], BF16, 'obuf')
        oT_r = ring(2, [8, 128], BF16, 'oTt')
        sm_r = ring(2, [8 + 2 * NIT + 16], F32, 'sm')
        rden_r = ring(2, [8], F32, 'rden')
        for qp_, kq in qp_r.items:
            memset('pool', qp_, 0.0, [(kq, 0), (kq, 1)])
        pi_r = Ring([0, 1])
        pl_r = Ring([2, 3])
        ACCB = [4, 5, 6]
        TRB = 2
        idx_state = {}
        ACT_COUNT = set()

        sc_state = {}
        CMX = 8 + 2 * NIT + 4

        def gen_scores(i):
            nk = 128 * (i + 1)
            qp, k_qp = qp_r.next()
            dma('sp', qp[0:64, :, 0, :], QITd[0:64, :, i * 128:(i + 1) * 128], [], [(k_qp, 0)])
            dma('sp', qp[64:128, :, 1, :], QITd[64:128, :, i * 128:(i + 1) * 128], [], [(k_qp, 1)])
            score, k_sc = score_r.next()
            sm, k_sm = sm_r.next()
            dg, k_dg = dg_r.next()
            for h in range(8):
                ts('dve', dg[:, h, :], identb, WIt[:, i, 8 + h:9 + h], None, ALU.mult, None, [k_identb, k_WI], [(k_dg, h)])
            rounds = [(k0, h) for k0 in range(0, nk, 512) for h in range(8)]
            piof = {}
            rof = {}

            def emit_mm(r):
                k0, h = rounds[r]
                n = min(512, nk - k0)
                c, par = h // 2, h % 2
                pi = pi_r.next()
                piof[r] = pi
                mm(bank(pi)[:, 0:n], qp[:, c, par, :], KIT[:, k0:k0 + n],
                   True, True, [(k_qp, 0), (k_qp, 1), k_KIT], [PB(pi)])

            def emit_sum(r):
                k0, h = rounds[r]
                n = min(512, nk - k0)
                R, k_R = rof[r]
                mm(bank(SCB)[:, 0:n], dg[:, h, :], R[:, 0:n], h == 0, h == 7, [k_R, (k_dg, h)], [PB(SCB)])
                if h == 7:
                    ci_ = k0 // 512
                    ts('dve', score[:, k0:k0 + n], bank(SCB)[:, 0:n], 1.0, None, ALU.mult, ALU.max,
                       [PB(SCB)], [(k_sc, k0), (k_sm, 'cm', ci_)], accum=sm[:, CMX + ci_:CMX + ci_ + 1])

            emit_mm(0)
            for r, (k0, h) in enumerate(rounds):
                n = min(512, nk - k0)
                if r + 1 < len(rounds):
                    emit_mm(r + 1)
                pi = piof[r]
                R, k_R = R_r.next()
                rof[r] = (R, k_R)
                act(R[:, 0:n], bank(pi)[:, 0:n], AF.Relu, [PB(pi), k_WI], [k_R], scale=WIt[:, i, h:h + 1])
                if r >= 1:
                    emit_sum(r - 1)
                yield
            emit_sum(len(rounds) - 1)
            allsc = [(k_sc, k0) for k0 in range(0, nk, 512)]
            memset('dve', score[:, 0:16], 1e30, [(k_sc, 0)])
            d0 = 128 * i
            tt('dve', score[:, d0:d0 + 128], score[:, d0:d0 + 128], cmt[:, 0:128], ALU.min,
               [k_cm, (k_sc, d0 // 512 * 512)], [(k_sc, d0 // 512 * 512)])
            nch_ = (nk + 511) // 512
            S.add('dve', lambda e: e.tensor_reduce(out=sm[:, 0:1], in_=sm[:, CMX:CMX + nch_], op=ALU.max, axis=AX.X),
                  [(k_sm, 'cm', c_) for c_ in range(nch_)], [(k_sm, 'hi')])
            yield
            dlo = min(d0, 512)
            S.add('dve', lambda e: e.tensor_reduce(out=sm[:, 1:2], in_=score[:, 0:dlo], op=ALU.min, axis=AX.X),
                  allsc, [(k_sm, 'lo')])
            tt('dve', sm[:, 2:3], sm[:, 0:1], sm[:, 1:2], ALU.subtract, [(k_sm, 'hi'), (k_sm, 'lo')], [(k_sm, 'r')])
            stt(sm[:, 8:9], sm[:, 2:3], 0.5, sm[:, 1:2], ALU.mult, ALU.add, [(k_sm, 'r'), (k_sm, 'lo')], [(k_sm, 'mid', 0)])
            ts('dve', sm[:, 8 + NIT + 1:8 + 2 * NIT + 2], pow2, sm[:, 2:3], None, ALU.mult, None, [k_pow2, (k_sm, 'r')],
               [(k_sm, 'rk')])
            sc_state[i] = (score, k_sc, sm, k_sm, nk, allsc)
            yield

        def gen_bisect(i):
            score, k_sc, sm, k_sm, nk, allsc = sc_state[i]
            mneg, k_mn = mneg_r.next()
            for k in range(1, NIT + 1):
                midp = sm[:, 8 + k - 1:8 + k]
                if k in ACT_COUNT:
                    act(junka[:, 0:nk], score[:, 0:nk], AF.Sign, allsc + [(k_sm, 'mid', k - 1)], [k_junka, (k_sm, 'cnt')],
                        bias=midp, scale=-1.0, accum=sm[:, 3:4])
                    ts('dve', sm[:, 4:5], sm[:, 3:4], float(nk - 511), 0.5, ALU.is_le, ALU.subtract, [(k_sm, 'cnt')], [(k_sm, 'sg')])
                else:
                    ts('dve', junkb[:, 0:nk], score[:, 0:nk], midp, None, ALU.is_ge, ALU.add,
                       allsc + [(k_sm, 'mid', k - 1)], [k_junkb, (k_sm, 'cnt')], accum=sm[:, 3:4])
                    ts('dve', sm[:, 4:5], sm[:, 3:4], 255.5, 0.5, ALU.is_ge, ALU.subtract, [(k_sm, 'cnt')], [(k_sm, 'sg')])
                stt(sm[:, 8 + k:8 + k + 1], sm[:, 4:5], sm[:, 8 + NIT + 1 + k:8 + NIT + 2 + k], midp, ALU.mult, ALU.add,
                    [(k_sm, 'sg'), (k_sm, 'rk'), (k_sm, 'mid', k - 1)], [(k_sm, 'mid', k)])
                yield
            stt(sm[:, 5:6], sm[:, 8 + 2 * NIT + 1:8 + 2 * NIT + 2], -0.5, sm[:, 8 + NIT:8 + NIT + 1], ALU.mult, ALU.add,
                [(k_sm, 'rk'), (k_sm, 'mid', NIT)], [(k_sm, 'thr')])
            ts('dve', mneg[:, 0:nk], score[:, 0:nk], sm[:, 5:6], None, ALU.is_lt, None, allsc + [(k_sm, 'thr')], [k_mn])
            idx_state[i] = (mneg, k_mn)
            yield

        def attention(i):
            qt, k_qt = qt_r.next()
            dma('sp', qt, QTd[:, :, i * 128:(i + 1) * 128], [], [k_qt])
            first_in_bank = {4: True, 5: True, 6: True}
            steps = [(j, g) for j in range(i + 1) for g in range(2)]
            plof = {}

            def emit_qk(sidx):
                j, g = steps[sidx]
                pl = pl_r.next()
                plof[sidx] = pl
                need_mask = (i >= 2) or (j == i)
                mm(bank(pl), KT[:, g, j * 128:(j + 1) * 128], qt[:, 4 * g:4 * g + 4, :], True, not need_mask,
                   [k_KT, k_qt], [PB(pl)])
                if need_mask:
                    if i >= 2:
                        mneg, k_mn = idx_state[i]
                        mm(bank(pl), mneg[:, j * 128:(j + 1) * 128], negI4, False, True, [k_mn, k_negI4], [PB(pl)])
                    else:
                        mm(bank(pl), cnegb, negI4, False, True, [k_cnegb, k_negI4], [PB(pl)])

            emit_qk(0)
            for sidx, (j, g) in enumerate(steps):
                if sidx + 1 < len(steps):
                    emit_qk(sidx + 1)
                pl = plof[sidx]
                pt, k_pt = pt_r.next()
                act(pt, bank(pl), AF.Exp, [PB(pl)], [k_pt], scale=128.0 ** -0.5)
                for hh in range(4):
                    h = 4 * g + hh
                    b = ACCB[h // 3]
                    o0 = (h % 3) * 129
                    stf = (j == 0) and first_in_bank[b]
                    if j == 0:
                        first_in_bank[b] = False
                    mm(bank(b)[:, o0:o0 + 129], pt[:, hh * 128:(hh + 1) * 128], Vt[:, j, g * 130:g * 130 + 129],
                       stf, j == i, [k_pt, k_V], [PB(b)], sgc=True)
                yield
            rden, k_rden = rden_r.next()
            ob, k_ob = obuf_r.next()
            for h in range(8):
                b = ACCB[h // 3]
                o0 = (h % 3) * 129
                recip(rden[:, h:h + 1], bank(b)[:, o0 + 128:o0 + 129], [PB(b)], [(k_rden, h)])
                ts('dve', ob[:, h * 128:(h + 1) * 128], bank(b)[:, o0:o0 + 128], rden[:, h:h + 1], None, ALU.mult, None,
                   [PB(b), (k_rden, h)], [(k_ob, h)])
            pT = bankbf(TRB).rearrange("p (a b) -> p a b", a=8)
            for h in range(8):
                tr(pT[:, h, :], ob[:, h * 128:(h + 1) * 128], identb, [(k_ob, h), k_identb], [PB(TRB)])
            oTt, k_oT = oT_r.next()
            cp('act', oTt, pT, [PB(TRB)], [k_oT])
            dma('pool', OTd[:, :, i * 128:(i + 1) * 128], oTt, [k_oT], [('OTd', i)])
            yield

        def run_interleaved(gens):
            state = [[g, max(n, 1), 0, True] for g, n in gens if g is not None]
            while any(a[3] for a in state):
                best = None
                for a in state:
                    if a[3] and (best is None or a[2] / a[1] < best[2] / best[1]):
                        best = a
                try:
                    next(best[0])
                    best[2] += 1
                except StopIteration:
                    best[3] = False

        for T in range(NT):
            gens = [(attention(T), 2 * (T + 1) + 1)]
            if 2 <= T + 1 < NT:
                gens.append((gen_bisect(T + 1), NIT + 1))
            if 2 <= T + 2 < NT:
                gens.append((gen_scores(T + 2), 8 * ((128 * (T + 3) + 511) // 512) + 2))
            run_interleaved(gens)
        S.barrier()
        st['off'] = persist_mark

    if 'D' in phases:
        wpw, k_wpw = tile([8, D], BF16, 'wpw')
        wo, k_wo = tile([8, D], BF16, 'wo')
        wm, k_wm = tile([8, D], BF16, 'wm')
        mD = st['off']
        stage_r = ring(2, [8, 256], F32, 'stage')
        bi = 0
        for (wsrc2, wdst, kd) in ((w_pw, wpw, k_wpw), (w_o, wo, k_wo), (w_m, wm, k_wm)):
            for b4 in range(4):
                stg, k_stg = stage_r.next()
                load_w(wsrc(wsrc2, b4 * 256, 256), 256, stg, k_stg, wdst[:, :, b4 * 256:(b4 + 1) * 256], (kd, b4),
                       None, None, 'pool' if bi % 2 else 'dve')
                bi += 1
        S.barrier()
        st['off'] = mD
        kwpw, kwo, kwm = [], [], []
        cvl, k_cvl = tile([8, 512], BF16, 'cvl')
        ot_r = ring(2, [8, 512], BF16, 'otl')
        zt_r = ring(2, [8, 512], BF16, 'zt')
        mt_r = ring(2, [8, 512], BF16, 'mt')
        lnt, k_lnt = tile([4, 512], F32, 'lnt')
        rn_r = ring(2, [2, 512], F32, 'rn')
        tA_r = ring(2, [512], F32, 'tA')
        tB_r = ring(2, [512], F32, 'tB')
        tA2_r = ring(1, [512], F32, 'tA2')
        tB2_r = ring(1, [512], F32, 'tB2')
        sga_r = ring(2, [512], F32, 'sga')
        sgb_r = ring(2, [512], F32, 'sgb')
        xr_r = ring(2, [D], F32, 'xr')
        s1_r = ring(2, [D], F32, 's1')
        xn_r = ring(2, [D], BF16, 'xn2')
        h2_r = ring(2, [8, 128], BF16, 'h2t')
        junk, k_junk = tile([D], BF16, 'junkD')
        ssT, k_ss = tile([3 * NT], F32, 'ssD')
        py_r = Ring([2, 3, 4, 5])
        dstate = {}

        def d_ln(tg):
            t0, n = TGS[tg]
            zt, k_zt = zt_r.next()
            otl, k_otl = ot_r.next()
            rn, k_rn = rn_r.next()
            dma('sp', cvl[:, :, 0:n], CVd[:, :, t0:t0 + n], [], [k_cvl])
            dma('sp', otl[:, :, 0:n], OTd[:, :, t0:t0 + n], [], [k_otl])
            kzt = [(k_zt, cc) for cc in range(8)]
            act(zt[:, :, 0:n], cvl[:, :, 0:n], AF.Square, [k_cvl], kzt)
            for kc in range(8):
                mm(bank(0)[:, 0:n], onesb, cvl[:, kc, 0:n], kc == 0, kc == 7, [k_onesb, k_cvl], [PB(0)])
            for kc in range(8):
                mm(bank(1)[:, 0:n], onesb, zt[:, kc, 0:n], kc == 0, kc == 7, [k_onesb] + kzt, [PB(1)])
            dstate[tg] = dict(zt=zt, kzt=kzt, otl=otl, k_otl=k_otl)
            yield
            mu, musq, var, sdv = [lnt[:, q, 0:n] for q in range(4)]
            rstd, nmr = rn[:, 0, 0:n], rn[:, 1, 0:n]
            ts('dve', mu, bank(0)[:, 0:n], 1.0 / D, None, ALU.mult, None, [PB(0)], [(k_lnt, 0)])
            tt('dve', musq, mu, mu, ALU.mult, [(k_lnt, 0)], [(k_lnt, 1)])
            stt(var, bank(1)[:, 0:n], 1.0 / D, musq, ALU.mult, ALU.subtract, [PB(1), (k_lnt, 1)], [(k_lnt, 2)])
            ts('dve', var, var, 0.0, None, ALU.max, None, [(k_lnt, 2)], [(k_lnt, 2)])
            act(sdv, var, AF.Sqrt, [(k_lnt, 2), k_eps], [(k_lnt, 3)], bias=epsT[:, 0:1], scale=1.0)
            recip(rstd, sdv, [(k_lnt, 3)], [(k_rn, 0)])
            stt(nmr, mu, -1.0, rstd, ALU.mult, ALU.mult, [(k_lnt, 0), (k_rn, 0)], [(k_rn, 1)])
            yield
            for cc in range(8):
                tA, k_tA = tA2_r.next()
                tB, k_tB = tB2_r.next()
                tt('dve', tA[:, 0:n], cvl[:, cc, 0:n], rstd, ALU.mult, [k_cvl, (k_rn, 0)], [k_tA])
                tt('dve', tB[:, 0:n], tA[:, 0:n], nmr, ALU.add, [k_tA, (k_rn, 1)], [k_tB])
                act(zt[:, cc, 0:n], tB[:, 0:n], AF.Silu, [k_tB, k_cvec], [(k_zt, cc)],
                    bias=cvecT[:, 16 + cc:17 + cc], scale=cvecT[:, 8 + cc:9 + cc])
                yield

        def d_proj(tg):
            t0, n = TGS[tg]
            d = dstate[tg]
            zt, kzt, otl, k_otl = d['zt'], d['kzt'], d['otl'], d['k_otl']
            mt, k_mt = mt_r.next()
            for c in range(8):
                sga, k_sga = sga_r.next()
                sgb, k_sgb = sgb_r.next()
                dma('sp', sga[:, 0:n], SGd[:, c, t0:t0 + n], [], [k_sga])
                dma('sp', sgb[:, 0:n], SGd[:, 8 + c, t0:t0 + n], [], [k_sgb])
                pya = py_r.next()
                pyb = py_r.next()
                for kc in range(8):
                    mm(bank(pya)[:, 0:n], wpw[:, kc, c * 128:(c + 1) * 128], zt[:, kc, 0:n], kc == 0, kc == 7,
                       kwpw + kzt, [PB(pya)])
                for kc in range(8):
                    mm(bank(pyb)[:, 0:n], wo[:, kc, c * 128:(c + 1) * 128], otl[:, kc, 0:n], kc == 0, kc == 7,
                       kwo + [k_otl], [PB(pyb)])
                tA, k_tA = tA_r.next()
                tB, k_tB = tB_r.next()
                tt('dve', tA[:, 0:n], bank(pya)[:, 0:n], sga[:, 0:n], ALU.mult, [PB(pya), k_sga], [k_tA])
                tt('dve', tB[:, 0:n], bank(pyb)[:, 0:n], sgb[:, 0:n], ALU.mult, [PB(pyb), k_sgb], [k_tB])
                tt('pool', mt[:, c, 0:n], tA[:, 0:n], tB[:, 0:n], ALU.add, [k_tA, k_tB], [(k_mt, c)])
                d['mt'] = mt
                d['kmt'] = [(k_mt, cq) for cq in range(8)]
                yield

        po_r = Ring([6, 7])

        def d_merge(tg):
            t0, n = TGS[tg]
            d = dstate[tg]
            mt, kmt = d['mt'], d['kmt']
            ntl = n // 128
            m1 = {}

            def M1(tl):
                i = t0 // 128 + tl
                xr, k_xr = xr_r.next()
                s1, k_s1 = s1_r.next()
                dma('sp', xr, xs[i * 128:(i + 1) * 128, :], [], [k_xr])
                for half in range(2):
                    po = po_r.next()
                    for c in range(8):
                        mm(bank(po), mt[:, c, tl * 128:(tl + 1) * 128], wm[:, c, half * 512:(half + 1) * 512],
                           c == 0, c == 7, kmt + kwm, [PB(po)])
                    tt('dve', s1[:, half * 512:(half + 1) * 512], bank(po), xr[:, half * 512:(half + 1) * 512], ALU.add,
                       [PB(po), k_xr], [(k_s1, half)])
                dma('pool', S1d[i * 128:(i + 1) * 128, :], s1, [(k_s1, 0), (k_s1, 1)], [('S1d', i)])
                m1[tl] = (s1, k_s1)

            def M2(tl):
                i = t0 // 128 + tl
                s1, k_s1 = m1[tl]
                xn, k_xn = xn_r.next()
                h2t, k_h2 = h2_r.next()
                rms_to_T(s1, [(k_s1, 0), (k_s1, 1)], i, ssT, k_ss, junk, k_junk, xn, k_xn, 1, h2t, k_h2, 'act')
                dma('pool', H2Td[:, :, i * 128:(i + 1) * 128], h2t, [k_h2], [('H2Td', i)])

            M1(0)
            for tl in range(ntl):
                if tl + 1 < ntl:
                    M1(tl + 1)
                M2(tl)

        NG = len(TGS)

        def alternate(ga, gb):
            a_alive, b_alive = ga is not None, gb is not None
            while a_alive or b_alive:
                if b_alive:
                    try:
                        next(gb)
                    except StopIteration:
                        b_alive = False
                if a_alive:
                    try:
                        next(ga)
                    except StopIteration:
                        a_alive = False

        for _ in d_ln(0):
            pass
        for tg in range(NG + 1):
            ga = d_ln(tg + 1) if tg + 1 < NG else None
            gb = d_proj(tg) if tg < NG else None
            alternate(ga, gb)
            if tg >= 1:
                d_merge(tg - 1)
        S.barrier()
        st['off'] = persist_mark

    if 'E' in phases:
        h2T, k_h2T = tile([8, LP], BF16, 'h2T')
        for q4 in range(4):
            c0 = q4 * 1056
            dma('sp', h2T[:, :, c0:c0 + 1056], H2Td[:, :, c0:c0 + 1056], [], [(k_h2T, q4)])
        kh2 = [(k_h2T, q4) for q4 in range(4)]
        fdwT, k_fdw = tile([44 * 3], F32, 'fdw')
        fdbT, k_fdb = tile([44], F32, 'fdb')
        dma('sp', fdwT, fdw[:, :], [], [k_fdw])
        dma('sp', fdbT, fdb[:, :], [], [k_fdb])
        stage_r = ring(2, [8, 256], F32, 'stage')
        wu_r = ring(2, [8, 256], BF16, 'wu')
        ua_r = ring(4, [514], F32, 'ua')
        ub_r = ring(4, [514], F32, 'ub')
        ca_r = ring(4, [512], F32, 'ca')
        cb_r = ring(4, [512], F32, 'cb')
        sa_r = ring(3, [512], F32, 'sa')
        at_r = ring(3, [512], BF16, 'at')
        pa_r = Ring([0, 1, 2])
        pb_r = Ring([3, 4, 5])
        est = {}
        wts = {}
        iters = [(p, tg) for p in range(NPAIR) for tg in range(len(TGS))]

        def e_stage1(it):
            p, tg = iters[it]
            t0, n = TGS[tg]
            if tg == 0:
                stg, k_stg = stage_r.next()
                wu, k_wu = wu_r.next()
                src = [(wsrc(ffn_up, p * 128, 128), stg[:, :, 0:128]), (wsrc(ffn_up, 2816 + p * 128, 128), stg[:, :, 128:256])]
                load_w(src, 256, stg, k_stg, wu, k_wu, gffnT, k_gffn, 'pool')
                wts[p] = (wu, k_wu)
            wu, k_wu = wts[p]
            pa = pa_r.next()
            pb = pb_r.next()
            for kc in range(8):
                mm(bank(pa)[:, 0:n], wu[:, kc, 0:128], h2T[:, kc, t0:t0 + n], kc == 0, kc == 7, [k_wu] + kh2, [PB(pa)])
            for kc in range(8):
                mm(bank(pb)[:, 0:n], wu[:, kc, 128:256], h2T[:, kc, t0:t0 + n], kc == 0, kc == 7, [k_wu] + kh2, [PB(pb)])
            ua, k_ua = ua_r.next()
            ub, k_ub = ub_r.next()
            if tg == 0:
                memset('dve', ua[:, 0:2], 0.0, [(k_ua, 'h')])
                memset('dve', ub[:, 0:2], 0.0, [(k_ub, 'h')])
            else:
                pv_ = est[it - 1]
                pn = TGS[tg - 1][1]
                cp('act', ua[:, 0:2], pv_['ua'][:, pn:pn + 2], [(pv_['k_ua'], 'b')], [(k_ua, 'h')])
                cp('act', ub[:, 0:2], pv_['ub'][:, pn:pn + 2], [(pv_['k_ub'], 'b')], [(k_ub, 'h')])
            cp('act', ua[:, 2:2 + n], bank(pa)[:, 0:n], [PB(pa)], [(k_ua, 'b')])
            cp('act', ub[:, 2:2 + n], bank(pb)[:, 0:n], [PB(pb)], [(k_ub, 'b')])
            ca, k_ca = ca_r.next()
            cb, k_cb = cb_r.next()
            for (cx, k_cx, ci, pbk) in ((ca, k_ca, p, pa), (cb, k_cb, NPAIR + p, pb)):
                act(cx[:, 0:n], bank(pbk)[:, 0:n], AF.Identity, [PB(pbk), k_fdw, k_fdb], [k_cx],
                    bias=fdbT[:, ci:ci + 1], scale=fdwT[:, ci * 3 + 2:ci * 3 + 3])
            est[it] = dict(ua=ua, k_ua=k_ua, ub=ub, k_ub=k_ub, ca=ca, k_ca=k_ca, cb=cb, k_cb=k_cb)

        def e_stage2(it):
            p, tg = iters[it]
            t0, n = TGS[tg]
            d = est[it]
            for (u, k_u, cx, k_cx, ci) in ((d['ua'], d['k_ua'], d['ca'], d['k_ca'], p),
                                           (d['ub'], d['k_ub'], d['cb'], d['k_cb'], NPAIR + p)):
                rk = [(k_u, 'h'), (k_u, 'b'), k_fdw]
                stt(cx[:, 0:n], u[:, 1:1 + n], fdwT[:, ci * 3 + 1:ci * 3 + 2], cx[:, 0:n], ALU.mult, ALU.add,
                    rk + [k_cx], [k_cx])
                stt(cx[:, 0:n], u[:, 0:n], fdwT[:, ci * 3:ci * 3 + 1], cx[:, 0:n], ALU.mult, ALU.add,
                    rk + [k_cx], [k_cx])

        def e_stage3(it):
            p, tg = iters[it]
            t0, n = TGS[tg]
            d = est[it]
            sa, k_sa = sa_r.next()
            at, k_at = at_r.next()
            act(sa[:, 0:n], d['ca'][:, 0:n], AF.Silu, [d['k_ca']], [k_sa])
            tt('dve', at[:, 0:n], sa[:, 0:n], d['cb'][:, 0:n], ALU.mult, [k_sa, d['k_cb']], [k_at])
            dma('pool', ACTd[:, p, t0:t0 + n], at[:, 0:n], [k_at], [('ACTd', p, tg)])

        NI = len(iters)
        for step in range(NI + 2):
            if step < NI:
                e_stage1(step)
            if 0 <= step - 1 < NI:
                e_stage2(step - 1)
            if 0 <= step - 2 < NI:
                e_stage3(step - 2)
        S.barrier()
        st['off'] = persist_mark

    if 'F' in phases:
        wd, k_wd = tile([NPAIR, D], BF16, 'wd')
        fgT, k_fg = tile([D], F32, 'fg')
        dma('sp', fgT, fgb[:, :], [], [k_fg])
        stage_r = ring(2, [2, D], F32, 'stageF')
        for b in range(NPAIR // 2):
            stg, k_stg = stage_r.next()
            src = ffn_down[b * 256:(b + 1) * 256, :].rearrange("(kc p) c -> p kc c", p=128)
            dma('sp', stg, src, [], [k_stg])
            cp('pool' if b % 2 else 'dve', wd[:, 2 * b:2 * b + 2, :], stg, [k_stg], [(k_wd, b)])
        kwd = [(k_wd, b) for b in range(NPAIR // 2)]
        al_r = ring(2, [NPAIR, 128], BF16, 'al')
        s1_r = ring(2, [D], F32, 's1l')
        s2_r = ring(2, [D], F32, 's2')
        y_r = ring(2, [D], F32, 'y')
        junk, k_junk = tile([D], F32, 'junkF')
        ssT, k_ss = tile([3 * NT], F32, 'ssF')
        po_r = Ring([0, 1, 2, 3])
        for i in range(NT):
            al, k_al = al_r.next()
            s1l, k_s1l = s1_r.next()
            dma('sp', al, ACTd[:, :, i * 128:(i + 1) * 128], [], [k_al])
            dma('sp', s1l, S1d[i * 128:(i + 1) * 128, :], [], [k_s1l])
            s2, k_s2 = s2_r.next()
            for half in range(2):
                po = po_r.next()
                for kc in range(NPAIR):
                    mm(bank(po), al[:, kc, :], wd[:, kc, half * 512:(half + 1) * 512], kc == 0, kc == NPAIR - 1,
                       [k_al] + kwd, [PB(po)])
                tt('dve', s2[:, half * 512:(half + 1) * 512], bank(po), s1l[:, half * 512:(half + 1) * 512], ALU.add,
                   [PB(po), k_s1l], [(k_s2, half)])
            ks2 = [(k_s2, 0), (k_s2, 1)]
            act(junk, s2, AF.Square, ks2, [k_junk, (k_ss, i, 0)], accum=ssT[:, 3 * i:3 * i + 1])
            act(ssT[:, 3 * i + 1:3 * i + 2], ssT[:, 3 * i:3 * i + 1], AF.Sqrt, [(k_ss, i, 0), k_eps], [(k_ss, i, 1)],
                bias=epsT[:, 0:1], scale=1.0 / D)
            recip(ssT[:, 3 * i + 2:3 * i + 3], ssT[:, 3 * i + 1:3 * i + 2], [(k_ss, i, 1)], [(k_ss, i, 2)])
            y, k_y = y_r.next()
            stt(y, s2, ssT[:, 3 * i + 2:3 * i + 3], fgT, ALU.mult, ALU.mult, ks2 + [(k_ss, i, 2), k_fg], [k_y])
            if i == 0:
                dma('pool', out[0:112, :], y[16:128, :], [k_y], [('out', i)])
            elif i < NT - 1:
                dma('pool', out[i * 128 - 16:i * 128 + 112, :], y, [k_y], [('out', i)])
            else:
                dma('pool', out[4080:4096, :], y[0:16, :], [k_y], [('out', i)])

    S.emit(nc, es)
    es.close()
    build.stats = S.stats
    return nc


def _fm(v, nchunk):
    return np.ascontiguousarray(np.asarray(v, np.float32).reshape(nchunk, 128).T)


def _rope_tab(hd):
    half = hd // 2
    inv = (np.float32(10000.0) ** (-np.arange(half, dtype=np.float32) / np.float32(half))).astype(np.float32)
    pos = np.arange(LP, dtype=np.float32)
    ang = (pos[:, None] * inv[None, :]).astype(np.float32)
    cos = np.cos(ang).astype(np.float32).T
    sin = np.sin(ang).astype(np.float32).T
    reps = 128 // half
    return np.ascontiguousarray(np.stack([np.tile(cos, (reps, 1)), np.tile(sin, (reps, 1))], 0))


def _prot():
    out = np.zeros((128, 256), np.float32)
    for col0, hd in ((0, 128), (128, 64)):
        half = hd // 2
        for dp in range(128):
            b, j = dp // hd, dp % hd
            if j < half:
                out[b * hd + j + half, col0 + dp] = -1.0
            else:
                out[b * hd + j - half, col0 + dp] = 1.0
    return out


def make_in_maps(x, meta_tokens, mix_norm_g, w_in, conv_ln_g, conv_ln_b, conv_dw_w, conv_dw_b,
                 conv_pw_out, attn_w_o, w_merge_out, ffn_norm_g, ffn_up, ffn_dw_w, ffn_dw_b,
                 ffn_down, final_norm_g):
    f = lambda a: np.ascontiguousarray(np.asarray(a, np.float32))
    x = f(x)
    B = x.shape[0]
    common = {
        "w_in": f(w_in[0]),
        "gmix": _fm(mix_norm_g[0], 8),
        "dwT": np.ascontiguousarray(np.transpose(f(conv_dw_w[0]).reshape(31, 8, 128), (2, 1, 0)).reshape(128, 8 * 31)),
        "cvec": np.ascontiguousarray(np.concatenate([_fm(conv_dw_b[0], 8), _fm(conv_ln_g[0], 8), _fm(conv_ln_b[0], 8)], 1)),
        "w_pw": f(conv_pw_out[0]),
        "w_o": f(attn_w_o[0]),
        "w_m": f(w_merge_out[0]),
        "gffn": _fm(ffn_norm_g[0], 8),
        "ffn_up": f(ffn_up[0]),
        "fdw": np.ascontiguousarray(np.transpose(f(ffn_dw_w[0]).reshape(3, 44, 128), (2, 1, 0)).reshape(128, 44 * 3)),
        "fdb": _fm(ffn_dw_b[0], 44),
        "ffn_down": f(ffn_down[0]),
        "fgb": np.ascontiguousarray(np.broadcast_to(f(final_norm_g)[None, :], (128, D))),
        "rope128": _rope_tab(128),
        "rope64": _rope_tab(64),
        "identd": np.eye(128, dtype=np.float32),
        "protd": _prot(),
        "pow2d": np.ascontiguousarray(np.broadcast_to((2.0 ** -np.arange(NIT + 1, dtype=np.float32))[None, :], (128, NIT + 1))),
    }
    tq = np.arange(128)[:, None]
    sk = np.arange(128)[None, :]
    vis = sk <= tq
    cm = np.concatenate([np.where(vis, np.float32(3e38), np.float32(-1e30)),
                         np.where(vis, np.float32(0.0), np.float32(1.0))], 1).astype(np.float32)
    common["cmaskd"] = np.ascontiguousarray(cm)
    meta = f(meta_tokens)
    pad = np.zeros((LP - L, D), np.float32)
    maps = []
    for b in range(B):
        d = dict(common)
        d["xs"] = np.ascontiguousarray(np.concatenate([meta, x[b], pad], 0))
        maps.append(d)
    return maps


_NC_CACHE = {}


def kernel(**inputs):
    maps = make_in_maps(**inputs)
    if 'nc' not in _NC_CACHE:
        _NC_CACHE['nc'] = build()
    nc = _NC_CACHE['nc']
    res = run_bass_kernel_spmd(nc, maps, core_ids=list(range(len(maps))))
    outs = [np.asarray(r["out"], np.float32) for r in res.results]
    return np.stack(outs, 0)
```

```python
import numpy as np
from contextlib import ExitStack
import concourse.bass as bass
import concourse.mybir as mybir
from concourse.bass_utils import run_bass_kernel_spmd

F32 = mybir.dt.float32
BF16 = mybir.dt.bfloat16
AF = mybir.ActivationFunctionType
ALU = mybir.AluOpType
AX = mybir.AxisListType

L = 4112
LP = 4224
NT = 33
D = 1024
KC = 8
NPAIR = 22
EPS = 1e-6
NIT = 17
C_Q, C_K, C_V, C_QI, C_KI, C_WI, C_G = 2048, 3072, 3328, 3584, 4096, 4160, 4168
TGS = [(t0, min(512, LP - t0)) for t0 in range(0, LP, 512)]

ENGS = ['pe', 'act', 'dve', 'pool', 'sp']
NDMASEM = 24


class Op:
    __slots__ = ('eng', 'fn', 'reads', 'writes', 'deps', 'sig', 'tok', 'is_dma', 'idx', 'prevdma')

    def __init__(self, eng, fn, reads, writes, is_dma):
        self.eng = eng
        self.fn = fn
        self.reads = reads
        self.writes = writes
        self.is_dma = is_dma
        self.deps = []
        self.sig = False
        self.tok = None
        self.prevdma = None


class Sched:
    def __init__(self):
        self.ops = {e: [] for e in ENGS}
        self.lastw = {}
        self.readers = {}
        self.n = 0
        self.pending = {e: [] for e in ENGS}
        self.lastc = {e: None for e in ENGS}
        self.dmas_since = []

    def barrier(self):
        for f in ENGS:
            lst = [self.lastc[e] for e in ENGS if e != f and self.lastc[e] is not None]
            self.pending[f] = lst + list(self.dmas_since)
        self.dmas_since = []
        self.lastw = {}
        self.readers = {}

    def add(self, eng, fn, reads=(), writes=(), dma=False):
        if not getattr(self, 'enabled', True):
            return None
        op = Op(eng, fn, tuple(reads), tuple(writes), dma)
        op.idx = self.n
        self.n += 1
        deps = {}
        for r in op.reads:
            w = self.lastw.get(r)
            if w is not None:
                deps[w.idx] = (w, True)
        for wk in op.writes:
            lw = self.lastw.get(wk)
            if lw is not None and lw.idx not in deps:
                deps[lw.idx] = (lw, False)
            for rd in self.readers.get(wk, ()):
                if rd.idx not in deps:
                    deps[rd.idx] = (rd, False)
        for d, raw in deps.values():
            if d is op:
                continue
            if (not d.is_dma) and (not op.is_dma) and d.eng == op.eng:
                if not raw or op.eng == 'pe':
                    continue
            op.deps.append(d)
            d.sig = True
        if self.pending[eng]:
            for d in self.pending[eng]:
                if d is not op and d not in op.deps:
                    op.deps.append(d)
                    d.sig = True
            self.pending[eng] = []
        for r in op.reads:
            self.readers.setdefault(r, []).append(op)
        for wk in op.writes:
            self.lastw[wk] = op
            self.readers[wk] = []
        self.ops[eng].append(op)
        if dma:
            self.dmas_since.append(op)
        else:
            self.lastc[eng] = op
        return op

    def emit(self, nc, es):
        esem = {e: es.enter_context(nc.semaphore('s_' + e)) for e in ENGS}
        dsem = [es.enter_context(nc.semaphore('d_%d' % i)) for i in range(NDMASEM)]
        duse = [0] * NDMASEM
        dlast = [None] * NDMASEM
        allops = sorted([o for e in ENGS for o in self.ops[e]], key=lambda o: o.idx)
        qengs = sorted({o.eng for o in allops if o.is_dma})
        per = NDMASEM // max(1, len(qengs))
        qsems = {e: list(range(i * per, (i + 1) * per)) for i, e in enumerate(qengs)}
        qrr = {e: 0 for e in qengs}
        cnt = {e: 0 for e in ENGS}
        pos = {}
        for e in ENGS:
            for n_, o in enumerate(self.ops[e]):
                pos[id(o)] = n_ + 1
        needed = set()
        for e in ENGS:
            waited0 = {}
            for o in self.ops[e]:
                best0 = {}
                for d in o.deps:
                    if d.is_dma:
                        continue
                    if pos[id(d)] > best0.get(d.eng, (0, None))[0]:
                        best0[d.eng] = (pos[id(d)], d)
                for k_, (v_, d_) in best0.items():
                    if waited0.get(k_, 0) < v_:
                        needed.add(id(d_))
                        waited0[k_] = v_
        for e in ENGS:
            for o in self.ops[e]:
                if not o.is_dma:
                    o.sig = id(o) in needed
        for o in allops:
            if o.is_dma:
                s = qsems[o.eng][qrr[o.eng] % per]
                qrr[o.eng] += 1
                duse[s] += 1
                o.prevdma = dlast[s]
                o.tok = (('d', s), 16 * duse[s])
                dlast[s] = o
            elif o.sig:
                cnt[o.eng] += 1
                o.tok = (('e', o.eng), cnt[o.eng])
        self.stats = dict(cnt)

        def semof(key):
            return esem[key[1]] if key[0] == 'e' else dsem[key[1]]

        for e in ENGS:
            nxt = None
            for o in reversed(self.ops[e]):
                if o.is_dma:
                    continue
                if o.sig:
                    nxt = o.tok
                elif nxt is not None:
                    o.tok = nxt
                else:
                    o.tok = None

        def run(ename, eng):
            waited = {}
            for o in self.ops[ename]:
                deps = list(o.deps)
                if o.is_dma and o.prevdma is not None:
                    deps.append(o.prevdma)
                best = {}
                for d in deps:
                    assert d.tok is not None, "dependency on op with no later signal"
                    k, v = d.tok
                    if best.get(k, 0) < v:
                        best[k] = v
                for k, v in best.items():
                    if waited.get(k, 0) < v:
                        eng.wait_ge(semof(k), v)
                        waited[k] = v
                ins = o.fn(eng)
                if o.is_dma:
                    ins.then_inc(semof(o.tok[0]), 16)
                elif o.sig:
                    ins.then_inc(semof(o.tok[0]), 1)

        with nc.Block() as block:
            @block.tensor
            def _(eng):
                run('pe', eng)

            @block.scalar
            def _(eng):
                run('act', eng)

            @block.vector
            def _(eng):
                run('dve', eng)

            @block.gpsimd
            def _(eng):
                run('pool', eng)

            @block.sync
            def _(eng):
                run('sp', eng)
                for s in range(NDMASEM):
                    if duse[s] > 0:
                        eng.wait_ge(dsem[s], 16 * duse[s])


class Ring:
    def __init__(self, items):
        self.items = items
        self.i = 0

    def next(self):
        it = self.items[self.i % len(self.items)]
        self.i += 1
        return it


def build(debug=False, phases="ACDEF", asub="1234"):
    nc = bass.Bass("TRN2", target_bir_lowering=False)

    def din(name, shape, dt=F32):
        return nc.dram_tensor(name, list(shape), dt, kind="ExternalInput").ap()

    skind = "ExternalOutput" if debug else "Internal"

    def dscr(name, shape, dt):
        return nc.dram_tensor(name, list(shape), dt, kind=skind).ap()

    xs = din("xs", [LP, D])
    w_in = din("w_in", [D, 6216])
    gmix = din("gmix", [128, 8])
    dwT = din("dwT", [128, 8 * 31])
    cvec = din("cvec", [128, 24])
    w_pw = din("w_pw", [D, D])
    w_o = din("w_o", [D, D])
    w_m = din("w_m", [D, D])
    gffn = din("gffn", [128, 8])
    ffn_up = din("ffn_up", [D, 2 * 2816])
    fdw = din("fdw", [128, 44 * 3])
    fdb = din("fdb", [128, 44])
    ffn_down = din("ffn_down", [2816, D])
    fgb = din("fgb", [128, D])
    rope128 = din("rope128", [2, 128, LP])
    rope64 = din("rope64", [2, 128, LP])
    identd = din("identd", [128, 128])
    protd = din("protd", [128, 256])
    cmaskd = din("cmaskd", [128, 256])
    pow2d = din("pow2d", [128, NIT + 1])
    out = nc.dram_tensor("out", [4096, D], F32, kind="ExternalOutput").ap()

    CVd = dscr("CVd", [128, 8, LP], BF16)
    QTd = dscr("QTd", [128, 8, LP], BF16)
    KTd = dscr("KTd", [128, 2, LP], BF16)
    QITd = dscr("QITd", [128, 4, LP], BF16)
    KITd = dscr("KITd", [128, LP], BF16)
    Vd = dscr("Vd", [128, NT, 260], BF16)
    WId = dscr("WId", [128, NT, 16], F32)
    SGd = dscr("SGd", [128, 16, LP], F32)
    OTd = dscr("OTd", [128, 8, LP], BF16)
    S1d = dscr("S1d", [LP, D], F32)
    H2Td = dscr("H2Td", [128, 8, LP], BF16)
    ACTd = dscr("ACTd", [128, NPAIR, LP], BF16)

    S = Sched()
    es = ExitStack()
    import os as _os
    AW = int(_os.environ.get('AWK', '42')) * 1024
    arena = es.enter_context(nc.sbuf_tensor("arena", [128, AW], F32))
    psum = es.enter_context(nc.psum_tensor("psum", [128, 8, 512], F32))
    st = {'off': 0}

    def alloc(shape, dt):
        n = int(np.prod(shape))
        words = n if dt == F32 else (n + 1) // 2
        words = (words + 3) // 4 * 4
        off = st['off']
        if off + words > AW and not getattr(S, 'enabled', True):
            off = 0
        st['off'] = off + words
        assert st['off'] <= AW, ("arena overflow", st['off'])
        ap = arena[:, off:off + words]
        if dt != F32:
            ap = ap.bitcast(dt)
        ap = ap[:, 0:n]
        if len(shape) == 2:
            ap = ap.rearrange("p (a b) -> p a b", a=shape[0])
        elif len(shape) == 3:
            ap = ap.rearrange("p (a b c) -> p a b c", a=shape[0], b=shape[1])
        return ap

    uid = [0]

    def tile(shape, dt, name=None):
        uid[0] += 1
        return (alloc(shape, dt), (name or 't', uid[0]))

    def ring(n, shape, dt, name=None):
        return Ring([tile(shape, dt, name) for _ in range(n)])

    def bank(b):
        return psum[:, b, :]

    def bankbf(b):
        return psum[:, b, :].bitcast(BF16)

    def PB(b):
        return ('ps', b)

    def dma(q, out_, in_, reads, writes):
        S.add(q, lambda e: e.dma_start(out=out_, in_=in_), reads, writes, dma=True)

    def mm(out_, lhsT, rhs, start, stop, reads, writes, sgc=False):
        if sgc:
            S.add('pe', lambda e: e.matmul(out_, lhsT=lhsT, rhs=rhs, start=start, stop=stop, skip_group_check=True), reads, writes)
        else:
            S.add('pe', lambda e: e.matmul(out_, lhsT=lhsT, rhs=rhs, start=start, stop=stop), reads, writes)

    def tr(out_, in_, ident, reads, writes):
        S.add('pe', lambda e: e.transpose(out=out_, in_=in_, identity=ident), reads, writes)

    def act(out_, in_, func, reads, writes, bias=None, scale=None, accum=None):
        kw = {}
        if bias is not None:
            kw['bias'] = bias
        if scale is not None:
            kw['scale'] = scale
        if accum is not None:
            kw['accum_out'] = accum
        S.add('act', lambda e: e.activation(out=out_, in_=in_, func=func, **kw), reads, writes)

    def ts(eng, out_, in0, s1, s2, op0, op1, reads, writes, accum=None):
        if accum is not None:
            S.add(eng, lambda e: e.tensor_scalar(out=out_, in0=in0, scalar1=s1, scalar2=s2, op0=op0, op1=op1, accum_out=accum), reads, writes)
        elif op1 is None:
            S.add(eng, lambda e: e.tensor_scalar(out=out_, in0=in0, scalar1=s1, scalar2=None, op0=op0), reads, writes)
        else:
            S.add(eng, lambda e: e.tensor_scalar(out=out_, in0=in0, scalar1=s1, scalar2=s2, op0=op0, op1=op1), reads, writes)

    def tt(eng, out_, in0, in1, op, reads, writes):
        S.add(eng, lambda e: e.tensor_tensor(out=out_, in0=in0, in1=in1, op=op), reads, writes)

    def stt(out_, in0, scalar, in1, op0, op1, reads, writes):
        S.add('dve', lambda e: e.scalar_tensor_tensor(out=out_, in0=in0, scalar=scalar, in1=in1, op0=op0, op1=op1), reads, writes)

    def cp(eng, out_, in_, reads, writes):
        if eng == 'act':
            act(out_, in_, AF.Copy, reads, writes)
        else:
            S.add(eng, lambda e: e.tensor_copy(out=out_, in_=in_), reads, writes)

    def memset(eng, ap, val, writes):
        S.add(eng, lambda e: e.memset(ap, val), (), writes)

    def recip(out_, in_, reads, writes):
        S.add('dve', lambda e: e.reciprocal(out=out_, in_=in_), reads, writes)

    identf, k_identf = tile([128], F32, 'identf')
    identb, k_identb = tile([128], BF16, 'identb')
    onesb, k_onesb = tile([128], BF16, 'onesb')
    epsT, k_eps = tile([1], F32, 'eps')
    gmixT, k_gmix = tile([8], F32, 'gmix')
    ngmixT, k_ngmix = tile([8], F32, 'ngmix')
    gffnT, k_gffn = tile([8], F32, 'gffn')
    cvecT, k_cvec = tile([24], F32, 'cvec')
    dma('sp', identf, identd[:, :], [], [k_identf])
    dma('sp', gmixT, gmix[:, :], [], [k_gmix])
    dma('sp', gffnT, gffn[:, :], [], [k_gffn])
    dma('sp', cvecT, cvec[:, :], [], [k_cvec])
    cp('dve', identb, identf, [k_identf], [k_identb])
    memset('dve', onesb, 1.0, [k_onesb])
    memset('dve', epsT, EPS, [k_eps])
    ts('dve', ngmixT, gmixT, -1.0, None, ALU.mult, None, [k_gmix], [k_ngmix])
    persist_mark = st['off']

    def rms_to_T(xt, k_xt, i, ssT, k_ss, junk, k_junk, xn, k_xn, trb, dstT, k_dst, evac_eng):
        kx = list(k_xt) if isinstance(k_xt, list) else [k_xt]
        act(junk, xt, AF.Square, kx, [k_junk, (k_ss, i, 0)], accum=ssT[:, 3 * i:3 * i + 1])
        act(ssT[:, 3 * i + 1:3 * i + 2], ssT[:, 3 * i:3 * i + 1], AF.Sqrt, [(k_ss, i, 0), k_eps], [(k_ss, i, 1)],
            bias=epsT[:, 0:1], scale=1.0 / D)
        recip(ssT[:, 3 * i + 2:3 * i + 3], ssT[:, 3 * i + 1:3 * i + 2], [(k_ss, i, 1)], [(k_ss, i, 2)])
        ts('dve', xn, xt, ssT[:, 3 * i + 2:3 * i + 3], None, ALU.mult, None, kx + [(k_ss, i, 2)], [k_xn])
        pT = bankbf(trb).rearrange("p (a b) -> p a b", a=8)
        for kc in range(8):
            tr(pT[:, kc, :], xn[:, kc * 128:(kc + 1) * 128], identb, [k_xn, k_identb], [PB(trb)])
        cp(evac_eng, dstT, pT, [PB(trb)], [k_dst])

    def load_w(src3, ncols, stage, k_stage, dst, k_dst, gT, k_g, ceng, rot=None):
        if isinstance(src3, list):
            for (s_ap, st_ap) in src3:
                dma('sp', st_ap, s_ap, [], [k_stage])
        else:
            dma('sp', stage, src3, [], [k_stage])
        if gT is None:
            cp(ceng, dst, stage, [k_stage], [k_dst])
        else:
            for kc in range(dst.shape[1]):
                ts(ceng, dst[:, kc], stage[:, kc], gT[:, kc:kc + 1], 0.0, ALU.mult, ALU.add,
                   [k_stage, k_g], [k_dst])
        if rot is not None:
            dstr, k_dstr, half, ngT, k_ng = rot
            for kc in range(8):
                sv = stage[:, kc].rearrange("p (b two h) -> p b two h", two=2, h=half)
                dv = dstr[:, kc].rearrange("p (b two h) -> p b two h", two=2, h=half)
                ts(ceng, dv[:, :, 0, :], sv[:, :, 1, :], ngT[:, kc:kc + 1], 0.0, ALU.mult, ALU.add,
                   [k_stage, k_ng], [k_dstr])
                ts(ceng, dv[:, :, 1, :], sv[:, :, 0, :], gT[:, kc:kc + 1], 0.0, ALU.mult, ALU.add,
                   [k_stage, k_g], [k_dstr])

    def wsrc(w2d, col0, ncols):
        return w2d[:, col0:col0 + ncols].rearrange("(kc p) c -> p kc c", p=128)

    if 'A' in phases:
        hT, k_hT = tile([8, LP], BF16, 'hT')

        def hk(t0, n):
            return [(k_hT, i) for i in range(t0 // 128, (t0 + n) // 128)]

        m0 = st['off']
        xt_r = ring(3, [D], F32, 'xt')
        xn_r = ring(2, [D], BF16, 'xn')
        junk, k_junk = tile([D], F32, 'junk')
        ssT, k_ss = tile([3 * NT], F32, 'ss')
        for i in range(NT):
            xt, k_xt = xt_r.next()
            xn, k_xn = xn_r.next()
            dma('sp', xt, xs[i * 128:(i + 1) * 128, :], [], [k_xt])
            rms_to_T(xt, k_xt, i, ssT, k_ss, junk, k_junk, xn, k_xn, 6 + (i % 2),
                     hT[:, :, i * 128:(i + 1) * 128], (k_hT, i), 'act' if i % 2 else 'dve')

        mA = st['off']
        stage_r = ring(2, [8, 256], F32, 'stage')

        S.enabled = '1' in asub
        wA_r = ring(2, [8, 256], BF16, 'wA')
        dg_r = ring(2, [31, 128], BF16, 'dg')
        glu_r = ring(2, [30 + LP], BF16, 'glu')
        sg_r = ring(2, [512], F32, 'sg')
        cv_r = ring(3, [512], BF16, 'cv')
        dwTt, k_dwT = tile([8 * 31], F32, 'dwT')
        dma('sp', dwTt, dwT[:, :], [], [k_dwT])
        for g_, kg in glu_r.items:
            memset('pool', g_[:, 0:30], 0.0, [(kg, -1)])
        pa_r = Ring([0, 1])
        pg_r = Ring([2, 3])
        pc_r = Ring([4, 5])
        for cc in range(8):
            stg, k_stg = stage_r.next()
            wA, k_wA = wA_r.next()
            dg, k_dg = dg_r.next()
            glu, k_glu = glu_r.next()
            src = [(wsrc(w_in, cc * 128, 128), stg[:, :, 0:128]), (wsrc(w_in, 1024 + cc * 128, 128), stg[:, :, 128:256])]
            load_w(src, 256, stg, k_stg, wA, k_wA, gmixT, k_gmix, 'pool')
            for k in range(31):
                ts('pool', dg[:, k, :], identb, dwTt[:, cc * 31 + k:cc * 31 + k + 1], 0.0, ALU.mult, ALU.add,
                   [k_identb, k_dwT], [k_dg])

            def proj(tg):
                t0, n = TGS[tg]
                pa = pa_r.next()
                pg = pg_r.next()
                for kc in range(8):
                    mm(bank(pa)[:, 0:n], wA[:, kc, 0:128], hT[:, kc, t0:t0 + n], kc == 0, kc == 7,
                       [k_wA] + hk(t0, n), [PB(pa)])
                for kc in range(8):
                    mm(bank(pg)[:, 0:n], wA[:, kc, 128:256], hT[:, kc, t0:t0 + n], kc == 0, kc == 7,
                       [k_wA] + hk(t0, n), [PB(pg)])
                sg, k_sg = sg_r.next()
                act(sg[:, 0:n], bank(pg)[:, 0:n], AF.Sigmoid, [PB(pg)], [k_sg])
                tt('dve', glu[:, 30 + t0:30 + t0 + n], bank(pa)[:, 0:n], sg[:, 0:n], ALU.mult,
                   [PB(pa), k_sg], [(k_glu, tg)])

            def conv(tg):
                t0, n = TGS[tg]
                pc = pc_r.next()
                for k in range(31):
                    mm(bank(pc)[:, 0:n], dg[:, k, :], glu[:, t0 + k:t0 + k + n], k == 0, k == 30,
                       [k_dg, (k_glu, tg), (k_glu, tg - 1)], [PB(pc)])
                cv, k_cv = cv_r.next()
                act(cv[:, 0:n], bank(pc)[:, 0:n], AF.Identity, [PB(pc), k_cvec], [k_cv], bias=cvecT[:, cc:cc + 1])
                dma('pool', CVd[:, cc, t0:t0 + n], cv[:, 0:n], [k_cv], [('CVd', cc, tg)])

            for tg in range(len(TGS) + 1):
                if tg < len(TGS):
                    proj(tg)
                if tg >= 1:
                    conv(tg - 1)

        S.barrier()
        mA = m0
        st['off'] = mA
        stage_r = ring(1, [8, 256], F32, 'stage')
        S.enabled = '2' in asub
        WR, k_WR = tile([8, 15 * 128], BF16, 'WR')
        protf, k_protf = tile([256], F32, 'protf')
        protb, k_protb = tile([256], BF16, 'protb')
        dma('sp', protf, protd[:, :], [], [k_protf])
        cp('dve', protb, protf, [k_protf], [k_protb])
        blocks = [(C_Q, 0), (C_Q + 256, 2), (C_Q + 512, 4), (C_Q + 768, 6), (C_K, 8), (C_QI, 10), (C_QI + 256, 12)]
        stage2_r = ring(2, [8, 256], F32, 'stage2')
        for bi, (col0, ch0) in enumerate(blocks):
            stg, k_stg = stage2_r.next()
            load_w(wsrc(w_in, col0, 256), 256, stg, k_stg, WR[:, :, ch0 * 128:(ch0 + 2) * 128], (k_WR, bi),
                   gmixT, k_gmix, 'pool' if bi % 2 else 'dve')
        stg, k_stg = stage2_r.next()
        for hh in range(2):
            load_w(wsrc(w_in, C_KI, 64), 64, stg[:, :, hh * 64:(hh + 1) * 64], k_stg,
                   WR[:, :, 14 * 128 + hh * 64:14 * 128 + (hh + 1) * 64], (k_WR, 7 + hh), gmixT, k_gmix, 'dve')
        kWRall = [(k_WR, b) for b in range(9)]
        rp_r = ring(2, [4, 512], F32, 'rp')
        t1_r = ring(2, [512], F32, 't1')
        t2_r = ring(2, [512], F32, 't2')
        ob_r = ring(3, [512], BF16, 'ob')
        qb_r = ring(3, [512], BF16, 'qb')
        qf_r = ring(3, [512], F32, 'qf')
        pA_r = Ring([0, 1, 2])
        pB_r = Ring([3, 4, 5])
        work = [(tg, c) for tg in range(len(TGS)) for c in range(15)]
        rpof = {}
        pAof = {}

        def a2_mm(w):
            tg, c = work[w]
            t0, n = TGS[tg]
            if c == 0:
                rp, k_rp = rp_r.next()
                dma('sp', rp[:, 0:2, 0:n], rope128[:, :, t0:t0 + n].rearrange("a p t -> p a t"), [], [(k_rp, 0)])
                dma('sp', rp[:, 2:4, 0:n], rope64[:, :, t0:t0 + n].rearrange("a p t -> p a t"), [], [(k_rp, 1)])
                rpof[tg] = (rp, k_rp)
            pA = pA_r.next()
            pAof[w] = pA
            for kc in range(8):
                mm(bank(pA)[:, 0:n], WR[:, kc, c * 128:(c + 1) * 128], hT[:, kc, t0:t0 + n], kc == 0, kc == 7,
                   kWRall + hk(t0, n), [PB(pA)])

        a2_mm(0)
        for w, (tg, c) in enumerate(work):
            t0, n = TGS[tg]
            if w + 1 < len(work):
                a2_mm(w + 1)
            rp, k_rp = rpof[tg]
            pA = pAof[w]
            pB = pB_r.next()
            qb, k_qb = qb_r.next()
            cp('act', qb[:, 0:n], bank(pA)[:, 0:n], [PB(pA)], [k_qb])
            qf, k_qf = qf_r.next()
            cp('act', qf[:, 0:n], bank(pA)[:, 0:n], [PB(pA)], [k_qf])
            pcol = 0 if c < 10 else 128
            mm(bank(pB)[:, 0:n], protb[:, pcol:pcol + 128], qb[:, 0:n], True, True, [k_protb, k_qb], [PB(pB)])
            ti = 0 if c < 10 else 2
            t1, k_t1 = t1_r.next()
            t2, k_t2 = t2_r.next()
            ob, k_ob = ob_r.next()
            tt('dve', t1[:, 0:n], qf[:, 0:n], rp[:, ti, 0:n], ALU.mult, [k_qf, (k_rp, ti // 2)], [k_t1])
            tt('dve', t2[:, 0:n], bank(pB)[:, 0:n], rp[:, ti + 1, 0:n], ALU.mult, [PB(pB), (k_rp, ti // 2)], [k_t2])
            tt('pool' if w % 2 else 'dve', ob[:, 0:n], t1[:, 0:n], t2[:, 0:n], ALU.add, [k_t1, k_t2], [k_ob])
            if c < 8:
                dst = QTd[:, c, t0:t0 + n]
            elif c < 10:
                dst = KTd[:, c - 8, t0:t0 + n]
            elif c < 14:
                dst = QITd[:, c - 10, t0:t0 + n]
            else:
                dst = KITd[:, t0:t0 + n]
            dma('pool', dst, ob[:, 0:n], [k_ob], [('A2o', c, tg)])

        S.barrier()
        st['off'] = mA
        stage_r = ring(2, [8, 256], F32, 'stage')
        S.enabled = '3' in asub
        wv, k_wv = tile([8, 264], BF16, 'wv')
        stg, k_stg = stage_r.next()
        load_w(wsrc(w_in, C_V, 256), 256, stg, k_stg, wv[:, :, 0:256], (k_wv, 0), gmixT, k_gmix, 'dve')
        stg, k_stg = stage_r.next()
        dma('sp', stg[:, :, 0:64], wsrc(w_in, C_WI, 64), [], [k_stg])
        for kc in range(8):
            ts('dve', wv[:, kc, 256:264], stg[:, kc, 0:8], gmixT[:, kc:kc + 1], 0.0, ALU.mult, ALU.add,
               [k_stg, k_gmix], [(k_wv, 1)])
        vt_r = ring(3, [2, 130], BF16, 'vt')
        wi_r = ring(3, [16], F32, 'wi')
        wr_r = ring(3, [8], F32, 'wr')
        for v_, kv in vt_r.items:
            memset('dve', v_, 1.0, [(kv, 'one'), (kv, 'v')])
        pv_r = Ring([6, 7])
        CIDX = (8.0 ** -0.5) * (64.0 ** -0.5)
        for i in range(NT):
            pv = pv_r.next()
            for kc in range(8):
                mm(bank(pv)[:, 0:264], hT[:, kc, i * 128:(i + 1) * 128], wv[:, kc, :], kc == 0, kc == 7,
                   [(k_wv, 0), (k_wv, 1), (k_hT, i)], [PB(pv)])
            vt, k_vt = vt_r.next()
            wi, k_wi = wi_r.next()
            CUT = int(_os.environ.get('A3CUT', '9'))
            if CUT < 3:
                continue
            cp('act', vt[:, :, 0:128], bank(pv)[:, 0:256].rearrange("p (g d) -> p g d", g=2), [PB(pv)], [(k_vt, 'v')])
            if CUT < 4:
                continue
            wr, k_wr = wr_r.next()
            cp('act', wr, bank(pv)[:, 256:264], [PB(pv)], [k_wr])
            ts('dve', wi[:, 8:16], wr, 0.0, 0.5, ALU.is_ge, ALU.subtract, [k_wr], [(k_wi, 1)])
            stt(wi[:, 0:8], wr, 4.0 * CIDX, wi[:, 8:16], ALU.mult, ALU.mult, [k_wr, (k_wi, 1)], [(k_wi, 0)])
            if CUT < 5:
                continue
            dma('sp', Vd[:, i, :], vt.rearrange("p g d -> p (g d)"), [(k_vt, 'v'), (k_vt, 'one')], [('Vd', i)])
            dma('sp', WId[:, i, :], wi, [(k_wi, 0), (k_wi, 1)], [('WId', i)])

        S.enabled = '4' in asub
        wg_r = ring(2, [8, 256], BF16, 'wg')
        sgo_r = ring(3, [512], F32, 'sgo')
        pq_r = Ring([0, 1, 2, 3])
        for blk in range(8):
            stg, k_stg = stage_r.next()
            wg, k_wg = wg_r.next()
            load_w(wsrc(w_in, C_G + blk * 256, 256), 256, stg, k_stg, wg, k_wg, gmixT, k_gmix, 'pool' if blk % 2 else 'dve')
            for tg, (t0, n) in enumerate(TGS):
                for cl in range(2):
                    c = blk * 2 + cl
                    pq = pq_r.next()
                    for kc in range(8):
                        mm(bank(pq)[:, 0:n], wg[:, kc, cl * 128:(cl + 1) * 128], hT[:, kc, t0:t0 + n], kc == 0, kc == 7,
                           [k_wg] + hk(t0, n), [PB(pq)])
                    sgo, k_sgo = sgo_r.next()
                    act(sgo[:, 0:n], bank(pq)[:, 0:n], AF.Sigmoid, [PB(pq)], [k_sgo])
                    dma('pool', SGd[:, c, t0:t0 + n], sgo[:, 0:n], [k_sgo], [('SGd', c, tg)])
        S.enabled = True
        S.barrier()
        st['off'] = persist_mark

    if 'C' in phases:
        KT, k_KT = tile([2, LP], BF16, 'KT')
        KIT, k_KIT = tile([LP], BF16, 'KIT')
        Vt, k_V = tile([NT, 260], BF16, 'V')
        WIt, k_WI = tile([NT, 16], F32, 'WI')
        cmt, k_cm = tile([256], F32, 'cm')
        cnegb, k_cnegb = tile([128], BF16, 'cnegb')
        negI4, k_negI4 = tile([512], BF16, 'negI4')
        pow2, k_pow2 = tile([NIT + 1], F32, 'pow2')
        dma('sp', KT, KTd[:, :, :], [], [k_KT])
        dma('sp', KIT, KITd[:, :], [], [k_KIT])
        dma('sp', Vt, Vd[:, :, :], [], [k_V])
        dma('sp', WIt, WId[:, :, :], [], [k_WI])
        dma('sp', cmt, cmaskd[:, :], [], [k_cm])
        dma('sp', pow2, pow2d[:, :], [], [k_pow2])
        cp('dve', cnegb, cmt[:, 128:256], [k_cm], [k_cnegb])
        for r4 in range(4):
            ts('dve', negI4[:, r4 * 128:(r4 + 1) * 128], identf, -32768.0, None, ALU.mult, None, [k_identf], [k_negI4])
        score_r = ring(2, [LP], F32, 'score')
        mneg_r = ring(2, [LP], BF16, 'mneg')
        junkb, k_junkb = tile([LP], BF16, 'junkb')
        junka, k_junka = tile([LP], BF16, 'junka')
        R_r = ring(4, [512], BF16, 'R')
        dg_r = ring(2, [8, 128], BF16, 'dgs')
        SCB = 7
        qt_r = ring(2, [8, 128], BF16, 'qt')
        qp_r = ring(2, [4, 2, 128], BF16, 'qp')
        pt_r = ring(3, [512], BF16, 'pt')
        obuf_r = ring(2, [D], BF16, 'obuf')
        oT_r = ring(2, [8, 128], BF16, 'oTt')
        sm_r = ring(2, [8 + 2 * NIT + 16], F32, 'sm')
        rden_r = ring(2, [8], F32, 'rden')
        for qp_, kq in qp_r.items:
            memset('pool', qp_, 0.0, [(kq, 0), (kq, 1)])
        pi_r = Ring([0, 1])
        pl_r = Ring([2, 3])
        ACCB = [4, 5, 6]
        TRB = 2
        idx_state = {}
        ACT_COUNT = set()

        sc_state = {}
        CMX = 8 + 2 * NIT + 4

        def gen_scores(i):
            nk = 128 * (i + 1)
            qp, k_qp = qp_r.next()
            dma('sp', qp[0:64, :, 0, :], QITd[0:64, :, i * 128:(i + 1) * 128], [], [(k_qp, 0)])
            dma('sp', qp[64:128, :, 1, :], QITd[64:128, :, i * 128:(i + 1) * 128], [], [(k_qp, 1)])
            score, k_sc = score_r.next()
            sm, k_sm = sm_r.next()
            dg, k_dg = dg_r.next()
            for h in range(8):
                ts('dve', dg[:, h, :], identb, WIt[:, i, 8 + h:9 + h], None, ALU.mult, None, [k_identb, k_WI], [(k_dg, h)])
            rounds = [(k0, h) for k0 in range(0, nk, 512) for h in range(8)]
            piof = {}
            rof = {}

            def emit_mm(r):
                k0, h = rounds[r]
                n = min(512, nk - k0)
                c, par = h // 2, h % 2
                pi = pi_r.next()
                piof[r] = pi
                mm(bank(pi)[:, 0:n], qp[:, c, par, :], KIT[:, k0:k0 + n],
                   True, True, [(k_qp, 0), (k_qp, 1), k_KIT], [PB(pi)])

            def emit_sum(r):
                k0, h = rounds[r]
                n = min(512, nk - k0)
                R, k_R = rof[r]
                mm(bank(SCB)[:, 0:n], dg[:, h, :], R[:, 0:n], h == 0, h == 7, [k_R, (k_dg, h)], [PB(SCB)])
                if h == 7:
                    ci_ = k0 // 512
                    ts('dve', score[:, k0:k0 + n], bank(SCB)[:, 0:n], 1.0, None, ALU.mult, ALU.max,
                       [PB(SCB)], [(k_sc, k0), (k_sm, 'cm', ci_)], accum=sm[:, CMX + ci_:CMX + ci_ + 1])

            emit_mm(0)
            for r, (k0, h) in enumerate(rounds):
                n = min(512, nk - k0)
                if r + 1 < len(rounds):
                    emit_mm(r + 1)
                pi = piof[r]
                R, k_R = R_r.next()
                rof[r] = (R, k_R)
                act(R[:, 0:n], bank(pi)[:, 0:n], AF.Relu, [PB(pi), k_WI], [k_R], scale=WIt[:, i, h:h + 1])
                if r >= 1:
                    emit_sum(r - 1)
                yield
            emit_sum(len(rounds) - 1)
            allsc = [(k_sc, k0) for k0 in range(0, nk, 512)]
            memset('dve', score[:, 0:16], 1e30, [(k_sc, 0)])
            d0 = 128 * i
            tt('dve', score[:, d0:d0 + 128], score[:, d0:d0 + 128], cmt[:, 0:128], ALU.min,
               [k_cm, (k_sc, d0 // 512 * 512)], [(k_sc, d0 // 512 * 512)])
            nch_ = (nk + 511) // 512
            S.add('dve', lambda e: e.tensor_reduce(out=sm[:, 0:1], in_=sm[:, CMX:CMX + nch_], op=ALU.max, axis=AX.X),
                  [(k_sm, 'cm', c_) for c_ in range(nch_)], [(k_sm, 'hi')])
            yield
            dlo = min(d0, 512)
            S.add('dve', lambda e: e.tensor_reduce(out=sm[:, 1:2], in_=score[:, 0:dlo], op=ALU.min, axis=AX.X),
                  allsc, [(k_sm, 'lo')])
            tt('dve', sm[:, 2:3], sm[:, 0:1], sm[:, 1:2], ALU.subtract, [(k_sm, 'hi'), (k_sm, 'lo')], [(k_sm, 'r')])
            stt(sm[:, 8:9], sm[:, 2:3], 0.5, sm[:, 1:2], ALU.mult, ALU.add, [(k_sm, 'r'), (k_sm, 'lo')], [(k_sm, 'mid', 0)])
            ts('dve', sm[:, 8 + NIT + 1:8 + 2 * NIT + 2], pow2, sm[:, 2:3], None, ALU.mult, None, [k_pow2, (k_sm, 'r')],
               [(k_sm, 'rk')])
            sc_state[i] = (score, k_sc, sm, k_sm, nk, allsc)
            yield

        def gen_bisect(i):
            score, k_sc, sm, k_sm, nk, allsc = sc_state[i]
            mneg, k_mn = mneg_r.next()
            for k in range(1, NIT + 1):
                midp = sm[:, 8 + k - 1:8 + k]
                if k in ACT_COUNT:
                    act(junka[:, 0:nk], score[:, 0:nk], AF.Sign, allsc + [(k_sm, 'mid', k - 1)], [k_junka, (k_sm, 'cnt')],
                        bias=midp, scale=-1.0, accum=sm[:, 3:4])
                    ts('dve', sm[:, 4:5], sm[:, 3:4], float(nk - 511), 0.5, ALU.is_le, ALU.subtract, [(k_sm, 'cnt')], [(k_sm, 'sg')])
                else:
                    ts('dve', junkb[:, 0:nk], score[:, 0:nk], midp, None, ALU.is_ge, ALU.add,
                       allsc + [(k_sm, 'mid', k - 1)], [k_junkb, (k_sm, 'cnt')], accum=sm[:, 3:4])
                    ts('dve', sm[:, 4:5], sm[:, 3:4], 255.5, 0.5, ALU.is_ge, ALU.subtract, [(k_sm, 'cnt')], [(k_sm, 'sg')])
                stt(sm[:, 8 + k:8 + k + 1], sm[:, 4:5], sm[:, 8 + NIT + 1 + k:8 + NIT + 2 + k], midp, ALU.mult, ALU.add,
                    [(k_sm, 'sg'), (k_sm, 'rk'), (k_sm, 'mid', k - 1)], [(k_sm, 'mid', k)])
                yield
            stt(sm[:, 5:6], sm[:, 8 + 2 * NIT + 1:8 + 2 * NIT + 2], -0.5, sm[:, 8 + NIT:8 + NIT + 1], ALU.mult, ALU.add,
                [(k_sm, 'rk'), (k_sm, 'mid', NIT)], [(k_sm, 'thr')])
            ts('dve', mneg[:, 0:nk], score[:, 0:nk], sm[:, 5:6], None, ALU.is_lt, None, allsc + [(k_sm, 'thr')], [k_mn])
            idx_state[i] = (mneg, k_mn)
            yield

        def attention(i):
            qt, k_qt = qt_r.next()
            dma('sp', qt, QTd[:, :, i * 128:(i + 1) * 128], [], [k_qt])
            first_in_bank = {4: True, 5: True, 6: True}
            steps = [(j, g) for j in range(i + 1) for g in range(2)]
            plof = {}

            def emit_qk(sidx):
                j, g = steps[sidx]
                pl = pl_r.next()
                plof[sidx] = pl
                need_mask = (i >= 2) or (j == i)
                mm(bank(pl), KT[:, g, j * 128:(j + 1) * 128], qt[:, 4 * g:4 * g + 4, :], True, not need_mask,
                   [k_KT, k_qt], [PB(pl)])
                if need_mask:
                    if i >= 2:
                        mneg, k_mn = idx_state[i]
                        mm(bank(pl), mneg[:, j * 128:(j + 1) * 128], negI4, False, True, [k_mn, k_negI4], [PB(pl)])
                    else:
                        mm(bank(pl), cnegb, negI4, False, True, [k_cnegb, k_negI4], [PB(pl)])

            emit_qk(0)
            for sidx, (j, g) in enumerate(steps):
                if sidx + 1 < len(steps):
                    emit_qk(sidx + 1)
                pl = plof[sidx]
                pt, k_pt = pt_r.next()
                act(pt, bank(pl), AF.Exp, [PB(pl)], [k_pt], scale=128.0 ** -0.5)
                for hh in range(4):
                    h = 4 * g + hh
                    b = ACCB[h // 3]
                    o0 = (h % 3) * 129
                    stf = (j == 0) and first_in_bank[b]
                    if j == 0:
                        first_in_bank[b] = False
                    mm(bank(b)[:, o0:o0 + 129], pt[:, hh * 128:(hh + 1) * 128], Vt[:, j, g * 130:g * 130 + 129],
                       stf, j == i, [k_pt, k_V], [PB(b)], sgc=True)
                yield
            rden, k_rden = rden_r.next()
            ob, k_ob = obuf_r.next()
            for h in range(8):
                b = ACCB[h // 3]
                o0 = (h % 3) * 129
                recip(rden[:, h:h + 1], bank(b)[:, o0 + 128:o0 + 129], [PB(b)], [(k_rden, h)])
                ts('dve', ob[:, h * 128:(h + 1) * 128], bank(b)[:, o0:o0 + 128], rden[:, h:h + 1], None, ALU.mult, None,
                   [PB(b), (k_rden, h)], [(k_ob, h)])
            pT = bankbf(TRB).rearrange("p (a b) -> p a b", a=8)
            for h in range(8):
                tr(pT[:, h, :], ob[:, h * 128:(h + 1) * 128], identb, [(k_ob, h), k_identb], [PB(TRB)])
            oTt, k_oT = oT_r.next()
            cp('act', oTt, pT, [PB(TRB)], [k_oT])
            dma('pool', OTd[:, :, i * 128:(i + 1) * 128], oTt, [k_oT], [('OTd', i)])
            yield

        def run_interleaved(gens):
            state = [[g, max(n, 1), 0, True] for g, n in gens if g is not None]
            while any(a[3] for a in state):
                best = None
                for a in state:
                    if a[3] and (best is None or a[2] / a[1] < best[2] / best[1]):
                        best = a
                try:
                    next(best[0])
                    best[2] += 1
                except StopIteration:
                    best[3] = False

        for T in range(NT):
            gens = [(attention(T), 2 * (T + 1) + 1)]
            if 2 <= T + 1 < NT:
                gens.append((gen_bisect(T + 1), NIT + 1))
            if 2 <= T + 2 < NT:
                gens.append((gen_scores(T + 2), 8 * ((128 * (T + 3) + 511) // 512) + 2))
            run_interleaved(gens)
        S.barrier()
        st['off'] = persist_mark

    if 'D' in phases:
        wpw, k_wpw = tile([8, D], BF16, 'wpw')
        wo, k_wo = tile([8, D], BF16, 'wo')
        wm, k_wm = tile([8, D], BF16, 'wm')
        mD = st['off']
        stage_r = ring(2, [8, 256], F32, 'stage')
        bi = 0
        for (wsrc2, wdst, kd) in ((w_pw, wpw, k_wpw), (w_o, wo, k_wo), (w_m, wm, k_wm)):
            for b4 in range(4):
                stg, k_stg = stage_r.next()
                load_w(wsrc(wsrc2, b4 * 256, 256), 256, stg, k_stg, wdst[:, :, b4 * 256:(b4 + 1) * 256], (kd, b4),
                       None, None, 'act' if bi % 2 else 'dve')
                bi += 1
        S.barrier()
        st['off'] = mD
        kwpw, kwo, kwm = [], [], []
        cvl, k_cvl = tile([8, 512], BF16, 'cvl')
        ot_r = ring(2, [8, 512], BF16, 'otl')
        zt_r = ring(2, [8, 512], BF16, 'zt')
        mt_r = ring(2, [8, 512], BF16, 'mt')
        lnt, k_lnt = tile([4, 512], F32, 'lnt')
        rn_r = ring(2, [2, 512], F32, 'rn')
        tA_r = ring(2, [512], F32, 'tA')
        tB_r = ring(2, [512], F32, 'tB')
        tA2_r = ring(1, [512], F32, 'tA2')
        tB2_r = ring(1, [512], F32, 'tB2')
        sga_r = ring(2, [512], F32, 'sga')
        sgb_r = ring(2, [512], F32, 'sgb')
        xr_r = ring(2, [D], F32, 'xr')
        s1_r = ring(2, [D], F32, 's1')
        xn_r = ring(2, [D], BF16, 'xn2')
        h2_r = ring(2, [8, 128], BF16, 'h2t')
        junk, k_junk = tile([D], BF16, 'junkD')
        ssT, k_ss = tile([3 * NT], F32, 'ssD')
        py_r = Ring([2, 3, 4, 5])
        dstate = {}

        def d_ln(tg):
            t0, n = TGS[tg]
            zt, k_zt = zt_r.next()
            otl, k_otl = ot_r.next()
            rn, k_rn = rn_r.next()
            dma('sp', cvl[:, :, 0:n], CVd[:, :, t0:t0 + n], [], [k_cvl])
            dma('sp', otl[:, :, 0:n], OTd[:, :, t0:t0 + n], [], [k_otl])
            kzt = [(k_zt, cc) for cc in range(8)]
            act(zt[:, :, 0:n], cvl[:, :, 0:n], AF.Square, [k_cvl], kzt)
            for kc in range(8):
                mm(bank(0)[:, 0:n], onesb, cvl[:, kc, 0:n], kc == 0, kc == 7, [k_onesb, k_cvl], [PB(0)])
            for kc in range(8):
                mm(bank(1)[:, 0:n], onesb, zt[:, kc, 0:n], kc == 0, kc == 7, [k_onesb] + kzt, [PB(1)])
            dstate[tg] = dict(zt=zt, kzt=kzt, otl=otl, k_otl=k_otl)
            yield
            mu, musq, var, sdv = [lnt[:, q, 0:n] for q in range(4)]
            rstd, nmr = rn[:, 0, 0:n], rn[:, 1, 0:n]
            ts('dve', mu, bank(0)[:, 0:n], 1.0 / D, None, ALU.mult, None, [PB(0)], [(k_lnt, 0)])
            tt('dve', musq, mu, mu, ALU.mult, [(k_lnt, 0)], [(k_lnt, 1)])
            stt(var, bank(1)[:, 0:n], 1.0 / D, musq, ALU.mult, ALU.subtract, [PB(1), (k_lnt, 1)], [(k_lnt, 2)])
            ts('dve', var, var, 0.0, None, ALU.max, None, [(k_lnt, 2)], [(k_lnt, 2)])
            act(sdv, var, AF.Sqrt, [(k_lnt, 2), k_eps], [(k_lnt, 3)], bias=epsT[:, 0:1], scale=1.0)
            recip(rstd, sdv, [(k_lnt, 3)], [(k_rn, 0)])
            stt(nmr, mu, -1.0, rstd, ALU.mult, ALU.mult, [(k_lnt, 0), (k_rn, 0)], [(k_rn, 1)])
            yield
            for cc in range(8):
                tA, k_tA = tA2_r.next()
                tB, k_tB = tB2_r.next()
                tt('dve', tA[:, 0:n], cvl[:, cc, 0:n], rstd, ALU.mult, [k_cvl, (k_rn, 0)], [k_tA])
                tt('dve', tB[:, 0:n], tA[:, 0:n], nmr, ALU.add, [k_tA, (k_rn, 1)], [k_tB])
                act(zt[:, cc, 0:n], tB[:, 0:n], AF.Silu, [k_tB, k_cvec], [(k_zt, cc)],
                    bias=cvecT[:, 16 + cc:17 + cc], scale=cvecT[:, 8 + cc:9 + cc])
                yield

        def d_proj(tg):
            t0, n = TGS[tg]
            d = dstate[tg]
            zt, kzt, otl, k_otl = d['zt'], d['kzt'], d['otl'], d['k_otl']
            mt, k_mt = mt_r.next()
            for c in range(8):
                sga, k_sga = sga_r.next()
                sgb, k_sgb = sgb_r.next()
                dma('sp', sga[:, 0:n], SGd[:, c, t0:t0 + n], [], [k_sga])
                dma('sp', sgb[:, 0:n], SGd[:, 8 + c, t0:t0 + n], [], [k_sgb])
                pya = py_r.next()
                pyb = py_r.next()
                for kc in range(8):
                    mm(bank(pya)[:, 0:n], wpw[:, kc, c * 128:(c + 1) * 128], zt[:, kc, 0:n], kc == 0, kc == 7,
                       kwpw + kzt, [PB(pya)])
                for kc in range(8):
                    mm(bank(pyb)[:, 0:n], wo[:, kc, c * 128:(c + 1) * 128], otl[:, kc, 0:n], kc == 0, kc == 7,
                       kwo + [k_otl], [PB(pyb)])
                tA, k_tA = tA_r.next()
                tB, k_tB = tB_r.next()
                tt('dve', tA[:, 0:n], bank(pya)[:, 0:n], sga[:, 0:n], ALU.mult, [PB(pya), k_sga], [k_tA])
                tt('dve', tB[:, 0:n], bank(pyb)[:, 0:n], sgb[:, 0:n], ALU.mult, [PB(pyb), k_sgb], [k_tB])
                tt('pool', mt[:, c, 0:n], tA[:, 0:n], tB[:, 0:n], ALU.add, [k_tA, k_tB], [(k_mt, c)])
                d['mt'] = mt
                d['kmt'] = [(k_mt, cq) for cq in range(8)]
                yield

        po_r = Ring([6, 7])

        def d_merge(tg):
            t0, n = TGS[tg]
            d = dstate[tg]
            mt, kmt = d['mt'], d['kmt']
            ntl = n // 128
            m1 = {}

            def M1(tl):
                i = t0 // 128 + tl
                xr, k_xr = xr_r.next()
                s1, k_s1 = s1_r.next()
                dma('sp', xr, xs[i * 128:(i + 1) * 128, :], [], [k_xr])
                for half in range(2):
                    po = po_r.next()
                    for c in range(8):
                        mm(bank(po), mt[:, c, tl * 128:(tl + 1) * 128], wm[:, c, half * 512:(half + 1) * 512],
                           c == 0, c == 7, kmt + kwm, [PB(po)])
                    tt('dve', s1[:, half * 512:(half + 1) * 512], bank(po), xr[:, half * 512:(half + 1) * 512], ALU.add,
                       [PB(po), k_xr], [(k_s1, half)])
                dma('pool', S1d[i * 128:(i + 1) * 128, :], s1, [(k_s1, 0), (k_s1, 1)], [('S1d', i)])
                m1[tl] = (s1, k_s1)

            def M2(tl):
                i = t0 // 128 + tl
                s1, k_s1 = m1[tl]
                xn, k_xn = xn_r.next()
                h2t, k_h2 = h2_r.next()
                rms_to_T(s1, [(k_s1, 0), (k_s1, 1)], i, ssT, k_ss, junk, k_junk, xn, k_xn, 1, h2t, k_h2, 'act')
                dma('pool', H2Td[:, :, i * 128:(i + 1) * 128], h2t, [k_h2], [('H2Td', i)])

            M1(0)
            for tl in range(ntl):
                if tl + 1 < ntl:
                    M1(tl + 1)
                M2(tl)

        NG = len(TGS)

        def alternate(ga, gb):
            a_alive, b_alive = ga is not None, gb is not None
            while a_alive or b_alive:
                if b_alive:
                    try:
                        next(gb)
                    except StopIteration:
                        b_alive = False
                if a_alive:
                    try:
                        next(ga)
                    except StopIteration:
                        a_alive = False

        for _ in d_ln(0):
            pass
        for tg in range(NG + 1):
            ga = d_ln(tg + 1) if tg + 1 < NG else None
            gb = d_proj(tg) if tg < NG else None
            alternate(ga, gb)
            if tg >= 1:
                d_merge(tg - 1)
        S.barrier()
        st['off'] = persist_mark

    if 'E' in phases:
        h2T, k_h2T = tile([8, LP], BF16, 'h2T')
        for q4 in range(4):
            c0 = q4 * 1056
            dma('sp', h2T[:, :, c0:c0 + 1056], H2Td[:, :, c0:c0 + 1056], [], [(k_h2T, q4)])
        kh2 = [(k_h2T, q4) for q4 in range(4)]
        fdwT, k_fdw = tile([44 * 3], F32, 'fdw')
        fdbT, k_fdb = tile([44], F32, 'fdb')
        dma('sp', fdwT, fdw[:, :], [], [k_fdw])
        dma('sp', fdbT, fdb[:, :], [], [k_fdb])
        stage_r = ring(2, [8, 256], F32, 'stage')
        wu_r = ring(2, [8, 256], BF16, 'wu')
        ua_r = ring(4, [514], F32, 'ua')
        ub_r = ring(4, [514], F32, 'ub')
        ca_r = ring(4, [512], F32, 'ca')
        cb_r = ring(4, [512], F32, 'cb')
        sa_r = ring(3, [512], F32, 'sa')
        at_r = ring(3, [512], BF16, 'at')
        pa_r = Ring([0, 1, 2])
        pb_r = Ring([3, 4, 5])
        est = {}
        wts = {}
        iters = [(p, tg) for p in range(NPAIR) for tg in range(len(TGS))]

        def e_stage1(it):
            p, tg = iters[it]
            t0, n = TGS[tg]
            if tg == 0:
                stg, k_stg = stage_r.next()
                wu, k_wu = wu_r.next()
                src = [(wsrc(ffn_up, p * 128, 128), stg[:, :, 0:128]), (wsrc(ffn_up, 2816 + p * 128, 128), stg[:, :, 128:256])]
                load_w(src, 256, stg, k_stg, wu, k_wu, gffnT, k_gffn, 'pool')
                wts[p] = (wu, k_wu)
            wu, k_wu = wts[p]
            pa = pa_r.next()
            pb = pb_r.next()
            for kc in range(8):
                mm(bank(pa)[:, 0:n], wu[:, kc, 0:128], h2T[:, kc, t0:t0 + n], kc == 0, kc == 7, [k_wu] + kh2, [PB(pa)])
            for kc in range(8):
                mm(bank(pb)[:, 0:n], wu[:, kc, 128:256], h2T[:, kc, t0:t0 + n], kc == 0, kc == 7, [k_wu] + kh2, [PB(pb)])
            ua, k_ua = ua_r.next()
            ub, k_ub = ub_r.next()
            if tg == 0:
                memset('dve', ua[:, 0:2], 0.0, [(k_ua, 'h')])
                memset('dve', ub[:, 0:2], 0.0, [(k_ub, 'h')])
            else:
                pv_ = est[it - 1]
                pn = TGS[tg - 1][1]
                cp('act', ua[:, 0:2], pv_['ua'][:, pn:pn + 2], [(pv_['k_ua'], 'b')], [(k_ua, 'h')])
                cp('act', ub[:, 0:2], pv_['ub'][:, pn:pn + 2], [(pv_['k_ub'], 'b')], [(k_ub, 'h')])
            cp('act', ua[:, 2:2 + n], bank(pa)[:, 0:n], [PB(pa)], [(k_ua, 'b')])
            cp('act', ub[:, 2:2 + n], bank(pb)[:, 0:n], [PB(pb)], [(k_ub, 'b')])
            ca, k_ca = ca_r.next()
            cb, k_cb = cb_r.next()
            for (cx, k_cx, ci, pbk) in ((ca, k_ca, p, pa), (cb, k_cb, NPAIR + p, pb)):
                act(cx[:, 0:n], bank(pbk)[:, 0:n], AF.Identity, [PB(pbk), k_fdw, k_fdb], [k_cx],
                    bias=fdbT[:, ci:ci + 1], scale=fdwT[:, ci * 3 + 2:ci * 3 + 3])
            est[it] = dict(ua=ua, k_ua=k_ua, ub=ub, k_ub=k_ub, ca=ca, k_ca=k_ca, cb=cb, k_cb=k_cb)

        def e_stage2(it):
            p, tg = iters[it]
            t0, n = TGS[tg]
            d = est[it]
            for (u, k_u, cx, k_cx, ci) in ((d['ua'], d['k_ua'], d['ca'], d['k_ca'], p),
                                           (d['ub'], d['k_ub'], d['cb'], d['k_cb'], NPAIR + p)):
                rk = [(k_u, 'h'), (k_u, 'b'), k_fdw]
                stt(cx[:, 0:n], u[:, 1:1 + n], fdwT[:, ci * 3 + 1:ci * 3 + 2], cx[:, 0:n], ALU.mult, ALU.add,
                    rk + [k_cx], [k_cx])
                stt(cx[:, 0:n], u[:, 0:n], fdwT[:, ci * 3:ci * 3 + 1], cx[:, 0:n], ALU.mult, ALU.add,
                    rk + [k_cx], [k_cx])

        def e_stage3(it):
            p, tg = iters[it]
            t0, n = TGS[tg]
            d = est[it]
            sa, k_sa = sa_r.next()
            at, k_at = at_r.next()
            act(sa[:, 0:n], d['ca'][:, 0:n], AF.Silu, [d['k_ca']], [k_sa])
            tt('dve', at[:, 0:n], sa[:, 0:n], d['cb'][:, 0:n], ALU.mult, [k_sa, d['k_cb']], [k_at])
            dma('pool', ACTd[:, p, t0:t0 + n], at[:, 0:n], [k_at], [('ACTd', p, tg)])

        NI = len(iters)
        for step in range(NI + 2):
            if step < NI:
                e_stage1(step)
            if 0 <= step - 1 < NI:
                e_stage2(step - 1)
            if 0 <= step - 2 < NI:
                e_stage3(step - 2)
        S.barrier()
        st['off'] = persist_mark

    if 'F' in phases:
        wd, k_wd = tile([NPAIR, D], BF16, 'wd')
        fgT, k_fg = tile([D], F32, 'fg')
        dma('sp', fgT, fgb[:, :], [], [k_fg])
        stage_r = ring(2, [2, D], F32, 'stageF')
        for b in range(NPAIR // 2):
            stg, k_stg = stage_r.next()
            src = ffn_down[b * 256:(b + 1) * 256, :].rearrange("(kc p) c -> p kc c", p=128)
            dma('sp', stg, src, [], [k_stg])
            cp('act' if b % 2 else 'dve', wd[:, 2 * b:2 * b + 2, :], stg, [k_stg], [(k_wd, b)])
        kwd = [(k_wd, b) for b in range(NPAIR // 2)]
        al_r = ring(2, [NPAIR, 128], BF16, 'al')
        s1_r = ring(2, [D], F32, 's1l')
        s2_r = ring(2, [D], F32, 's2')
        y_r = ring(2, [D], F32, 'y')
        junk, k_junk = tile([D], F32, 'junkF')
        ssT, k_ss = tile([3 * NT], F32, 'ssF')
        po_r = Ring([0, 1, 2, 3])
        for i in range(NT):
            al, k_al = al_r.next()
            s1l, k_s1l = s1_r.next()
            dma('sp', al, ACTd[:, :, i * 128:(i + 1) * 128], [], [k_al])
            dma('sp', s1l, S1d[i * 128:(i + 1) * 128, :], [], [k_s1l])
            s2, k_s2 = s2_r.next()
            for half in range(2):
                po = po_r.next()
                for kc in range(NPAIR):
                    mm(bank(po), al[:, kc, :], wd[:, kc, half * 512:(half + 1) * 512], kc == 0, kc == NPAIR - 1,
                       [k_al] + kwd, [PB(po)])
                tt('dve', s2[:, half * 512:(half + 1) * 512], bank(po), s1l[:, half * 512:(half + 1) * 512], ALU.add,
                   [PB(po), k_s1l], [(k_s2, half)])
            ks2 = [(k_s2, 0), (k_s2, 1)]
            act(junk, s2, AF.Square, ks2, [k_junk, (k_ss, i, 0)], accum=ssT[:, 3 * i:3 * i + 1])
            act(ssT[:, 3 * i + 1:3 * i + 2], ssT[:, 3 * i:3 * i + 1], AF.Sqrt, [(k_ss, i, 0), k_eps], [(k_ss, i, 1)],
                bias=epsT[:, 0:1], scale=1.0 / D)
            recip(ssT[:, 3 * i + 2:3 * i + 3], ssT[:, 3 * i + 1:3 * i + 2], [(k_ss, i, 1)], [(k_ss, i, 2)])
            y, k_y = y_r.next()
            stt(y, s2, ssT[:, 3 * i + 2:3 * i + 3], fgT, ALU.mult, ALU.mult, ks2 + [(k_ss, i, 2), k_fg], [k_y])
            if i == 0:
                dma('pool', out[0:112, :], y[16:128, :], [k_y], [('out', i)])
            elif i < NT - 1:
                dma('pool', out[i * 128 - 16:i * 128 + 112, :], y, [k_y], [('out', i)])
            else:
                dma('pool', out[4080:4096, :], y[0:16, :], [k_y], [('out', i)])

    S.emit(nc, es)
    es.close()
    build.stats = S.stats
    return nc


def _fm(v, nchunk):
    return np.ascontiguousarray(np.asarray(v, np.float32).reshape(nchunk, 128).T)


def _rope_tab(hd):
    half = hd // 2
    inv = (np.float32(10000.0) ** (-np.arange(half, dtype=np.float32) / np.float32(half))).astype(np.float32)
    pos = np.arange(LP, dtype=np.float32)
    ang = (pos[:, None] * inv[None, :]).astype(np.float32)
    cos = np.cos(ang).astype(np.float32).T
    sin = np.sin(ang).astype(np.float32).T
    reps = 128 // half
    return np.ascontiguousarray(np.stack([np.tile(cos, (reps, 1)), np.tile(sin, (reps, 1))], 0))


def _prot():
    out = np.zeros((128, 256), np.float32)
    for col0, hd in ((0, 128), (128, 64)):
        half = hd // 2
        for dp in range(128):
            b, j = dp // hd, dp % hd
            if j < half:
                out[b * hd + j + half, col0 + dp] = -1.0
            else:
                out[b * hd + j - half, col0 + dp] = 1.0
    return out


def make_in_maps(x, meta_tokens, mix_norm_g, w_in, conv_ln_g, conv_ln_b, conv_dw_w, conv_dw_b,
                 conv_pw_out, attn_w_o, w_merge_out, ffn_norm_g, ffn_up, ffn_dw_w, ffn_dw_b,
                 ffn_down, final_norm_g):
    f = lambda a: np.ascontiguousarray(np.asarray(a, np.float32))
    x = f(x)
    B = x.shape[0]
    common = {
        "w_in": f(w_in[0]),
        "gmix": _fm(mix_norm_g[0], 8),
        "dwT": np.ascontiguousarray(np.transpose(f(conv_dw_w[0]).reshape(31, 8, 128), (2, 1, 0)).reshape(128, 8 * 31)),
        "cvec": np.ascontiguousarray(np.concatenate([_fm(conv_dw_b[0], 8), _fm(conv_ln_g[0], 8), _fm(conv_ln_b[0], 8)], 1)),
        "w_pw": f(conv_pw_out[0]),
        "w_o": f(attn_w_o[0]),
        "w_m": f(w_merge_out[0]),
        "gffn": _fm(ffn_norm_g[0], 8),
        "ffn_up": f(ffn_up[0]),
        "fdw": np.ascontiguousarray(np.transpose(f(ffn_dw_w[0]).reshape(3, 44, 128), (2, 1, 0)).reshape(128, 44 * 3)),
        "fdb": _fm(ffn_dw_b[0], 44),
        "ffn_down": f(ffn_down[0]),
        "fgb": np.ascontiguousarray(np.broadcast_to(f(final_norm_g)[None, :], (128, D))),
        "rope128": _rope_tab(128),
        "rope64": _rope_tab(64),
        "identd": np.eye(128, dtype=np.float32),
        "protd": _prot(),
        "pow2d": np.ascontiguousarray(np.broadcast_to((2.0 ** -np.arange(NIT + 1, dtype=np.float32))[None, :], (128, NIT + 1))),
    }
    tq = np.arange(128)[:, None]
    sk = np.arange(128)[None, :]
    vis = sk <= tq
    cm = np.concatenate([np.where(vis, np.float32(3e38), np.float32(-1e30)),
                         np.where(vis, np.float32(0.0), np.float32(1.0))], 1).astype(np.float32)
    common["cmaskd"] = np.ascontiguousarray(cm)
    meta = f(meta_tokens)
    pad = np.zeros((LP - L, D), np.float32)
    maps = []
    for b in range(B):
        d = dict(common)
        d["xs"] = np.ascontiguousarray(np.concatenate([meta, x[b], pad], 0))
        maps.append(d)
    return maps


_NC_CACHE = {}


def kernel(**inputs):
    maps = make_in_maps(**inputs)
    if 'nc' not in _NC_CACHE:
        _NC_CACHE['nc'] = build()
    nc = _NC_CACHE['nc']
    res = run_bass_kernel_spmd(nc, maps, core_ids=list(range(len(maps))))
    outs = [np.asarray(r["out"], np.float32) for r in res.results]
    return np.stack(outs, 0)
```

```python
import numpy as np
from contextlib import ExitStack
import concourse.bass as bass
import concourse.mybir as mybir
from concourse.bass_utils import run_bass_kernel_spmd

F32 = mybir.dt.float32
BF16 = mybir.dt.bfloat16
AF = mybir.ActivationFunctionType
ALU = mybir.AluOpType
AX = mybir.AxisListType

L = 4112
LP = 4224
NT = 33
D = 1024
KC = 8
NPAIR = 22
EPS = 1e-6
NIT = 17
C_Q, C_K, C_V, C_QI, C_KI, C_WI, C_G = 2048, 3072, 3328, 3584, 4096, 4160, 4168
TGS = [(t0, min(512, LP - t0)) for t0 in range(0, LP, 512)]

ENGS = ['pe', 'act', 'dve', 'pool', 'sp']
NDMASEM = 24


class Op:
    __slots__ = ('eng', 'fn', 'reads', 'writes', 'deps', 'sig', 'tok', 'is_dma', 'idx', 'prevdma')

    def __init__(self, eng, fn, reads, writes, is_dma):
        self.eng = eng
        self.fn = fn
        self.reads = reads
        self.writes = writes
        self.is_dma = is_dma
        self.deps = []
        self.sig = False
        self.tok = None
        self.prevdma = None


class Sched:
    def __init__(self):
        self.ops = {e: [] for e in ENGS}
        self.lastw = {}
        self.readers = {}
        self.n = 0
        self.pending = {e: [] for e in ENGS}
        self.lastc = {e: None for e in ENGS}
        self.dmas_since = []

    def barrier(self):
        for f in ENGS:
            lst = [self.lastc[e] for e in ENGS if e != f and self.lastc[e] is not None]
            self.pending[f] = lst + list(self.dmas_since)
        self.dmas_since = []
        self.lastw = {}
        self.readers = {}

    def add(self, eng, fn, reads=(), writes=(), dma=False):
        if not getattr(self, 'enabled', True):
            return None
        op = Op(eng, fn, tuple(reads), tuple(writes), dma)
        op.idx = self.n
        self.n += 1
        deps = {}
        for r in op.reads:
            w = self.lastw.get(r)
            if w is not None:
                deps[w.idx] = (w, True)
        for wk in op.writes:
            lw = self.lastw.get(wk)
            if lw is not None and lw.idx not in deps:
                deps[lw.idx] = (lw, False)
            for rd in self.readers.get(wk, ()):
                if rd.idx not in deps:
                    deps[rd.idx] = (rd, False)
        for d, raw in deps.values():
            if d is op:
                continue
            if (not d.is_dma) and (not op.is_dma) and d.eng == op.eng:
                if not raw or op.eng == 'pe':
                    continue
            op.deps.append(d)
            d.sig = True
        if self.pending[eng]:
            for d in self.pending[eng]:
                if d is not op and d not in op.deps:
                    op.deps.append(d)
                    d.sig = True
            self.pending[eng] = []
        for r in op.reads:
            self.readers.setdefault(r, []).append(op)
        for wk in op.writes:
            self.lastw[wk] = op
            self.readers[wk] = []
        self.ops[eng].append(op)
        if dma:
            self.dmas_since.append(op)
        else:
            self.lastc[eng] = op
        return op

    def emit(self, nc, es):
        esem = {e: es.enter_context(nc.semaphore('s_' + e)) for e in ENGS}
        dsem = [es.enter_context(nc.semaphore('d_%d' % i)) for i in range(NDMASEM)]
        duse = [0] * NDMASEM
        dlast = [None] * NDMASEM
        allops = sorted([o for e in ENGS for o in self.ops[e]], key=lambda o: o.idx)
        qengs = sorted({o.eng for o in allops if o.is_dma})
        per = NDMASEM // max(1, len(qengs))
        qsems = {e: list(range(i * per, (i + 1) * per)) for i, e in enumerate(qengs)}
        qrr = {e: 0 for e in qengs}
        cnt = {e: 0 for e in ENGS}
        pos = {}
        for e in ENGS:
            for n_, o in enumerate(self.ops[e]):
                pos[id(o)] = n_ + 1
        needed = set()
        for e in ENGS:
            waited0 = {}
            for o in self.ops[e]:
                best0 = {}
                for d in o.deps:
                    if d.is_dma:
                        continue
                    if pos[id(d)] > best0.get(d.eng, (0, None))[0]:
                        best0[d.eng] = (pos[id(d)], d)
                for k_, (v_, d_) in best0.items():
                    if waited0.get(k_, 0) < v_:
                        needed.add(id(d_))
                        waited0[k_] = v_
        for e in ENGS:
            for o in self.ops[e]:
                if not o.is_dma:
                    o.sig = id(o) in needed
        for o in allops:
            if o.is_dma:
                s = qsems[o.eng][qrr[o.eng] % per]
                qrr[o.eng] += 1
                duse[s] += 1
                o.prevdma = dlast[s]
                o.tok = (('d', s), 16 * duse[s])
                dlast[s] = o
            elif o.sig:
                cnt[o.eng] += 1
                o.tok = (('e', o.eng), cnt[o.eng])
        self.stats = dict(cnt)

        def semof(key):
            return esem[key[1]] if key[0] == 'e' else dsem[key[1]]

        for e in ENGS:
            nxt = None
            for o in reversed(self.ops[e]):
                if o.is_dma:
                    continue
                if o.sig:
                    nxt = o.tok
                elif nxt is not None:
                    o.tok = nxt
                else:
                    o.tok = None

        def run(ename, eng):
            waited = {}
            for o in self.ops[ename]:
                deps = list(o.deps)
                if o.is_dma and o.prevdma is not None:
                    deps.append(o.prevdma)
                best = {}
                for d in deps:
                    assert d.tok is not None, "dependency on op with no later signal"
                    k, v = d.tok
                    if best.get(k, 0) < v:
                        best[k] = v
                for k, v in best.items():
                    if waited.get(k, 0) < v:
                        eng.wait_ge(semof(k), v)
                        waited[k] = v
                ins = o.fn(eng)
                if o.is_dma:
                    ins.then_inc(semof(o.tok[0]), 16)
                elif o.sig:
                    ins.then_inc(semof(o.tok[0]), 1)

        with nc.Block() as block:
            @block.tensor
            def _(eng):
                run('pe', eng)

            @block.scalar
            def _(eng):
                run('act', eng)

            @block.vector
            def _(eng):
                run('dve', eng)

            @block.gpsimd
            def _(eng):
                run('pool', eng)

            @block.sync
            def _(eng):
                run('sp', eng)
                for s in range(NDMASEM):
                    if duse[s] > 0:
                        eng.wait_ge(dsem[s], 16 * duse[s])


class Ring:
    def __init__(self, items):
        self.items = items
        self.i = 0

    def next(self):
        it = self.items[self.i % len(self.items)]
        self.i += 1
        return it


def build(debug=False, phases="ACDEF", asub="1234"):
    nc = bass.Bass("TRN2", target_bir_lowering=False)

    def din(name, shape, dt=F32):
        return nc.dram_tensor(name, list(shape), dt, kind="ExternalInput").ap()

    skind = "ExternalOutput" if debug else "Internal"

    def dscr(name, shape, dt):
        return nc.dram_tensor(name, list(shape), dt, kind=skind).ap()

    xs = din("xs", [LP, D])
    w_in = din("w_in", [D, 6216])
    gmix = din("gmix", [128, 8])
    dwT = din("dwT", [128, 8 * 31])
    cvec = din("cvec", [128, 24])
    w_pw = din("w_pw", [D, D])
    w_o = din("w_o", [D, D])
    w_m = din("w_m", [D, D])
    gffn = din("gffn", [128, 8])
    ffn_up = din("ffn_up", [D, 2 * 2816])
    fdw = din("fdw", [128, 44 * 3])
    fdb = din("fdb", [128, 44])
    ffn_down = din("ffn_down", [2816, D])
    fgb = din("fgb", [128, D])
    rope128 = din("rope128", [2, 128, LP])
    rope64 = din("rope64", [2, 128, LP])
    identd = din("identd", [128, 128])
    protd = din("protd", [128, 256])
    cmaskd = din("cmaskd", [128, 256])
    pow2d = din("pow2d", [128, NIT + 1])
    out = nc.dram_tensor("out", [4096, D], F32, kind="ExternalOutput").ap()

    CVd = dscr("CVd", [128, 8, LP], BF16)
    QTd = dscr("QTd", [128, 8, LP], BF16)
    KTd = dscr("KTd", [128, 2, LP], BF16)
    QITd = dscr("QITd", [128, 4, LP], BF16)
    KITd = dscr("KITd", [128, LP], BF16)
    Vd = dscr("Vd", [128, NT, 260], BF16)
    WId = dscr("WId", [128, NT, 16], F32)
    SGd = dscr("SGd", [128, 16, LP], F32)
    OTd = dscr("OTd", [128, 8, LP], BF16)
    S1d = dscr("S1d", [LP, D], F32)
    H2Td = dscr("H2Td", [128, 8, LP], BF16)
    ACTd = dscr("ACTd", [128, NPAIR, LP], BF16)

    S = Sched()
    es = ExitStack()
    import os as _os
    AW = int(_os.environ.get('AWK', '42')) * 1024
    arena = es.enter_context(nc.sbuf_tensor("arena", [128, AW], F32))
    psum = es.enter_context(nc.psum_tensor("psum", [128, 8, 512], F32))
    st = {'off': 0}

    def alloc(shape, dt):
        n = int(np.prod(shape))
        words = n if dt == F32 else (n + 1) // 2
        words = (words + 3) // 4 * 4
        off = st['off']
        if off + words > AW and not getattr(S, 'enabled', True):
            off = 0
        st['off'] = off + words
        assert st['off'] <= AW, ("arena overflow", st['off'])
        ap = arena[:, off:off + words]
        if dt != F32:
            ap = ap.bitcast(dt)
        ap = ap[:, 0:n]
        if len(shape) == 2:
            ap = ap.rearrange("p (a b) -> p a b", a=shape[0])
        elif len(shape) == 3:
            ap = ap.rearrange("p (a b c) -> p a b c", a=shape[0], b=shape[1])
        return ap

    uid = [0]

    def tile(shape, dt, name=None):
        uid[0] += 1
        return (alloc(shape, dt), (name or 't', uid[0]))

    def ring(n, shape, dt, name=None):
        return Ring([tile(shape, dt, name) for _ in range(n)])

    def bank(b):
        return psum[:, b, :]

    def bankbf(b):
        return psum[:, b, :].bitcast(BF16)

    def PB(b):
        return ('ps', b)

    def dma(q, out_, in_, reads, writes):
        S.add(q, lambda e: e.dma_start(out=out_, in_=in_), reads, writes, dma=True)

    def mm(out_, lhsT, rhs, start, stop, reads, writes, sgc=False):
        if sgc:
            S.add('pe', lambda e: e.matmul(out_, lhsT=lhsT, rhs=rhs, start=start, stop=stop, skip_group_check=True), reads, writes)
        else:
            S.add('pe', lambda e: e.matmul(out_, lhsT=lhsT, rhs=rhs, start=start, stop=stop), reads, writes)

    def tr(out_, in_, ident, reads, writes):
        S.add('pe', lambda e: e.transpose(out=out_, in_=in_, identity=ident), reads, writes)

    def act(out_, in_, func, reads, writes, bias=None, scale=None, accum=None):
        kw = {}
        if bias is not None:
            kw['bias'] = bias
        if scale is not None:
            kw['scale'] = scale
        if accum is not None:
            kw['accum_out'] = accum
        S.add('act', lambda e: e.activation(out=out_, in_=in_, func=func, **kw), reads, writes)

    def ts(eng, out_, in0, s1, s2, op0, op1, reads, writes, accum=None):
        if accum is not None:
            S.add(eng, lambda e: e.tensor_scalar(out=out_, in0=in0, scalar1=s1, scalar2=s2, op0=op0, op1=op1, accum_out=accum), reads, writes)
        elif op1 is None:
            S.add(eng, lambda e: e.tensor_scalar(out=out_, in0=in0, scalar1=s1, scalar2=None, op0=op0), reads, writes)
        else:
            S.add(eng, lambda e: e.tensor_scalar(out=out_, in0=in0, scalar1=s1, scalar2=s2, op0=op0, op1=op1), reads, writes)

    def tt(eng, out_, in0, in1, op, reads, writes):
        S.add(eng, lambda e: e.tensor_tensor(out=out_, in0=in0, in1=in1, op=op), reads, writes)

    def stt(out_, in0, scalar, in1, op0, op1, reads, writes):
        S.add('dve', lambda e: e.scalar_tensor_tensor(out=out_, in0=in0, scalar=scalar, in1=in1, op0=op0, op1=op1), reads, writes)

    def cp(eng, out_, in_, reads, writes):
        if eng == 'act':
            act(out_, in_, AF.Copy, reads, writes)
        else:
            S.add(eng, lambda e: e.tensor_copy(out=out_, in_=in_), reads, writes)

    def memset(eng, ap, val, writes):
        S.add(eng, lambda e: e.memset(ap, val), (), writes)

    def recip(out_, in_, reads, writes):
        S.add('dve', lambda e: e.reciprocal(out=out_, in_=in_), reads, writes)

    identf, k_identf = tile([128], F32, 'identf')
    identb, k_identb = tile([128], BF16, 'identb')
    onesb, k_onesb = tile([128], BF16, 'onesb')
    epsT, k_eps = tile([1], F32, 'eps')
    gmixT, k_gmix = tile([8], F32, 'gmix')
    ngmixT, k_ngmix = tile([8], F32, 'ngmix')
    gffnT, k_gffn = tile([8], F32, 'gffn')
    cvecT, k_cvec = tile([24], F32, 'cvec')
    dma('sp', identf, identd[:, :], [], [k_identf])
    dma('sp', gmixT, gmix[:, :], [], [k_gmix])
    dma('sp', gffnT, gffn[:, :], [], [k_gffn])
    dma('sp', cvecT, cvec[:, :], [], [k_cvec])
    cp('dve', identb, identf, [k_identf], [k_identb])
    memset('dve', onesb, 1.0, [k_onesb])
    memset('dve', epsT, EPS, [k_eps])
    ts('dve', ngmixT, gmixT, -1.0, None, ALU.mult, None, [k_gmix], [k_ngmix])
    persist_mark = st['off']

    def rms_to_T(xt, k_xt, i, ssT, k_ss, junk, k_junk, xn, k_xn, trb, dstT, k_dst, evac_eng):
        kx = list(k_xt) if isinstance(k_xt, list) else [k_xt]
        act(junk, xt, AF.Square, kx, [k_junk, (k_ss, i, 0)], accum=ssT[:, 3 * i:3 * i + 1])
        act(ssT[:, 3 * i + 1:3 * i + 2], ssT[:, 3 * i:3 * i + 1], AF.Sqrt, [(k_ss, i, 0), k_eps], [(k_ss, i, 1)],
            bias=epsT[:, 0:1], scale=1.0 / D)
        recip(ssT[:, 3 * i + 2:3 * i + 3], ssT[:, 3 * i + 1:3 * i + 2], [(k_ss, i, 1)], [(k_ss, i, 2)])
        ts('dve', xn, xt, ssT[:, 3 * i + 2:3 * i + 3], None, ALU.mult, None, kx + [(k_ss, i, 2)], [k_xn])
        pT = bankbf(trb).rearrange("p (a b) -> p a b", a=8)
        for kc in range(8):
            tr(pT[:, kc, :], xn[:, kc * 128:(kc + 1) * 128], identb, [k_xn, k_identb], [PB(trb)])
        cp(evac_eng, dstT, pT, [PB(trb)], [k_dst])

    def load_w(src3, ncols, stage, k_stage, dst, k_dst, gT, k_g, ceng, rot=None):
        if isinstance(src3, list):
            for (s_ap, st_ap) in src3:
                dma('sp', st_ap, s_ap, [], [k_stage])
        else:
            dma('sp', stage, src3, [], [k_stage])
        if gT is None:
            cp(ceng, dst, stage, [k_stage], [k_dst])
        else:
            for kc in range(dst.shape[1]):
                ts(ceng, dst[:, kc], stage[:, kc], gT[:, kc:kc + 1], 0.0, ALU.mult, ALU.add,
                   [k_stage, k_g], [k_dst])
        if rot is not None:
            dstr, k_dstr, half, ngT, k_ng = rot
            for kc in range(8):
                sv = stage[:, kc].rearrange("p (b two h) -> p b two h", two=2, h=half)
                dv = dstr[:, kc].rearrange("p (b two h) -> p b two h", two=2, h=half)
                ts(ceng, dv[:, :, 0, :], sv[:, :, 1, :], ngT[:, kc:kc + 1], 0.0, ALU.mult, ALU.add,
                   [k_stage, k_ng], [k_dstr])
                ts(ceng, dv[:, :, 1, :], sv[:, :, 0, :], gT[:, kc:kc + 1], 0.0, ALU.mult, ALU.add,
                   [k_stage, k_g], [k_dstr])

    def wsrc(w2d, col0, ncols):
        return w2d[:, col0:col0 + ncols].rearrange("(kc p) c -> p kc c", p=128)

    if 'A' in phases:
        hT, k_hT = tile([8, LP], BF16, 'hT')

        def hk(t0, n):
            return [(k_hT, i) for i in range(t0 // 128, (t0 + n) // 128)]

        m0 = st['off']
        xt_r = ring(3, [D], F32, 'xt')
        xn_r = ring(2, [D], BF16, 'xn')
        junk, k_junk = tile([D], F32, 'junk')
        ssT, k_ss = tile([3 * NT], F32, 'ss')
        for i in range(NT):
            xt, k_xt = xt_r.next()
            xn, k_xn = xn_r.next()
            dma('sp', xt, xs[i * 128:(i + 1) * 128, :], [], [k_xt])
            rms_to_T(xt, k_xt, i, ssT, k_ss, junk, k_junk, xn, k_xn, 6 + (i % 2),
                     hT[:, :, i * 128:(i + 1) * 128], (k_hT, i), 'act' if i % 2 else 'dve')

        mA = st['off']
        stage_r = ring(2, [8, 256], F32, 'stage')

        S.enabled = '1' in asub
        wA_r = ring(2, [8, 256], BF16, 'wA')
        dg_r = ring(2, [31, 128], BF16, 'dg')
        glu_r = ring(2, [30 + LP], BF16, 'glu')
        sg_r = ring(2, [512], F32, 'sg')
        cv_r = ring(3, [512], BF16, 'cv')
        dwTt, k_dwT = tile([8 * 31], F32, 'dwT')
        dma('sp', dwTt, dwT[:, :], [], [k_dwT])
        for g_, kg in glu_r.items:
            memset('pool', g_[:, 0:30], 0.0, [(kg, -1)])
        pa_r = Ring([0, 1])
        pg_r = Ring([2, 3])
        pc_r = Ring([4, 5])
        for cc in range(8):
            stg, k_stg = stage_r.next()
            wA, k_wA = wA_r.next()
            dg, k_dg = dg_r.next()
            glu, k_glu = glu_r.next()
            src = [(wsrc(w_in, cc * 128, 128), stg[:, :, 0:128]), (wsrc(w_in, 1024 + cc * 128, 128), stg[:, :, 128:256])]
            load_w(src, 256, stg, k_stg, wA, k_wA, gmixT, k_gmix, 'pool')
            for k in range(31):
                ts('pool', dg[:, k, :], identb, dwTt[:, cc * 31 + k:cc * 31 + k + 1], 0.0, ALU.mult, ALU.add,
                   [k_identb, k_dwT], [k_dg])

            def proj(tg):
                t0, n = TGS[tg]
                pa = pa_r.next()
                pg = pg_r.next()
                for kc in range(8):
                    mm(bank(pa)[:, 0:n], wA[:, kc, 0:128], hT[:, kc, t0:t0 + n], kc == 0, kc == 7,
                       [k_wA] + hk(t0, n), [PB(pa)])
                for kc in range(8):
                    mm(bank(pg)[:, 0:n], wA[:, kc, 128:256], hT[:, kc, t0:t0 + n], kc == 0, kc == 7,
                       [k_wA] + hk(t0, n), [PB(pg)])
                sg, k_sg = sg_r.next()
                act(sg[:, 0:n], bank(pg)[:, 0:n], AF.Sigmoid, [PB(pg)], [k_sg])
                tt('dve', glu[:, 30 + t0:30 + t0 + n], bank(pa)[:, 0:n], sg[:, 0:n], ALU.mult,
                   [PB(pa), k_sg], [(k_glu, tg)])

            def conv(tg):
                t0, n = TGS[tg]
                pc = pc_r.next()
                for k in range(31):
                    mm(bank(pc)[:, 0:n], dg[:, k, :], glu[:, t0 + k:t0 + k + n], k == 0, k == 30,
                       [k_dg, (k_glu, tg), (k_glu, tg - 1)], [PB(pc)])
                cv, k_cv = cv_r.next()
                act(cv[:, 0:n], bank(pc)[:, 0:n], AF.Identity, [PB(pc), k_cvec], [k_cv], bias=cvecT[:, cc:cc + 1])
                dma('pool', CVd[:, cc, t0:t0 + n], cv[:, 0:n], [k_cv], [('CVd', cc, tg)])

            for tg in range(len(TGS) + 1):
                if tg < len(TGS):
                    proj(tg)
                if tg >= 1:
                    conv(tg - 1)

        S.barrier()
        mA = m0
        st['off'] = mA
        stage_r = ring(1, [8, 256], F32, 'stage')
        S.enabled = '2' in asub
        WR, k_WR = tile([8, 15 * 128], BF16, 'WR')
        protf, k_protf = tile([256], F32, 'protf')
        protb, k_protb = tile([256], BF16, 'protb')
        dma('sp', protf, protd[:, :], [], [k_protf])
        cp('dve', protb, protf, [k_protf], [k_protb])
        blocks = [(C_Q, 0), (C_Q + 256, 2), (C_Q + 512, 4), (C_Q + 768, 6), (C_K, 8), (C_QI, 10), (C_QI + 256, 12)]
        stage2_r = ring(2, [8, 256], F32, 'stage2')
        for bi, (col0, ch0) in enumerate(blocks):
            stg, k_stg = stage2_r.next()
            load_w(wsrc(w_in, col0, 256), 256, stg, k_stg, WR[:, :, ch0 * 128:(ch0 + 2) * 128], (k_WR, bi),
                   gmixT, k_gmix, 'pool' if bi % 2 else 'dve')
        stg, k_stg = stage2_r.next()
        for hh in range(2):
            load_w(wsrc(w_in, C_KI, 64), 64, stg[:, :, hh * 64:(hh + 1) * 64], k_stg,
                   WR[:, :, 14 * 128 + hh * 64:14 * 128 + (hh + 1) * 64], (k_WR, 7 + hh), gmixT, k_gmix, 'dve')
        kWRall = [(k_WR, b) for b in range(9)]
        rp_r = ring(2, [4, 512], F32, 'rp')
        t1_r = ring(2, [512], F32, 't1')
        t2_r = ring(2, [512], F32, 't2')
        ob_r = ring(3, [512], BF16, 'ob')
        qb_r = ring(3, [512], BF16, 'qb')
        qf_r = ring(3, [512], F32, 'qf')
        pA_r = Ring([0, 1, 2])
        pB_r = Ring([3, 4, 5])
        work = [(tg, c) for tg in range(len(TGS)) for c in range(15)]
        rpof = {}
        pAof = {}

        def a2_mm(w):
            tg, c = work[w]
            t0, n = TGS[tg]
            if c == 0:
                rp, k_rp = rp_r.next()
                dma('sp', rp[:, 0:2, 0:n], rope128[:, :, t0:t0 + n].rearrange("a p t -> p a t"), [], [(k_rp, 0)])
                dma('sp', rp[:, 2:4, 0:n], rope64[:, :, t0:t0 + n].rearrange("a p t -> p a t"), [], [(k_rp, 1)])
                rpof[tg] = (rp, k_rp)
            pA = pA_r.next()
            pAof[w] = pA
            for kc in range(8):
                mm(bank(pA)[:, 0:n], WR[:, kc, c * 128:(c + 1) * 128], hT[:, kc, t0:t0 + n], kc == 0, kc == 7,
                   kWRall + hk(t0, n), [PB(pA)])

        a2_mm(0)
        for w, (tg, c) in enumerate(work):
            t0, n = TGS[tg]
            if w + 1 < len(work):
                a2_mm(w + 1)
            rp, k_rp = rpof[tg]
            pA = pAof[w]
            pB = pB_r.next()
            qb, k_qb = qb_r.next()
            cp('act', qb[:, 0:n], bank(pA)[:, 0:n], [PB(pA)], [k_qb])
            qf, k_qf = qf_r.next()
            cp('act', qf[:, 0:n], bank(pA)[:, 0:n], [PB(pA)], [k_qf])
            pcol = 0 if c < 10 else 128
            mm(bank(pB)[:, 0:n], protb[:, pcol:pcol + 128], qb[:, 0:n], True, True, [k_protb, k_qb], [PB(pB)])
            ti = 0 if c < 10 else 2
            t1, k_t1 = t1_r.next()
            t2, k_t2 = t2_r.next()
            ob, k_ob = ob_r.next()
            tt('dve', t1[:, 0:n], qf[:, 0:n], rp[:, ti, 0:n], ALU.mult, [k_qf, (k_rp, ti // 2)], [k_t1])
            tt('dve', t2[:, 0:n], bank(pB)[:, 0:n], rp[:, ti + 1, 0:n], ALU.mult, [PB(pB), (k_rp, ti // 2)], [k_t2])
            tt('pool' if w % 2 else 'dve', ob[:, 0:n], t1[:, 0:n], t2[:, 0:n], ALU.add, [k_t1, k_t2], [k_ob])
            if c < 8:
                dst = QTd[:, c, t0:t0 + n]
            elif c < 10:
                dst = KTd[:, c - 8, t0:t0 + n]
            elif c < 14:
                dst = QITd[:, c - 10, t0:t0 + n]
            else:
                dst = KITd[:, t0:t0 + n]
            dma('pool', dst, ob[:, 0:n], [k_ob], [('A2o', c, tg)])

        S.barrier()
        st['off'] = mA
        stage_r = ring(2, [8, 256], F32, 'stage')
        S.enabled = '3' in asub
        wv, k_wv = tile([8, 264], BF16, 'wv')
        stg, k_stg = stage_r.next()
        load_w(wsrc(w_in, C_V, 256), 256, stg, k_stg, wv[:, :, 0:256], (k_wv, 0), gmixT, k_gmix, 'dve')
        stg, k_stg = stage_r.next()
        dma('sp', stg[:, :, 0:64], wsrc(w_in, C_WI, 64), [], [k_stg])
        for kc in range(8):
            ts('dve', wv[:, kc, 256:264], stg[:, kc, 0:8], gmixT[:, kc:kc + 1], 0.0, ALU.mult, ALU.add,
               [k_stg, k_gmix], [(k_wv, 1)])
        vt_r = ring(3, [2, 130], BF16, 'vt')
        wi_r = ring(3, [16], F32, 'wi')
        wr_r = ring(3, [8], F32, 'wr')
        for v_, kv in vt_r.items:
            memset('dve', v_, 1.0, [(kv, 'one'), (kv, 'v')])
        pv_r = Ring([6, 7])
        CIDX = (8.0 ** -0.5) * (64.0 ** -0.5)
        for i in range(NT):
            pv = pv_r.next()
            for kc in range(8):
                mm(bank(pv)[:, 0:264], hT[:, kc, i * 128:(i + 1) * 128], wv[:, kc, :], kc == 0, kc == 7,
                   [(k_wv, 0), (k_wv, 1), (k_hT, i)], [PB(pv)])
            vt, k_vt = vt_r.next()
            wi, k_wi = wi_r.next()
            CUT = int(_os.environ.get('A3CUT', '9'))
            if CUT < 3:
                continue
            cp('act', vt[:, :, 0:128], bank(pv)[:, 0:256].rearrange("p (g d) -> p g d", g=2), [PB(pv)], [(k_vt, 'v')])
            if CUT < 4:
                continue
            wr, k_wr = wr_r.next()
            cp('act', wr, bank(pv)[:, 256:264], [PB(pv)], [k_wr])
            ts('dve', wi[:, 8:16], wr, 0.0, 0.5, ALU.is_ge, ALU.subtract, [k_wr], [(k_wi, 1)])
            stt(wi[:, 0:8], wr, 4.0 * CIDX, wi[:, 8:16], ALU.mult, ALU.mult, [k_wr, (k_wi, 1)], [(k_wi, 0)])
            if CUT < 5:
                continue
            dma('sp', Vd[:, i, :], vt.rearrange("p g d -> p (g d)"), [(k_vt, 'v'), (k_vt, 'one')], [('Vd', i)])
            dma('sp', WId[:, i, :], wi, [(k_wi, 0), (k_wi, 1)], [('WId', i)])

        S.enabled = '4' in asub
        wg_r = ring(2, [8, 256], BF16, 'wg')
        sgo_r = ring(3, [512], F32, 'sgo')
        pq_r = Ring([0, 1, 2, 3])
        for blk in range(8):
            stg, k_stg = stage_r.next()
            wg, k_wg = wg_r.next()
            load_w(wsrc(w_in, C_G + blk * 256, 256), 256, stg, k_stg, wg, k_wg, gmixT, k_gmix, 'pool' if blk % 2 else 'dve')
            for tg, (t0, n) in enumerate(TGS):
                for cl in range(2):
                    c = blk * 2 + cl
                    pq = pq_r.next()
                    for kc in range(8):
                        mm(bank(pq)[:, 0:n], wg[:, kc, cl * 128:(cl + 1) * 128], hT[:, kc, t0:t0 + n], kc == 0, kc == 7,
                           [k_wg] + hk(t0, n), [PB(pq)])
                    sgo, k_sgo = sgo_r.next()
                    act(sgo[:, 0:n], bank(pq)[:, 0:n], AF.Sigmoid, [PB(pq)], [k_sgo])
                    dma('pool', SGd[:, c, t0:t0 + n], sgo[:, 0:n], [k_sgo], [('SGd', c, tg)])
        S.enabled = True
        S.barrier()
        st['off'] = persist_mark

    if 'C' in phases:
        KT, k_KT = tile([2, LP], BF16, 'KT')
        KIT, k_KIT = tile([LP], BF16, 'KIT')
        Vt, k_V = tile([NT, 260], BF16, 'V')
        WIt, k_WI = tile([NT, 16], F32, 'WI')
        cmt, k_cm = tile([256], F32, 'cm')
        cnegb, k_cnegb = tile([128], BF16, 'cnegb')
        negI4, k_negI4 = tile([512], BF16, 'negI4')
        pow2, k_pow2 = tile([NIT + 1], F32, 'pow2')
        dma('sp', KT, KTd[:, :, :], [], [k_KT])
        dma('sp', KIT, KITd[:, :], [], [k_KIT])
        dma('sp', Vt, Vd[:, :, :], [], [k_V])
        dma('sp', WIt, WId[:, :, :], [], [k_WI])
        dma('sp', cmt, cmaskd[:, :], [], [k_cm])
        dma('sp', pow2, pow2d[:, :], [], [k_pow2])
        cp('dve', cnegb, cmt[:, 128:256], [k_cm], [k_cnegb])
        for r4 in range(4):
            ts('dve', negI4[:, r4 * 128:(r4 + 1) * 128], identf, -32768.0, None, ALU.mult, None, [k_identf], [k_negI4])
        score_r = ring(2, [LP], F32, 'score')
        mneg_r = ring(2, [LP], BF16, 'mneg')
        junkb, k_junkb = tile([LP], BF16, 'junkb')
        junka, k_junka = tile([LP], BF16, 'junka')
        R_r = ring(4, [512], BF16, 'R')
        dg_r = ring(2, [8, 128], BF16, 'dgs')
        SCB = 7
        qt_r = ring(2, [8, 128], BF16, 'qt')
        qp_r = ring(2, [4, 2, 128], BF16, 'qp')
        pt_r = ring(3, [512], BF16, 'pt')
        obuf_r = ring(2, [D], BF16, 'obuf')
        oT_r = ring(2, [8, 128], BF16, 'oTt')
        sm_r = ring(2, [8 + 2 * NIT + 16], F32, 'sm')
        rden_r = ring(2, [8], F32, 'rden')
        for qp_, kq in qp_r.items:
            memset('pool', qp_, 0.0, [(kq, 0), (kq, 1)])
        pi_r = Ring([0, 1])
        pl_r = Ring([2, 3])
        ACCB = [4, 5, 6]
        TRB = 2
        idx_state = {}
        ACT_COUNT = set()

        sc_state = {}
        CMX = 8 + 2 * NIT + 4

        def gen_scores(i):
            nk = 128 * (i + 1)
            qp, k_qp = qp_r.next()
            dma('sp', qp[0:64, :, 0, :], QITd[0:64, :, i * 128:(i + 1) * 128], [], [(k_qp, 0)])
            dma('sp', qp[64:128, :, 1, :], QITd[64:128, :, i * 128:(i + 1) * 128], [], [(k_qp, 1)])
            score, k_sc = score_r.next()
            sm, k_sm = sm_r.next()
            dg, k_dg = dg_r.next()
            for h in range(8):
                ts('dve', dg[:, h, :], identb, WIt[:, i, 8 + h:9 + h], None, ALU.mult, None, [k_identb, k_WI], [(k_dg, h)])
            rounds = [(k0, h) for k0 in range(0, nk, 512) for h in range(8)]
            piof = {}
            rof = {}

            def emit_mm(r):
                k0, h = rounds[r]
                n = min(512, nk - k0)
                c, par = h // 2, h % 2
                pi = pi_r.next()
                piof[r] = pi
                mm(bank(pi)[:, 0:n], qp[:, c, par, :], KIT[:, k0:k0 + n],
                   True, True, [(k_qp, 0), (k_qp, 1), k_KIT], [PB(pi)])

            def emit_sum(r):
                k0, h = rounds[r]
                n = min(512, nk - k0)
                R, k_R = rof[r]
                mm(bank(SCB)[:, 0:n], dg[:, h, :], R[:, 0:n], h == 0, h == 7, [k_R, (k_dg, h)], [PB(SCB)])
                if h == 7:
                    ci_ = k0 // 512
                    ts('dve', score[:, k0:k0 + n], bank(SCB)[:, 0:n], 1.0, None, ALU.mult, ALU.max,
                       [PB(SCB)], [(k_sc, k0), (k_sm, 'cm', ci_)], accum=sm[:, CMX + ci_:CMX + ci_ + 1])

            emit_mm(0)
            for r, (k0, h) in enumerate(rounds):
                n = min(512, nk - k0)
                if r + 1 < len(rounds):
                    emit_mm(r + 1)
                pi = piof[r]
                R, k_R = R_r.next()
                rof[r] = (R, k_R)
                act(R[:, 0:n], bank(pi)[:, 0:n], AF.Relu, [PB(pi), k_WI], [k_R], scale=WIt[:, i, h:h + 1])
                if r >= 1:
                    emit_sum(r - 1)
                yield
            emit_sum(len(rounds) - 1)
            allsc = [(k_sc, k0) for k0 in range(0, nk, 512)]
            memset('dve', score[:, 0:16], 1e30, [(k_sc, 0)])
            d0 = 128 * i
            tt('dve', score[:, d0:d0 + 128], score[:, d0:d0 + 128], cmt[:, 0:128], ALU.min,
               [k_cm, (k_sc, d0 // 512 * 512)], [(k_sc, d0 // 512 * 512)])
            nch_ = (nk + 511) // 512
            S.add('dve', lambda e: e.tensor_reduce(out=sm[:, 0:1], in_=sm[:, CMX:CMX + nch_], op=ALU.max, axis=AX.X),
                  [(k_sm, 'cm', c_) for c_ in range(nch_)], [(k_sm, 'hi')])
            yield
            dlo = min(d0, 512)
            S.add('dve', lambda e: e.tensor_reduce(out=sm[:, 1:2], in_=score[:, 0:dlo], op=ALU.min, axis=AX.X),
                  allsc, [(k_sm, 'lo')])
            tt('dve', sm[:, 2:3], sm[:, 0:1], sm[:, 1:2], ALU.subtract, [(k_sm, 'hi'), (k_sm, 'lo')], [(k_sm, 'r')])
            stt(sm[:, 8:9], sm[:, 2:3], 0.5, sm[:, 1:2], ALU.mult, ALU.add, [(k_sm, 'r'), (k_sm, 'lo')], [(k_sm, 'mid', 0)])
            ts('dve', sm[:, 8 + NIT + 1:8 + 2 * NIT + 2], pow2, sm[:, 2:3], None, ALU.mult, None, [k_pow2, (k_sm, 'r')],
               [(k_sm, 'rk')])
            sc_state[i] = (score, k_sc, sm, k_sm, nk, allsc)
            yield

        def gen_bisect(i):
            score, k_sc, sm, k_sm, nk, allsc = sc_state[i]
            mneg, k_mn = mneg_r.next()
            for k in range(1, NIT + 1):
                midp = sm[:, 8 + k - 1:8 + k]
                if k in ACT_COUNT:
                    act(junka[:, 0:nk], score[:, 0:nk], AF.Sign, allsc + [(k_sm, 'mid', k - 1)], [k_junka, (k_sm, 'cnt')],
                        bias=midp, scale=-1.0, accum=sm[:, 3:4])
                    ts('dve', sm[:, 4:5], sm[:, 3:4], float(nk - 511), 0.5, ALU.is_le, ALU.subtract, [(k_sm, 'cnt')], [(k_sm, 'sg')])
                else:
                    ts('dve', junkb[:, 0:nk], score[:, 0:nk], midp, None, ALU.is_ge, ALU.add,
                       allsc + [(k_sm, 'mid', k - 1)], [k_junkb, (k_sm, 'cnt')], accum=sm[:, 3:4])
                    ts('dve', sm[:, 4:5], sm[:, 3:4], 255.5, 0.5, ALU.is_ge, ALU.subtract, [(k_sm, 'cnt')], [(k_sm, 'sg')])
                stt(sm[:, 8 + k:8 + k + 1], sm[:, 4:5], sm[:, 8 + NIT + 1 + k:8 + NIT + 2 + k], midp, ALU.mult, ALU.add,
                    [(k_sm, 'sg'), (k_sm, 'rk'), (k_sm, 'mid', k - 1)], [(k_sm, 'mid', k)])
                yield
            stt(sm[:, 5:6], sm[:, 8 + 2 * NIT + 1:8 + 2 * NIT + 2], -0.5, sm[:, 8 + NIT:8 + NIT + 1], ALU.mult, ALU.add,
                [(k_sm, 'rk'), (k_sm, 'mid', NIT)], [(k_sm, 'thr')])
            ts('dve', mneg[:, 0:nk], score[:, 0:nk], sm[:, 5:6], None, ALU.is_lt, None, allsc + [(k_sm, 'thr')], [k_mn])
            idx_state[i] = (mneg, k_mn)
            yield

        def attention(i):
            qt, k_qt = qt_r.next()
            dma('sp', qt, QTd[:, :, i * 128:(i + 1) * 128], [], [k_qt])
            first_in_bank = {4: True, 5: True, 6: True}
            steps = [(j, g) for j in range(i + 1) for g in range(2)]
            plof = {}

            def emit_qk(sidx):
                j, g = steps[sidx]
                pl = pl_r.next()
                plof[sidx] = pl
                need_mask = (i >= 2) or (j == i)
                mm(bank(pl), KT[:, g, j * 128:(j + 1) * 128], qt[:, 4 * g:4 * g + 4, :], True, not need_mask,
                   [k_KT, k_qt], [PB(pl)])
                if need_mask:
                    if i >= 2:
                        mneg, k_mn = idx_state[i]
                        mm(bank(pl), mneg[:, j * 128:(j + 1) * 128], negI4, False, True, [k_mn, k_negI4], [PB(pl)])
                    else:
                        mm(bank(pl), cnegb, negI4, False, True, [k_cnegb, k_negI4], [PB(pl)])

            emit_qk(0)
            for sidx, (j, g) in enumerate(steps):
                if sidx + 1 < len(steps):
                    emit_qk(sidx + 1)
                pl = plof[sidx]
                pt, k_pt = pt_r.next()
                act(pt, bank(pl), AF.Exp, [PB(pl)], [k_pt], scale=128.0 ** -0.5)
                for hh in range(4):
                    h = 4 * g + hh
                    b = ACCB[h // 3]
                    o0 = (h % 3) * 129
                    stf = (j == 0) and first_in_bank[b]
                    if j == 0:
                        first_in_bank[b] = False
                    mm(bank(b)[:, o0:o0 + 129], pt[:, hh * 128:(hh + 1) * 128], Vt[:, j, g * 130:g * 130 + 129],
                       stf, j == i, [k_pt, k_V], [PB(b)], sgc=True)
                yield
            rden, k_rden = rden_r.next()
            ob, k_ob = obuf_r.next()
            for h in range(8):
                b = ACCB[h // 3]
                o0 = (h % 3) * 129
                recip(rden[:, h:h + 1], bank(b)[:, o0 + 128:o0 + 129], [PB(b)], [(k_rden, h)])
                ts('dve', ob[:, h * 128:(h + 1) * 128], bank(b)[:, o0:o0 + 128], rden[:, h:h + 1], None, ALU.mult, None,
                   [PB(b), (k_rden, h)], [(k_ob, h)])
            pT = bankbf(TRB).rearrange("p (a b) -> p a b", a=8)
            for h in range(8):
                tr(pT[:, h, :], ob[:, h * 128:(h + 1) * 128], identb, [(k_ob, h), k_identb], [PB(TRB)])
            oTt, k_oT = oT_r.next()
            cp('act', oTt, pT, [PB(TRB)], [k_oT])
            dma('pool', OTd[:, :, i * 128:(i + 1) * 128], oTt, [k_oT], [('OTd', i)])
            yield

        def run_interleaved(gens):
            state = [[g, max(n, 1), 0, True] for g, n in gens if g is not None]
            while any(a[3] for a in state):
                best = None
                for a in state:
                    if a[3] and (best is None or a[2] / a[1] < best[2] / best[1]):
                        best = a
                try:
                    next(best[0])
                    best[2] += 1
                except StopIteration:
                    best[3] = False

        for T in range(NT):
            gens = [(attention(T), 2 * (T + 1) + 1)]
            if 2 <= T + 1 < NT:
                gens.append((gen_bisect(T + 1), NIT + 1))
            if 2 <= T + 2 < NT:
                gens.append((gen_scores(T + 2), 8 * ((128 * (T + 3) + 511) // 512) + 2))
            run_interleaved(gens)
        S.barrier()
        st['off'] = persist_mark

    if 'D' in phases:
        wpw, k_wpw = tile([8, D], BF16, 'wpw')
        wo, k_wo = tile([8, D], BF16, 'wo')
        wm, k_wm = tile([8, D], BF16, 'wm')
        mD = st['off']
        stage_r = ring(2, [8, 256], F32, 'stage')
        bi = 0
        for (wsrc2, wdst, kd) in ((w_pw, wpw, k_wpw), (w_o, wo, k_wo), (w_m, wm, k_wm)):
            for b4 in range(4):
                stg, k_stg = stage_r.next()
                load_w(wsrc(wsrc2, b4 * 256, 256), 256, stg, k_stg, wdst[:, :, b4 * 256:(b4 + 1) * 256], (kd, b4),
                       None, None, 'act' if bi % 2 else 'dve')
                bi += 1
        S.barrier()
        st['off'] = mD
        kwpw, kwo, kwm = [], [], []
        cvl, k_cvl = tile([8, 512], BF16, 'cvl')
        ot_r = ring(2, [8, 512], BF16, 'otl')
        zt_r = ring(2, [8, 512], BF16, 'zt')
        mt_r = ring(2, [8, 512], BF16, 'mt')
        lnt, k_lnt = tile([4, 512], F32, 'lnt')
        rn_r = ring(2, [2, 512], F32, 'rn')
        tA_r = ring(2, [512], F32, 'tA')
        tB_r = ring(2, [512], F32, 'tB')
        tA2_r = ring(1, [512], F32, 'tA2')
        tB2_r = ring(1, [512], F32, 'tB2')
        sga_r = ring(2, [512], F32, 'sga')
        sgb_r = ring(2, [512], F32, 'sgb')
        xr_r = ring(2, [D], F32, 'xr')
        s1_r = ring(2, [D], F32, 's1')
        xn_r = ring(2, [D], BF16, 'xn2')
        h2_r = ring(2, [8, 128], BF16, 'h2t')
        junk, k_junk = tile([D], BF16, 'junkD')
        ssT, k_ss = tile([3 * NT], F32, 'ssD')
        py_r = Ring([2, 3, 4, 5])
        dstate = {}

        def d_ln(tg):
            t0, n = TGS[tg]
            zt, k_zt = zt_r.next()
            otl, k_otl = ot_r.next()
            rn, k_rn = rn_r.next()
            dma('sp', cvl[:, :, 0:n], CVd[:, :, t0:t0 + n], [], [k_cvl])
            dma('sp', otl[:, :, 0:n], OTd[:, :, t0:t0 + n], [], [k_otl])
            kzt = [(k_zt, cc) for cc in range(8)]
            act(zt[:, :, 0:n], cvl[:, :, 0:n], AF.Square, [k_cvl], kzt)
            for kc in range(8):
                mm(bank(0)[:, 0:n], onesb, cvl[:, kc, 0:n], kc == 0, kc == 7, [k_onesb, k_cvl], [PB(0)])
            for kc in range(8):
                mm(bank(1)[:, 0:n], onesb, zt[:, kc, 0:n], kc == 0, kc == 7, [k_onesb] + kzt, [PB(1)])
            dstate[tg] = dict(zt=zt, kzt=kzt, otl=otl, k_otl=k_otl)
            yield
            mu, musq, var, sdv = [lnt[:, q, 0:n] for q in range(4)]
            rstd, nmr = rn[:, 0, 0:n], rn[:, 1, 0:n]
            ts('dve', mu, bank(0)[:, 0:n], 1.0 / D, None, ALU.mult, None, [PB(0)], [(k_lnt, 0)])
            tt('dve', musq, mu, mu, ALU.mult, [(k_lnt, 0)], [(k_lnt, 1)])
            stt(var, bank(1)[:, 0:n], 1.0 / D, musq, ALU.mult, ALU.subtract, [PB(1), (k_lnt, 1)], [(k_lnt, 2)])
            ts('dve', var, var, 0.0, None, ALU.max, None, [(k_lnt, 2)], [(k_lnt, 2)])
            act(sdv, var, AF.Sqrt, [(k_lnt, 2), k_eps], [(k_lnt, 3)], bias=epsT[:, 0:1], scale=1.0)
            recip(rstd, sdv, [(k_lnt, 3)], [(k_rn, 0)])
            stt(nmr, mu, -1.0, rstd, ALU.mult, ALU.mult, [(k_lnt, 0), (k_rn, 0)], [(k_rn, 1)])
            yield
            for cc in range(8):
                tA, k_tA = tA2_r.next()
                tB, k_tB = tB2_r.next()
                tt('dve', tA[:, 0:n], cvl[:, cc, 0:n], rstd, ALU.mult, [k_cvl, (k_rn, 0)], [k_tA])
                tt('dve', tB[:, 0:n], tA[:, 0:n], nmr, ALU.add, [k_tA, (k_rn, 1)], [k_tB])
                act(zt[:, cc, 0:n], tB[:, 0:n], AF.Silu, [k_tB, k_cvec], [(k_zt, cc)],
                    bias=cvecT[:, 16 + cc:17 + cc], scale=cvecT[:, 8 + cc:9 + cc])
                yield

        def d_proj(tg):
            t0, n = TGS[tg]
            d = dstate[tg]
            zt, kzt, otl, k_otl = d['zt'], d['kzt'], d['otl'], d['k_otl']
            mt, k_mt = mt_r.next()
            for c in range(8):
                sga, k_sga = sga_r.next()
                sgb, k_sgb = sgb_r.next()
                dma('sp', sga[:, 0:n], SGd[:, c, t0:t0 + n], [], [k_sga])
                dma('sp', sgb[:, 0:n], SGd[:, 8 + c, t0:t0 + n], [], [k_sgb])
                pya = py_r.next()
                pyb = py_r.next()
                for kc in range(8):
                    mm(bank(pya)[:, 0:n], wpw[:, kc, c * 128:(c + 1) * 128], zt[:, kc, 0:n], kc == 0, kc == 7,
                       kwpw + kzt, [PB(pya)])
                for kc in range(8):
                    mm(bank(pyb)[:, 0:n], wo[:, kc, c * 128:(c + 1) * 128], otl[:, kc, 0:n], kc == 0, kc == 7,
                       kwo + [k_otl], [PB(pyb)])
                tA, k_tA = tA_r.next()
                tB, k_tB = tB_r.next()
                tt('dve', tA[:, 0:n], bank(pya)[:, 0:n], sga[:, 0:n], ALU.mult, [PB(pya), k_sga], [k_tA])
                tt('dve', tB[:, 0:n], bank(pyb)[:, 0:n], sgb[:, 0:n], ALU.mult, [PB(pyb), k_sgb], [k_tB])
                tt('pool', mt[:, c, 0:n], tA[:, 0:n], tB[:, 0:n], ALU.add, [k_tA, k_tB], [(k_mt, c)])
                d['mt'] = mt
                d['kmt'] = [(k_mt, cq) for cq in range(8)]
                yield

        po_r = Ring([6, 7])

        def d_merge(tg):
            t0, n = TGS[tg]
            d = dstate[tg]
            mt, kmt = d['mt'], d['kmt']
            ntl = n // 128
            m1 = {}

            def M1(tl):
                i = t0 // 128 + tl
                xr, k_xr = xr_r.next()
                s1, k_s1 = s1_r.next()
                dma('sp', xr, xs[i * 128:(i + 1) * 128, :], [], [k_xr])
                for half in range(2):
                    po = po_r.next()
                    for c in range(8):
                        mm(bank(po), mt[:, c, tl * 128:(tl + 1) * 128], wm[:, c, half * 512:(half + 1) * 512],
                           c == 0, c == 7, kmt + kwm, [PB(po)])
                    tt('dve', s1[:, half * 512:(half + 1) * 512], bank(po), xr[:, half * 512:(half + 1) * 512], ALU.add,
                       [PB(po), k_xr], [(k_s1, half)])
                dma('pool', S1d[i * 128:(i + 1) * 128, :], s1, [(k_s1, 0), (k_s1, 1)], [('S1d', i)])
                m1[tl] = (s1, k_s1)

            def M2(tl):
                i = t0 // 128 + tl
                s1, k_s1 = m1[tl]
                xn, k_xn = xn_r.next()
                h2t, k_h2 = h2_r.next()
                rms_to_T(s1, [(k_s1, 0), (k_s1, 1)], i, ssT, k_ss, junk, k_junk, xn, k_xn, 1, h2t, k_h2, 'act')
                dma('pool', H2Td[:, :, i * 128:(i + 1) * 128], h2t, [k_h2], [('H2Td', i)])

            M1(0)
            for tl in range(ntl):
                if tl + 1 < ntl:
                    M1(tl + 1)
                M2(tl)

        NG = len(TGS)

        def alternate(ga, gb):
            a_alive, b_alive = ga is not None, gb is not None
            while a_alive or b_alive:
                if b_alive:
                    try:
                        next(gb)
                    except StopIteration:
                        b_alive = False
                if a_alive:
                    try:
                        next(ga)
                    except StopIteration:
                        a_alive = False

        for _ in d_ln(0):
            pass
        for tg in range(NG + 1):
            ga = d_ln(tg + 1) if tg + 1 < NG else None
            gb = d_proj(tg) if tg < NG else None
            alternate(ga, gb)
            if tg >= 1:
                d_merge(tg - 1)
        S.barrier()
        st['off'] = persist_mark

    if 'E' in phases:
        h2T, k_h2T = tile([8, LP], BF16, 'h2T')
        for q4 in range(4):
            c0 = q4 * 1056
            dma('sp', h2T[:, :, c0:c0 + 1056], H2Td[:, :, c0:c0 + 1056], [], [(k_h2T, q4)])
        kh2 = [(k_h2T, q4) for q4 in range(4)]
        fdwT, k_fdw = tile([44 * 3], F32, 'fdw')
        fdbT, k_fdb = tile([44], F32, 'fdb')
        dma('sp', fdwT, fdw[:, :], [], [k_fdw])
        dma('sp', fdbT, fdb[:, :], [], [k_fdb])
        stage_r = ring(2, [8, 256], F32, 'stage')
        wu_r = ring(2, [8, 256], BF16, 'wu')
        ua_r = ring(4, [514], F32, 'ua')
        ub_r = ring(4, [514], F32, 'ub')
        ca_r = ring(4, [512], F32, 'ca')
        cb_r = ring(4, [512], F32, 'cb')
        sa_r = ring(3, [512], F32, 'sa')
        at_r = ring(3, [512], BF16, 'at')
        pa_r = Ring([0, 1, 2])
        pb_r = Ring([3, 4, 5])
        est = {}
        wts = {}
        iters = [(p, tg) for p in range(NPAIR) for tg in range(len(TGS))]

        def e_stage1(it):
            p, tg = iters[it]
            t0, n = TGS[tg]
            if tg == 0:
                stg, k_stg = stage_r.next()
                wu, k_wu = wu_r.next()
                src = [(wsrc(ffn_up, p * 128, 128), stg[:, :, 0:128]), (wsrc(ffn_up, 2816 + p * 128, 128), stg[:, :, 128:256])]
                load_w(src, 256, stg, k_stg, wu, k_wu, gffnT, k_gffn, 'pool')
                wts[p] = (wu, k_wu)
            wu, k_wu = wts[p]
            pa = pa_r.next()
            pb = pb_r.next()
            kh2q = [(k_h2T, q_) for q_ in range(t0 // 1056, (t0 + n - 1) // 1056 + 1)]
            for kc in range(8):
                mm(bank(pa)[:, 0:n], wu[:, kc, 0:128], h2T[:, kc, t0:t0 + n], kc == 0, kc == 7, [k_wu] + kh2q, [PB(pa)])
            for kc in range(8):
                mm(bank(pb)[:, 0:n], wu[:, kc, 128:256], h2T[:, kc, t0:t0 + n], kc == 0, kc == 7, [k_wu] + kh2q, [PB(pb)])
            ua, k_ua = ua_r.next()
            ub, k_ub = ub_r.next()
            if tg == 0:
                memset('dve', ua[:, 0:2], 0.0, [(k_ua, 'h')])
                memset('dve', ub[:, 0:2], 0.0, [(k_ub, 'h')])
            else:
                pv_ = est[it - 1]
                pn = TGS[tg - 1][1]
                cp('act', ua[:, 0:2], pv_['ua'][:, pn:pn + 2], [(pv_['k_ua'], 'b')], [(k_ua, 'h')])
                cp('act', ub[:, 0:2], pv_['ub'][:, pn:pn + 2], [(pv_['k_ub'], 'b')], [(k_ub, 'h')])
            cp('act', ua[:, 2:2 + n], bank(pa)[:, 0:n], [PB(pa)], [(k_ua, 'b')])
            cp('act', ub[:, 2:2 + n], bank(pb)[:, 0:n], [PB(pb)], [(k_ub, 'b')])
            ca, k_ca = ca_r.next()
            cb, k_cb = cb_r.next()
            for (cx, k_cx, ci, pbk) in ((ca, k_ca, p, pa), (cb, k_cb, NPAIR + p, pb)):
                act(cx[:, 0:n], bank(pbk)[:, 0:n], AF.Identity, [PB(pbk), k_fdw, k_fdb], [k_cx],
                    bias=fdbT[:, ci:ci + 1], scale=fdwT[:, ci * 3 + 2:ci * 3 + 3])
            est[it] = dict(ua=ua, k_ua=k_ua, ub=ub, k_ub=k_ub, ca=ca, k_ca=k_ca, cb=cb, k_cb=k_cb)

        def e_stage2(it):
            p, tg = iters[it]
            t0, n = TGS[tg]
            d = est[it]
            for (u, k_u, cx, k_cx, ci) in ((d['ua'], d['k_ua'], d['ca'], d['k_ca'], p),
                                           (d['ub'], d['k_ub'], d['cb'], d['k_cb'], NPAIR + p)):
                rk = [(k_u, 'h'), (k_u, 'b'), k_fdw]
                stt(cx[:, 0:n], u[:, 1:1 + n], fdwT[:, ci * 3 + 1:ci * 3 + 2], cx[:, 0:n], ALU.mult, ALU.add,
                    rk + [k_cx], [k_cx])
                stt(cx[:, 0:n], u[:, 0:n], fdwT[:, ci * 3:ci * 3 + 1], cx[:, 0:n], ALU.mult, ALU.add,
                    rk + [k_cx], [k_cx])

        def e_stage3(it):
            p, tg = iters[it]
            t0, n = TGS[tg]
            d = est[it]
            sa, k_sa = sa_r.next()
            at, k_at = at_r.next()
            act(sa[:, 0:n], d['ca'][:, 0:n], AF.Silu, [d['k_ca']], [k_sa])
            tt('dve', at[:, 0:n], sa[:, 0:n], d['cb'][:, 0:n], ALU.mult, [k_sa, d['k_cb']], [k_at])
            dma('pool', ACTd[:, p, t0:t0 + n], at[:, 0:n], [k_at], [('ACTd', p, tg)])

        NI = len(iters)
        for step in range(NI + 2):
            if step < NI:
                e_stage1(step)
            if 0 <= step - 1 < NI:
                e_stage2(step - 1)
            if 0 <= step - 2 < NI:
                e_stage3(step - 2)
        S.barrier()
        st['off'] = persist_mark

    if 'F' in phases:
        wd, k_wd = tile([NPAIR, D], BF16, 'wd')
        fgT, k_fg = tile([D], F32, 'fg')
        dma('sp', fgT, fgb[:, :], [], [k_fg])
        stage_r = ring(2, [2, D], F32, 'stageF')
        for b in range(NPAIR // 2):
            stg, k_stg = stage_r.next()
            src = ffn_down[b * 256:(b + 1) * 256, :].rearrange("(kc p) c -> p kc c", p=128)
            dma('sp', stg, src, [], [k_stg])
            cp('act' if b % 2 else 'dve', wd[:, 2 * b:2 * b + 2, :], stg, [k_stg], [(k_wd, b)])
        kwd = [(k_wd, b) for b in range(NPAIR // 2)]
        al_r = ring(2, [NPAIR, 128], BF16, 'al')
        s1_r = ring(2, [D], F32, 's1l')
        s2_r = ring(2, [D], F32, 's2')
        y_r = ring(2, [D], F32, 'y')
        junk, k_junk = tile([D], F32, 'junkF')
        ssT, k_ss = tile([3 * NT], F32, 'ssF')
        po_r = Ring([0, 1, 2, 3])
        for i in range(NT):
            al, k_al = al_r.next()
            s1l, k_s1l = s1_r.next()
            dma('sp', al, ACTd[:, :, i * 128:(i + 1) * 128], [], [k_al])
            dma('sp', s1l, S1d[i * 128:(i + 1) * 128, :], [], [k_s1l])
            s2, k_s2 = s2_r.next()
            for half in range(2):
                po = po_r.next()
                for kc in range(NPAIR):
                    mm(bank(po), al[:, kc, :], wd[:, kc, half * 512:(half + 1) * 512], kc == 0, kc == NPAIR - 1,
                       [k_al, (k_wd, kc // 2)], [PB(po)])
                tt('dve', s2[:, half * 512:(half + 1) * 512], bank(po), s1l[:, half * 512:(half + 1) * 512], ALU.add,
                   [PB(po), k_s1l], [(k_s2, half)])
            ks2 = [(k_s2, 0), (k_s2, 1)]
            act(junk, s2, AF.Square, ks2, [k_junk, (k_ss, i, 0)], accum=ssT[:, 3 * i:3 * i + 1])
            act(ssT[:, 3 * i + 1:3 * i + 2], ssT[:, 3 * i:3 * i + 1], AF.Sqrt, [(k_ss, i, 0), k_eps], [(k_ss, i, 1)],
                bias=epsT[:, 0:1], scale=1.0 / D)
            recip(ssT[:, 3 * i + 2:3 * i + 3], ssT[:, 3 * i + 1:3 * i + 2], [(k_ss, i, 1)], [(k_ss, i, 2)])
            y, k_y = y_r.next()
            stt(y, s2, ssT[:, 3 * i + 2:3 * i + 3], fgT, ALU.mult, ALU.mult, ks2 + [(k_ss, i, 2), k_fg], [k_y])
            if i == 0:
                dma('pool', out[0:112, :], y[16:128, :], [k_y], [('out', i)])
            elif i < NT - 1:
                dma('pool', out[i * 128 - 16:i * 128 + 112, :], y, [k_y], [('out', i)])
            else:
                dma('pool', out[4080:4096, :], y[0:16, :], [k_y], [('out', i)])

    S.emit(nc, es)
    es.close()
    build.stats = S.stats
    return nc


def _fm(v, nchunk):
    return np.ascontiguousarray(np.asarray(v, np.float32).reshape(nchunk, 128).T)


def _rope_tab(hd):
    half = hd // 2
    inv = (np.float32(10000.0) ** (-np.arange(half, dtype=np.float32) / np.float32(half))).astype(np.float32)
    pos = np.arange(LP, dtype=np.float32)
    ang = (pos[:, None] * inv[None, :]).astype(np.float32)
    cos = np.cos(ang).astype(np.float32).T
    sin = np.sin(ang).astype(np.float32).T
    reps = 128 // half
    return np.ascontiguousarray(np.stack([np.tile(cos, (reps, 1)), np.tile(sin, (reps, 1))], 0))


def _prot():
    out = np.zeros((128, 256), np.float32)
    for col0, hd in ((0, 128), (128, 64)):
        half = hd // 2
        for dp in range(128):
            b, j = dp // hd, dp % hd
            if j < half:
                out[b * hd + j + half, col0 + dp] = -1.0
            else:
                out[b * hd + j - half, col0 + dp] = 1.0
    return out


def make_in_maps(x, meta_tokens, mix_norm_g, w_in, conv_ln_g, conv_ln_b, conv_dw_w, conv_dw_b,
                 conv_pw_out, attn_w_o, w_merge_out, ffn_norm_g, ffn_up, ffn_dw_w, ffn_dw_b,
                 ffn_down, final_norm_g):
    f = lambda a: np.ascontiguousarray(np.asarray(a, np.float32))
    x = f(x)
    B = x.shape[0]
    common = {
        "w_in": f(w_in[0]),
        "gmix": _fm(mix_norm_g[0], 8),
        "dwT": np.ascontiguousarray(np.transpose(f(conv_dw_w[0]).reshape(31, 8, 128), (2, 1, 0)).reshape(128, 8 * 31)),
        "cvec": np.ascontiguousarray(np.concatenate([_fm(conv_dw_b[0], 8), _fm(conv_ln_g[0], 8), _fm(conv_ln_b[0], 8)], 1)),
        "w_pw": f(conv_pw_out[0]),
        "w_o": f(attn_w_o[0]),
        "w_m": f(w_merge_out[0]),
        "gffn": _fm(ffn_norm_g[0], 8),
        "ffn_up": f(ffn_up[0]),
        "fdw": np.ascontiguousarray(np.transpose(f(ffn_dw_w[0]).reshape(3, 44, 128), (2, 1, 0)).reshape(128, 44 * 3)),
        "fdb": _fm(ffn_dw_b[0], 44),
        "ffn_down": f(ffn_down[0]),
        "fgb": np.ascontiguousarray(np.broadcast_to(f(final_norm_g)[None, :], (128, D))),
        "rope128": _rope_tab(128),
        "rope64": _rope_tab(64),
        "identd": np.eye(128, dtype=np.float32),
        "protd": _prot(),
        "pow2d": np.ascontiguousarray(np.broadcast_to((2.0 ** -np.arange(NIT + 1, dtype=np.float32))[None, :], (128, NIT + 1))),
    }
    tq = np.arange(128)[:, None]
    sk = np.arange(128)[None, :]
    vis = sk <= tq
    cm = np.concatenate([np.where(vis, np.float32(3e38), np.float32(-1e30)),
                         np.where(vis, np.float32(0.0), np.float32(1.0))], 1).astype(np.float32)
    common["cmaskd"] = np.ascontiguousarray(cm)
    meta = f(meta_tokens)
    pad = np.zeros((LP - L, D), np.float32)
    maps = []
    for b in range(B):
        d = dict(common)
        d["xs"] = np.ascontiguousarray(np.concatenate([meta, x[b], pad], 0))
        maps.append(d)
    return maps


_NC_CACHE = {}


def kernel(**inputs):
    maps = make_in_maps(**inputs)
    if 'nc' not in _NC_CACHE:
        _NC_CACHE['nc'] = build()
    nc = _NC_CACHE['nc']
    res = run_bass_kernel_spmd(nc, maps, core_ids=list(range(len(maps))))
    outs = [np.asarray(r["out"], np.float32) for r in res.results]
    return np.stack(outs, 0)
```
